# Optimizing a Trainium2 kernel written in Bass

```python
import math
import jax
import jax.numpy as jnp
from jax import lax
import numpy as np

D_MODEL = 1024
BATCH = 4
SEQ = 4096
DEPTH = 1

HG_HEADS = 8
HG_DK = 128
HG_DV = 128
HG_WIDTH = HG_HEADS * HG_DK
HG_VWIDTH = HG_HEADS * HG_DV
HG_CHUNK = 64
NSA_HEADS = 16
NSA_KV = 4
NSA_GROUP = NSA_HEADS // NSA_KV
NSA_DH = 64
NSA_WIDTH = NSA_HEADS * NSA_DH
NSA_KVW = NSA_KV * NSA_DH
CMP_LEN = 32
CMP_STRIDE = 16
CMP_HIDDEN = 256
SEL_BLOCK = 64
SEL_TOPK = 16
WINDOW = 512
Q_BLOCK = 64
ROPE_THETA = 500000.0
ROPE_DIM = NSA_DH // 4
D_FF = 4 * D_MODEL
PLE_DIM = 256
EPS = 1e-6
NEG = -1e30

IN_SIZES = [HG_WIDTH, HG_WIDTH, HG_VWIDTH, HG_VWIDTH,
            NSA_WIDTH, 6 * NSA_KVW, 3 * NSA_HEADS,
            D_MODEL, D_MODEL]
N_IN = int(sum(IN_SIZES))
SPLIT_POINTS = [int(v) for v in np.cumsum(IN_SIZES)[:-1]]

kernel_name = "hybrid_hgrn2_nsa_sandwich_block"


def rms_norm(x, w):
    xf = x.astype(jnp.float32)
    y = xf * lax.rsqrt(jnp.mean(xf * xf, axis=-1, keepdims=True) + EPS)
    return y.astype(x.dtype) * w


def masked_softmax(s, mask):
    s = jnp.where(mask, s.astype(jnp.float32), NEG)
    pr = jax.nn.softmax(s, axis=-1)
    return jnp.where(mask, pr, 0.0)


def rope_tables(T):
    half = ROPE_DIM // 2
    inv = jnp.asarray(ROPE_THETA ** (-np.arange(half) * 2.0 / ROPE_DIM), jnp.float32)
    ang = jnp.arange(T, dtype=jnp.float32)[:, None] * inv[None, :]
    return jnp.cos(ang), jnp.sin(ang)


def apply_partial_rope(x, cos, sin):
    half = ROPE_DIM // 2
    extra = x.ndim - 3
    shp = (cos.shape[0],) + (1,) * extra + (half,)
    c = cos.reshape(shp).astype(x.dtype)
    s = sin.reshape(shp).astype(x.dtype)
    x1 = x[..., :half]
    x2 = x[..., half:ROPE_DIM]
    rest = x[..., ROPE_DIM:]
    return jnp.concatenate([x1 * c - x2 * s, x2 * c + x1 * s, rest], axis=-1)


def hgrn2_mixer(q, f_pre, i_in, g, lb, gnorm_w):
    B, T, _ = q.shape
    n = T // HG_CHUNK
    f32 = jnp.float32
    qf = jax.nn.silu(q.astype(f32)) * (HG_DK ** -0.5)
    forget = lb + (1.0 - lb) * jax.nn.sigmoid(f_pre.astype(f32))
    k = 1.0 - forget
    logf = jnp.log(forget)

    def heads(t, d):
        return t.reshape(B, n, HG_CHUNK, HG_HEADS, d).transpose(1, 0, 3, 2, 4)

    qc, kc, lc = heads(qf, HG_DK), heads(k, HG_DK), heads(logf, HG_DK)
    vc = heads(i_in.astype(f32), HG_DV)
    causal = jnp.tril(jnp.ones((HG_CHUNK, HG_CHUNK), dtype=bool))

    def step(S, inp):
        qj, kj, vj, lj = inp
        b = jnp.cumsum(lj, axis=2)
        o_inter = jnp.einsum('bhtk,bhkv->bhtv', qj * jnp.exp(b), S)
        rel = b[:, :, :, None, :] - b[:, :, None, :, :]
        decay = jnp.exp(jnp.where(causal[:, :, None], rel, -jnp.inf))
        A = jnp.einsum('bhtk,bhsk,bhtsk->bhts', qj, kj, decay)
        o = o_inter + jnp.einsum('bhts,bhsv->bhtv', A, vj)
        b_last = b[:, :, -1:, :]
        S = jnp.exp(b_last[:, :, 0, :, None]) * S + jnp.einsum(
            'bhsk,bhsv->bhkv', kj * jnp.exp(b_last - b), vj)
        return S, o

    S0 = jnp.zeros((B, HG_HEADS, HG_DK, HG_DV), f32)
    _, o = lax.scan(step, S0, (qc, kc, vc, lc))
    o = o.transpose(1, 0, 3, 2, 4).reshape(B, T, HG_HEADS, HG_DV)
    gh = g.astype(f32).reshape(B, T, HG_HEADS, HG_DV)
    o = rms_norm(o, gnorm_w.astype(f32)) * jax.nn.silu(gh)
    return o.reshape(B, T, HG_VWIDTH).astype(q.dtype)


def nsa_mixer(q, kc, vc, ks, vs, kw, vw, gates, pe_k, pe_v, wk1, wk2, wv1, wv2):
    B, T, _ = q.shape
    G, R, Dh = NSA_KV, NSA_GROUP, NSA_DH
    scale = Dh ** -0.5
    cos, sin = rope_tables(T)
    q = q.reshape(B, T, G, R, Dh) * scale
    q_rot = apply_partial_rope(q, cos, sin)
    kvh = lambda t: t.reshape(B, T, G, Dh)
    ks = apply_partial_rope(kvh(ks), cos, sin)
    kw = apply_partial_rope(kvh(kw), cos, sin)

    n_cmp = (T - CMP_LEN) // CMP_STRIDE + 1
    blk_idx = np.arange(n_cmp)[:, None] * CMP_STRIDE + np.arange(CMP_LEN)[None, :]
    cmp_end = blk_idx[:, -1]

    def compress(t, pe, w1, w2):
        blocks = kvh(t)[:, blk_idx] + pe[:, None, :]
        flat = blocks.transpose(0, 1, 3, 2, 4).reshape(B, n_cmp, G, CMP_LEN * Dh)
        out = jax.nn.gelu(flat @ w1) @ w2
        return out.transpose(0, 2, 1, 3)

    k_cmp = compress(kc, pe_k, wk1, wk2)
    v_cmp = compress(vc, pe_v, wv1, wv2)

    n_sel = T // SEL_BLOCK
    top = min(SEL_TOPK, n_sel)
    sel_start = np.arange(n_sel) * SEL_BLOCK
    overlap = jnp.asarray(((blk_idx[:, :1] < sel_start[None, :] + SEL_BLOCK)
                           & (cmp_end[:, None] >= sel_start[None, :])).astype(np.float32))
    ks_blocks = ks.transpose(0, 2, 1, 3).reshape(B, G, n_sel, SEL_BLOCK * Dh)
    vs_blocks = kvh(vs).transpose(0, 2, 1, 3).reshape(B, G, n_sel, SEL_BLOCK * Dh)
    sel_ids = jnp.arange(n_sel)

    pad = ((0, 0), (0, 0), (WINDOW, 0), (0, 0))
    kw_pad = jnp.pad(kw.transpose(0, 2, 1, 3), pad)
    vw_pad = jnp.pad(kvh(vw).transpose(0, 2, 1, 3), pad)

    q_all = q.transpose(0, 2, 3, 1, 4)
    qr_all = q_rot.transpose(0, 2, 3, 1, 4)

    def block(c):
        s0 = c * Q_BLOCK
        t_pos = s0 + jnp.arange(Q_BLOCK)
        qb = lax.dynamic_slice_in_dim(q_all, s0, Q_BLOCK, axis=3)
        qr = lax.dynamic_slice_in_dim(qr_all, s0, Q_BLOCK, axis=3)
        s = jnp.einsum('bgrqd,bgnd->bgrqn', qb, k_cmp)
        p_cmp = masked_softmax(s, cmp_end[None, :] <= t_pos[:, None])
        o_cmp = jnp.einsum('bgrqn,bgnd->bgrqd', p_cmp, v_cmp)
        imp = jnp.einsum('bgrqn,ns->bgqs', p_cmp, overlap)
        cur = t_pos // SEL_BLOCK
        forced = ((sel_ids[None, :] == 0) | (sel_ids[None, :] == cur[:, None])
                  | (sel_ids[None, :] == cur[:, None] - 1))
        causal_blk = sel_ids[None, :] * SEL_BLOCK <= t_pos[:, None]
        imp = jnp.where(forced, jnp.inf, imp)
        imp = jnp.where(causal_blk, imp, -jnp.inf)
        _, idx = lax.top_k(imp, top)
        flat_idx = idx.reshape(B, G, Q_BLOCK * top, 1)
        kg = jnp.take_along_axis(ks_blocks, flat_idx, axis=2).reshape(
            B, G, Q_BLOCK, top * SEL_BLOCK, Dh)
        vg = jnp.take_along_axis(vs_blocks, flat_idx, axis=2).reshape(
            B, G, Q_BLOCK, top * SEL_BLOCK, Dh)
        key_pos = (idx[..., None] * SEL_BLOCK + jnp.arange(SEL_BLOCK)).reshape(
            B, G, Q_BLOCK, top * SEL_BLOCK)
        m_sel = (key_pos <= t_pos[None, None, :, None])[:, :, None]
        s = jnp.einsum('bgrqd,bgqkd->bgrqk', qr, kg)
        o_sel = jnp.einsum('bgrqk,bgqkd->bgrqd', masked_softmax(s, m_sel), vg)
        kwb = lax.dynamic_slice_in_dim(kw_pad, s0, WINDOW + Q_BLOCK, axis=2)
        vwb = lax.dynamic_slice_in_dim(vw_pad, s0, WINDOW + Q_BLOCK, axis=2)
        wpos = s0 - WINDOW + jnp.arange(WINDOW + Q_BLOCK)
        diff = t_pos[:, None] - wpos[None, :]
        m_win = (diff >= 0) & (diff < WINDOW) & (wpos[None, :] >= 0)
        s = jnp.einsum('bgrqd,bgkd->bgrqk', qr, kwb)
        o_win = jnp.einsum('bgrqk,bgkd->bgrqd', masked_softmax(s, m_win), vwb)
        return o_cmp, o_sel, o_win

    o_cmp, o_sel, o_win = lax.map(block, jnp.arange(T // Q_BLOCK))
    to_bthd = lambda o: o.transpose(1, 0, 4, 2, 3, 5).reshape(B, T, NSA_HEADS, Dh)
    gt = jax.nn.sigmoid(gates.astype(jnp.float32)).reshape(B, T, 3, NSA_HEADS, 1)
    out = (gt[:, :, 0] * to_bthd(o_cmp) + gt[:, :, 1] * to_bthd(o_sel)
           + gt[:, :, 2] * to_bthd(o_win))
    return out.reshape(B, T, NSA_WIDTH).astype(q.dtype)


def setup_inputs(seed: int = 0) -> dict:
    key = jax.random.key(seed)
    k = jax.random.split(key, 24)
    f32 = jnp.float32
    nrm = lambda kk, shape, sc: jax.random.normal(kk, shape, f32) * sc
    gain = lambda kk, n: 1.0 + 0.05 * jax.random.normal(kk, (DEPTH, n), f32)
    return {
        "x": nrm(k[0], (BATCH, SEQ, D_MODEL), 1.0),
        "p": nrm(k[1], (DEPTH, BATCH, SEQ, PLE_DIM), 1.0),
        "w_in": nrm(k[2], (DEPTH, D_MODEL, N_IN), D_MODEL ** -0.5),
        "w_branch_a": nrm(k[3], (DEPTH, HG_VWIDTH, D_MODEL), HG_VWIDTH ** -0.5),
        "w_branch_b": nrm(k[4], (DEPTH, NSA_WIDTH, D_MODEL), NSA_WIDTH ** -0.5),
        "w_out": nrm(k[5], (DEPTH, D_MODEL, D_MODEL), D_MODEL ** -0.5),
        "norm_pre_mix": gain(k[6], D_MODEL),
        "norm_post_mix": gain(k[7], D_MODEL),
        "norm_pre_mlp": gain(k[8], D_MODEL),
        "norm_post_mlp": gain(k[9], D_MODEL),
        "hg_lb_logits": nrm(k[10], (DEPTH + 1, HG_WIDTH), 0.5),
        "hg_gnorm": gain(k[11], HG_DV),
        "cmp_pe_k": nrm(k[12], (DEPTH, CMP_LEN, NSA_DH), 0.1),
        "cmp_pe_v": nrm(k[13], (DEPTH, CMP_LEN, NSA_DH), 0.1),
        "cmp_wk1": nrm(k[14], (DEPTH, CMP_LEN * NSA_DH, CMP_HIDDEN), (CMP_LEN * NSA_DH) ** -0.5),
        "cmp_wk2": nrm(k[15], (DEPTH, CMP_HIDDEN, NSA_DH), CMP_HIDDEN ** -0.5),
        "cmp_wv1": nrm(k[16], (DEPTH, CMP_LEN * NSA_DH, CMP_HIDDEN), (CMP_LEN * NSA_DH) ** -0.5),
        "cmp_wv2": nrm(k[17], (DEPTH, CMP_HIDDEN, NSA_DH), CMP_HIDDEN ** -0.5),
        "w_up": nrm(k[18], (DEPTH, D_MODEL, D_FF), D_MODEL ** -0.5),
        "w_down": nrm(k[19], (DEPTH, D_FF, D_MODEL), D_FF ** -0.5),
        "w_ple": nrm(k[20], (DEPTH, PLE_DIM, D_MODEL), PLE_DIM ** -0.5),
        "w_ple_gate": nrm(k[21], (DEPTH, D_MODEL, D_MODEL), D_MODEL ** -0.5),
        "norm_ple": gain(k[22], D_MODEL),
    }


def reference(x, p, w_in, w_branch_a, w_branch_b, w_out, norm_pre_mix, norm_post_mix,
              norm_pre_mlp, norm_post_mlp, hg_lb_logits, hg_gnorm, cmp_pe_k, cmp_pe_v,
              cmp_wk1, cmp_wk2, cmp_wv1, cmp_wv2, w_up, w_down, w_ple, w_ple_gate,
              norm_ple):
    lower_bounds = jnp.cumsum(jax.nn.softmax(hg_lb_logits.astype(jnp.float32), axis=0), axis=0)
    h = x
    for i in range(DEPTH):
        u = rms_norm(h, norm_pre_mix[i])
        proj = u @ w_in[i]
        hq, hf, hi, hg, nq, nkv, ngate, ga, gb = jnp.split(proj, SPLIT_POINTS, axis=-1)
        kc, vc, ks, vs, kw, vw = jnp.split(nkv, 6, axis=-1)
        y_a = hgrn2_mixer(hq, hf, hi, hg, lower_bounds[i], hg_gnorm[i])
        y_b = nsa_mixer(nq, kc, vc, ks, vs, kw, vw, ngate, cmp_pe_k[i], cmp_pe_v[i],
                        cmp_wk1[i], cmp_wk2[i], cmp_wv1[i], cmp_wv2[i])
        merged = (jax.nn.sigmoid(ga) * (y_a @ w_branch_a[i])
                  + jax.nn.sigmoid(gb) * (y_b @ w_branch_b[i]))
        h = h + rms_norm(merged @ w_out[i], norm_post_mix[i])
        v = rms_norm(h, norm_pre_mlp[i])
        ff = jnp.square(jax.nn.relu(v @ w_up[i])) @ w_down[i]
        h = h + rms_norm(ff, norm_post_mlp[i])
        e = (p[i] @ w_ple[i]) * jax.nn.sigmoid(h @ w_ple_gate[i])
        h = h + rms_norm(e, norm_ple[i])
    return h
```

```python
import numpy as np
import ml_dtypes
from contextlib import ExitStack
import concourse.bass as bass
import concourse.mybir as mybir
from concourse.bass_utils import run_bass_kernel_spmd

F32 = mybir.dt.float32
BF16 = mybir.dt.bfloat16
AF = mybir.ActivationFunctionType
ALU = mybir.AluOpType
NPBF = ml_dtypes.bfloat16

NDMASEM = 8
import os
RELAX_OK = os.environ.get("KDBG_RELAX", "0") == "1"
QUEUES = ("sp", "act", "pool")
ENGS = ("pe", "act", "dve", "pool", "sp")


class Op:
    __slots__ = ("eng", "fn", "deps", "marked", "cnt", "is_dma", "dsem", "dval", "relax")

    def __init__(self, eng, fn, is_dma=False):
        self.relax = False
        self.eng = eng
        self.fn = fn
        self.deps = []
        self.marked = False
        self.cnt = None
        self.is_dma = is_dma
        self.dsem = None
        self.dval = None


class Prog:
    def __init__(self):
        self.ops = {e: [] for e in ENGS}
        self.last_w = {}
        self.readers = {}
        self.dma_n = {q: 0 for q in QUEUES}
        self.dma_hist = {q: [] for q in QUEUES}
        self.out_dmas = []
        self.bar_deps = []
        self.bar_pending = set()

    def _add_dep(self, op, d):
        if d is None or d is op:
            return
        if (not d.is_dma) and d.eng == op.eng and op.eng == "pe" and not op.is_dma:
            return
        if op.relax and (not d.is_dma) and (not op.is_dma) and d.eng == op.eng:
            return
        if d not in op.deps:
            op.deps.append(d)
            d.marked = True

    def _track(self, op, reads, writes):
        if op.eng in self.bar_pending:
            self.bar_pending.discard(op.eng)
            for d in self.bar_deps:
                self._add_dep(op, d)
        for k in reads:
            self._add_dep(op, self.last_w.get(k))
        for k in writes:
            self._add_dep(op, self.last_w.get(k))
            for r in self.readers.get(k, ()):
                self._add_dep(op, r)
        for k in reads:
            lst = self.readers.setdefault(k, [])
            if not op.is_dma:
                for i in range(len(lst) - 1, -1, -1):
                    if (not lst[i].is_dma) and lst[i].eng == op.eng:
                        del lst[i]
            lst.append(op)
        for k in writes:
            self.last_w[k] = op
            self.readers[k] = []

    def op(self, eng, fn, reads=(), writes=(), relax=False):
        o = Op(eng, fn)
        o.relax = relax and RELAX_OK
        self._track(o, reads, writes)
        self.ops[eng].append(o)
        return o

    def dma(self, q, fn, reads=(), writes=(), is_output=False):
        o = Op(q, fn, is_dma=True)
        n = self.dma_n[q]
        self.dma_n[q] = n + 1
        o.dsem = (q, n % NDMASEM)
        o.dval = 16 * (n // NDMASEM + 1)
        hist = self.dma_hist[q]
        if n >= NDMASEM:
            o.deps.append(hist[n - NDMASEM])
        hist.append(o)
        self._track(o, reads, writes)
        self.ops[q].append(o)
        if is_output:
            self.out_dmas.append(o)
        return o

    def barrier(self):
        deps = []
        for e in ENGS:
            for o in reversed(self.ops[e]):
                if not o.is_dma:
                    deps.append(o)
                    break
        for q in QUEUES:
            deps.extend(self.dma_hist[q][-NDMASEM:])
        self.bar_deps = deps
        self.bar_pending = set(ENGS)
        self.last_w = {}
        self.readers = {}

    def emit(self, nc):
        EPOCH = 4000
        nep = {}
        for e in ENGS:
            c = 0
            for o in self.ops[e]:
                if not o.is_dma and o.marked:
                    c += 1
                    o.cnt = c
            nep[e] = (c + EPOCH - 1) // EPOCH
        with ExitStack() as es:
            csem = {}
            for e in ENGS:
                for ep in range(nep[e]):
                    csem[(e, ep)] = es.enter_context(nc.semaphore(f"s_{e}{ep}"))
            dsem = {}
            for q in QUEUES:
                if self.dma_n[q] == 0:
                    continue
                for i in range(NDMASEM):
                    dsem[(q, i)] = es.enter_context(nc.semaphore(f"d_{q}{i}"))
            block = es.enter_context(nc.Block())
            final = list(self.out_dmas)

            def run(ename, eng):
                waited = {}
                for o in self.ops[ename]:
                    need = {}
                    for d in o.deps:
                        if d.is_dma:
                            key, val, sem = d.dsem, d.dval, dsem[d.dsem]
                            if waited.get(key, 0) >= val:
                                continue
                            if key not in need or need[key][1] < val:
                                need[key] = (sem, val, val)
                        else:
                            ep, v = (d.cnt - 1) // EPOCH, (d.cnt - 1) % EPOCH + 1
                            if waited.get(d.eng, (-1, 0)) >= (ep, v):
                                continue
                            if d.eng not in need or need[d.eng][2] < (ep, v):
                                need[d.eng] = (csem[(d.eng, ep)], v, (ep, v))
                    items = list(need.items())
                    for key, (sem, val, rec) in items[:-1]:
                        eng.wait_ge(sem, val)
                        waited[key] = rec
                    ins = o.fn(eng)
                    if items:
                        key, (sem, val, rec) = items[-1]
                        ins._wait_ge(sem, val)
                        waited[key] = rec
                    if o.is_dma:
                        ins.then_inc(dsem[o.dsem], 16)
                    elif o.marked:
                        ins.then_inc(csem[(ename, (o.cnt - 1) // EPOCH)], 1)
                if ename == "sp":
                    for d in final:
                        if waited.get(d.dsem, 0) >= d.dval:
                            continue
                        eng.wait_ge(dsem[d.dsem], d.dval)
                        waited[d.dsem] = d.dval

            block.tensor(lambda eng: run("pe", eng))
            block.scalar(lambda eng: run("act", eng))
            block.vector(lambda eng: run("dve", eng))
            block.gpsimd(lambda eng: run("pool", eng))
            block.sync(lambda eng: run("sp", eng))


class Arena:
    def __init__(self, tile, size, name):
        self.tile = tile
        self.size = size
        self.off = 0
        self.name = name
        self.gen = 0

    def reset(self):
        self.off = 0
        self.gen += 1

    def alloc_bf16(self, shape):
        n = int(np.prod(shape))
        assert n % 2 == 0
        nf = n // 2
        assert self.off + nf <= self.size, (self.name, self.off, nf, self.size)
        ap = self.tile[:, self.off:self.off + nf].bitcast(BF16)
        key = f"{self.name}{self.gen}_{self.off}b"
        self.off += nf
        if len(shape) == 2:
            ap = ap.rearrange("p (a b) -> p a b", a=shape[0])
        elif len(shape) == 3:
            ap = ap.rearrange("p (a b c) -> p a b c", a=shape[0], b=shape[1])
        return ap, key

    def alloc(self, shape):
        n = int(np.prod(shape))
        assert self.off + n <= self.size, (self.name, self.off, n, self.size)
        ap = self.tile[:, self.off:self.off + n]
        key = f"{self.name}{self.gen}_{self.off}"
        self.off += n
        if len(shape) == 2:
            ap = ap.rearrange("p (a b) -> p a b", a=shape[0])
        elif len(shape) == 3:
            ap = ap.rearrange("p (a b c) -> p a b c", a=shape[0], b=shape[1])
        return ap, key


T = 4096
HALF = 2048
D = 1024
C_HQ, C_HF, C_HI, C_HG, C_NQ = 0, 1024, 2048, 3072, 4096
C_KC, C_VC, C_KS, C_VS, C_KW, C_VW = 5120, 5376, 5632, 5888, 6144, 6400
C_NG, C_GA, C_GB, N_IN = 6656, 6704, 7728, 8752
EPS = 1e-6
NEGB = -30000.0

import os
MASK_ENG = os.environ.get("KDBG_MASKENG", "pool")
DBG_BRANCHES = tuple(int(c) for c in os.environ.get("KDBG_BRANCHES", "12"))
DBG_NOBIAS = os.environ.get("KDBG_NOBIAS", "0") == "1"
DBG_NOEPI = os.environ.get("KDBG_NOEPI", "0") == "1"
DBG_NOPV = os.environ.get("KDBG_NOPV", "0") == "1"
KD_BCAST = os.environ.get("KDBG_KDBCAST", "1") == "1"
STORE_QUEUES = tuple(x for x in os.environ.get("KDBG_STOREQ", "act").split(",") if x)
FSZ = 22 * 1024
BSZ = 46 * 1024


def build(debug=None, stop_after=None):
    nc = bass.Bass("TRN2", target_bir_lowering=False)
    P = Prog()

    def din(name, shape, dt=F32):
        return nc.dram_tensor(name, list(shape), dt, kind="ExternalInput").ap()

    def dscr(name, shape, dt=BF16):
        return nc.dram_tensor(name, list(shape), dt, kind="Internal").ap()

    xo = din("xo", [HALF, D]); xc = din("xc", [HALF, D]); po = din("po", [HALF, 256])
    w_in = din("w_in", [D, N_IN])
    w_a = din("w_a", [D, D]); w_b = din("w_b", [D, D]); w_o = din("w_o", [D, D])
    w_up = din("w_up", [D, 4096]); w_down = din("w_down", [4096, D])
    w_ple = din("w_ple", [256, D]); w_pg = din("w_pg", [D, D])
    n_pre = din("n_pre", [1, D]); n_post = din("n_post", [1, D]); n_mpre = din("n_mpre", [1, D])
    n_mpost = din("n_mpost", [1, D]); n_ple = din("n_ple", [1, D])
    lbl = din("lbl", [2, D]); gnw = din("gnw", [1, 128])
    pe_k = din("pe_k", [32, 64]); pe_v = din("pe_v", [32, 64])
    wk1 = din("wk1", [2048, 256]); wk2 = din("wk2", [256, 64])
    wv1 = din("wv1", [2048, 256]); wv2 = din("wv2", [256, 64])
    c_ident = din("c_ident", [128, 128], BF16)
    c_perm = din("c_perm", [128, 128], BF16)
    c_cos = din("c_cos", [128, T]); c_sin = din("c_sin", [128, T])
    c_E = din("c_E", [64, T], BF16)
    c_triL = din("c_triL", [128, 128], BF16); c_triU = din("c_triU", [128, 128], BF16)
    c_ovl = din("c_ovl", [256, 64])
    c_cmpb = din("c_cmpb", [2, 128, HALF], BF16)
    c_selA = din("c_selA", [HALF, 64]); c_selB = din("c_selB", [HALF, 64])
    c_kval = din("c_kval", [128, 32])
    out = nc.dram_tensor("out", [HALF, D], F32, kind="ExternalOutput").ap()

    s_qh = dscr("s_qh", [D, HALF]); s_kh = dscr("s_kh", [D, HALF])
    s_kd = dscr("s_kd", [T, D]); s_v = dscr("s_v", [T, D]); s_g = dscr("s_g", [HALF, D], F32)
    s_q = dscr("s_q", [D, HALF]); s_qr = dscr("s_qr", [D, HALF])
    s_kc = dscr("s_kc", [256, T]); s_vc = dscr("s_vc", [256, T])
    s_ks = dscr("s_ks", [256, T]); s_kw = dscr("s_kw", [256, T])
    s_vs = dscr("s_vs", [T, 256]); s_vw = dscr("s_vw", [T, 256])
    s_ga = dscr("s_ga", [D, HALF]); s_gb = dscr("s_gb", [D, HALF])
    s_ya = dscr("s_ya", [HALF, D]); s_yb = dscr("s_yb", [HALF, D])
    s_h1 = dscr("s_h1", [HALF, D], F32); s_h2 = dscr("s_h2", [HALF, D], F32); s_vT = dscr("s_vT", [D, HALF])

    dbg = {}
    if debug:
        for name, spec in debug.items():
            shape, dts = spec
            dbg[name] = nc.dram_tensor("dbg_" + name, list(shape), BF16 if dts == "bf16" else F32, kind="ExternalOutput").ap()

    es = ExitStack()
    with es:
        sbt = lambda name, shape, dt: es.enter_context(nc.sbuf_tensor(name, shape, dt))
        pst = lambda name, shape, dt: es.enter_context(nc.psum_tensor(name, shape, dt))
        fa_t = sbt("arenaF", [128, FSZ], F32)
        ba_t = sbt("arenaB", [128, BSZ], BF16)
        FA = Arena(fa_t, FSZ, "F")
        BA = Arena(ba_t, BSZ, "B")
        ident = sbt("ident", [128, 128], BF16)
        perm = sbt("perm", [128, 128], BF16)
        triL = sbt("triL", [128, 128], BF16)
        triU = sbt("triU", [128, 128], BF16)
        lb = sbt("lb", [128, 8], F32)
        omlb = sbt("omlb", [128, 8], F32)
        lb2 = sbt("lb2", [128, 8], F32)
        PL = sbt("PL", [128, 8, 64], F32)
        gates = sbt("gates", [128, 16, 48], F32)
        zeros = sbt("zeros", [128, 64], F32)
        epsT = sbt("epsT", [128, 1], F32)
        tinyT = sbt("tinyT", [128, 1], F32)
        onesT = sbt("onesT", [128, 1], F32)
        junk = sbt("junk", [128, 1024], BF16)
        stat = sbt("stat", [128, 64], F32)
        PSF = [pst(f"psf{i}", [128, 512], F32) for i in range(6)]
        PSB = [pst(f"psb{i}", [128, 1024], BF16) for i in range(2)]

        def ld(dst, src, key, q="sp", reads=(), slow=False):
            if slow:
                return P.dma(q, lambda e: e.dma_start(out=dst, in_=src, allow_slow_non_contiguous=True), reads=list(reads), writes=[key])
            return P.dma(q, lambda e: e.dma_start(out=dst, in_=src), reads=list(reads), writes=[key])

        uq = [0]

        def st(dst, src, key, wkey, q=None, is_output=False):
            if wkey == "SCR_ALL":
                uq[0] += 1
                wkey = f"scr{uq[0]}"
            if q is None:
                lw = P.last_w.get(key)
                q = lw.eng if (lw is not None and not lw.is_dma and lw.eng in STORE_QUEUES) else "pool"
            return P.dma(q, lambda e: e.dma_start(out=dst, in_=src), reads=[key], writes=[wkey], is_output=is_output)

        def mm(o, lhsT, rhs, start, stop, reads, okey, skip=False):
            if skip:
                return P.op("pe", lambda e: e.matmul(o, lhsT=lhsT, rhs=rhs, start=start, stop=stop, skip_group_check=True), reads=reads, writes=[okey])
            return P.op("pe", lambda e: e.matmul(o, lhsT=lhsT, rhs=rhs, start=start, stop=stop), reads=reads, writes=[okey])

        def tr(o, in_, reads, okey):
            return P.op("pe", lambda e: e.transpose(out=o, in_=in_, identity=ident[:]), reads=list(reads) + ["ident"], writes=[okey])

        def act(o, in_, func, reads, okey, bias=None, scale=None, accum=None, eng="act"):
            kw = {}
            if bias is not None:
                kw["bias"] = bias
            if scale is not None:
                kw["scale"] = scale
            if accum is not None:
                kw["accum_out"] = accum
            wk = [okey] if isinstance(okey, str) else list(okey)
            return P.op("act", lambda e: e.activation(out=o, in_=in_, func=func, **kw), reads=reads, writes=wk)

        def cp(eng, o, in_, reads, okey):
            if eng == "act":
                return P.op("act", lambda e: e.copy(out=o, in_=in_), reads=reads, writes=[okey])
            return P.op(eng, lambda e: e.tensor_copy(out=o, in_=in_), reads=reads, writes=[okey])

        def tt(eng, o, a, b, op, reads, okey, relax=False):
            return P.op(eng, lambda e: e.tensor_tensor(out=o, in0=a, in1=b, op=op), reads=reads, writes=[okey], relax=relax)

        def tsc(eng, o, a, s1, s2, op0, op1, reads, okey, relax=False):
            if op1 is None:
                return P.op(eng, lambda e: e.tensor_scalar(out=o, in0=a, scalar1=s1, scalar2=None, op0=op0), reads=reads, writes=[okey], relax=relax)
            return P.op(eng, lambda e: e.tensor_scalar(out=o, in0=a, scalar1=s1, scalar2=s2, op0=op0, op1=op1), reads=reads, writes=[okey], relax=relax)

        def stt(eng, o, a, s, b, op0, op1, reads, okey, relax=False):
            return P.op(eng, lambda e: e.scalar_tensor_tensor(out=o, in0=a, scalar=s, in1=b, op0=op0, op1=op1), reads=reads, writes=[okey], relax=relax)

        def dbg_out(name, src_ap, key):
            if name in dbg:
                st(dbg[name], src_ap, key, "dbg_" + name, q="sp", is_output=True)

        ld(ident[:], c_ident, "ident"); ld(perm[:], c_perm, "perm")
        ld(triL[:], c_triL, "triL"); ld(triU[:], c_triU, "triU")
        P.op("pool", lambda e: e.memset(zeros[:], 0.0), writes=["zeros"])
        P.op("pool", lambda e: e.memset(epsT[:], EPS), writes=["epsT"])
        P.op("pool", lambda e: e.memset(tinyT[:], 1e-30), writes=["tinyT"])
        P.op("pool", lambda e: e.memset(onesT[:], 1.0), writes=["onesT"])
        ld(lb[:], lbl[0, :].rearrange("(h p) -> p h", p=128), "lb", slow=True)
        ld(lb2[:], lbl[1, :].rearrange("(h p) -> p h", p=128), "lb2", slow=True)
        tt("dve", lb[:], lb[:], lb2[:], ALU.subtract, ["lb", "lb2"], "lb")
        act(lb[:], lb[:], AF.Sigmoid, ["lb"], "lb")
        tsc("dve", omlb[:], lb[:], -1.0, 1.0, ALU.mult, ALU.add, ["lb"], "omlb")

        def rms_rstd(src, skey, slot, n=1024.0, eng_reads=()):
            ss = stat[:, slot:slot + 1]
            k = f"stat{slot}"
            act(junk[:, :src.shape[-1]] if len(src.shape) == 2 else junk[:], src, AF.Square, [skey] + list(eng_reads), [k, "junk"], accum=ss)
            act(ss, ss, AF.Sqrt, [k, "epsT"], k, bias=epsT[:, 0:1], scale=1.0 / n)
            P.op("dve", lambda e: e.reciprocal(out=ss, in_=ss), reads=[k], writes=[k])
            return ss, k

        uT, k_uT = BA.alloc([8, T])
        nbc, k_nbc = FA.alloc([D])
        ld(nbc, n_pre.to_broadcast([128, D]), k_nbc)
        xts = [FA.alloc([D]) for _ in range(2)]
        xns = [BA.alloc([D]) for _ in range(2)]
        for ti in range(32):
            src = xc if ti < 16 else xo
            r0 = (ti % 16) * 128
            xt, kx = xts[ti % 2]
            xn, kn = xns[ti % 2]
            ld(xt, src[r0:r0 + 128, :], kx)
            ss, ks_ = rms_rstd(xt, kx, ti % 2)
            stt("dve", xn, xt, ss, nbc, ALU.mult, ALU.mult, [kx, ks_, k_nbc], kn)
            pb = PSB[ti % 2]
            for k in range(8):
                tr(pb[:, k * 128:(k + 1) * 128], xn[:, k * 128:(k + 1) * 128], [kn], f"psb{ti % 2}")
            dst = uT[:, :, ti * 128:(ti + 1) * 128]
            cp("act" if ti % 2 == 0 else "dve", dst, pb[:].rearrange("p (k t) -> p k t", k=8), [f"psb{ti % 2}"], f"uT{ti}")

        pending_dumps = []

        def dump_scr(name, src):
            if name in dbg:
                pending_dumps.append((name, src))

        def flush_dumps():
            for name, src in pending_dumps:
                P.dma("sp", (lambda d_, s_: (lambda e: e.dma_start(out=d_, in_=s_)))(dbg[name], src), reads=[], writes=["dbg_" + name], is_output=True)
            pending_dumps.clear()

        P.barrier()
        FA.reset()
        BA.off = 8 * T
        uT_keys_blk = lambda tb: [f"uT{ti}" for ti in range(4 * tb, 4 * tb + 4)]
        Wst = [FA.alloc([8, 512]) for _ in range(2)]
        Wb = [BA.alloc([8, 512]) for _ in range(2)]
        gw512, k_gw = FA.alloc([512])
        for i in range(4):
            ld(gw512[:, i * 128:(i + 1) * 128], gnw.to_broadcast([128, 128]), k_gw)
        STREAM = [0]
        WbH = FA.alloc_bf16([8, 512])
        ftiles_s = [[FA.alloc([512]) for _ in range(12)], [FA.alloc([512]) for _ in range(8)]]
        btiles_s = [[BA.alloc([512]) for _ in range(8)], [BA.alloc([512]) for _ in range(4)]]
        fctr = [0, 0]; bctr = [0, 0]

        def ftile():
            s_ = STREAM[0]
            fctr[s_] += 1
            return ftiles_s[s_][fctr[s_] % len(ftiles_s[s_])]

        def btile():
            s_ = STREAM[0]
            bctr[s_] += 1
            return btiles_s[s_][bctr[s_] % len(btiles_s[s_])]

        gctr = [0]
        wbctr = [0]
        psctr = [0, 0]

        def next_ps():
            s_ = STREAM[0]
            psctr[s_] += 1
            if s_ == 0:
                i = psctr[0] % 2
            else:
                i = 2 + psctr[1] % 4
            return PSF[i], f"psf{i}"

        def load_group(col_segs):
            gi = gctr[0] % 2
            gctr[0] += 1
            ws, kws = Wst[gi]
            if STREAM[0] == 0:
                wb, kwb = WbH
            else:
                wb, kwb = Wb[wbctr[0] % 2]
                wbctr[0] += 1
            off = 0
            for (c0, n) in col_segs:
                ld(ws[:, :, off:off + n], w_in[:, c0:c0 + n].rearrange("(k p) n -> p k n", p=128), kws)
                off += n
            cp("act", wb[:, 0:4, 0:off], ws[:, 0:4, 0:off], [kws], kwb)
            cp("act", wb[:, 4:8, 0:off], ws[:, 4:8, 0:off], [kws], kwb)
            return wb, kwb

        def ftype(wb, kwb, c_off, tb):
            ps, kps = next_ps()
            for k in range(8):
                mm(ps[:], wb[:, k, c_off:c_off + 128], uT[:, k, tb * 512:(tb + 1) * 512], k == 0, k == 7,
                   [kwb] + uT_keys_blk(tb), kps)
            return ps, kps

        def ttype(wb, kwb, ncols, ti):
            ps, kps = next_ps()
            for k in range(8):
                mm(ps[:, 0:ncols], uT[:, k, ti * 128:(ti + 1) * 128], wb[:, k, 0:ncols], k == 0, k == 7,
                   [kwb, f"uT{ti}"], kps)
            return ps, kps

        hg_tails = []

        def hgrn_gen():
            STREAM[0] = 0
            HSCALE = 128.0 ** -0.5
            for gi4 in range(4):
                wb, kwb = load_group([(C_HQ + 256 * gi4, 256), (C_HF + 256 * gi4, 256)])
                for hh in range(2):
                    hd = 2 * gi4 + hh
                    for tb in range(8):
                        own = tb >= 4
                        pf, kpf = ftype(wb, kwb, 256 + hh * 128, tb)
                        if hg_tails:
                            hg_tails.pop(0)()
                        sg, ksg = ftile()
                        act(sg, pf[:], AF.Sigmoid, [kpf], ksg)
                        if own:
                            pq, kpq = ftype(wb, kwb, hh * 128, tb)
                            sq, ksq = ftile()
                            act(sq, pq[:], AF.Sigmoid, [kpq], ksq)
                            tt("dve", sq, sq, pq[:], ALU.mult, [ksq, kpq], ksq, relax=True)
                        fg, kfg = ftile()
                        act(fg, sg, AF.Identity, [ksg, "omlb", "lb"], kfg, bias=lb[:, hd:hd + 1], scale=omlb[:, hd:hd + 1])
                        Pt, kP = ftile()
                        for c in range(8):
                            P.op("dve", (lambda o_, d0: (lambda e: e.tensor_tensor_scan(out=o_, data0=d0, data1=zeros[:], initial=1.0,
                                                                                        op0=ALU.mult, op1=ALU.add)))(Pt[:, c * 64:(c + 1) * 64], fg[:, c * 64:(c + 1) * 64]),
                                 reads=[kfg, "zeros"], writes=[kP], relax=True)
                        cp("act", PL[:, hd, tb * 8:(tb + 1) * 8], Pt[:, 63::64], [kP], f"PL{hd}")
                        rP, krP = ftile()
                        tsc("dve", rP, Pt, 1e-30, None, ALU.max, None, [kP], krP, relax=True)
                        P.op("dve", (lambda o_: (lambda e: e.reciprocal(out=o_, in_=o_)))(rP), reads=[krP], writes=[krP], relax=True)
                        kk_, kkk = ftile()
                        act(kk_, fg, AF.Identity, [kfg, "onesT"], kkk, bias=onesT[:, 0:1], scale=-1.0)
                        kt, kkt = btile()
                        tt("dve", kt, kk_, rP, ALU.mult, [kkk, krP], kkt, relax=True)
                        kd, kkd = btile()
                        if KD_BCAST:
                            plb = Pt.rearrange("p (c s) -> p c s", s=64)[:, :, 63:64].to_broadcast([128, 8, 64])
                            tt("pool", kd.rearrange("p (c s) -> p c s", s=64), kt.rearrange("p (c s) -> p c s", s=64), plb, ALU.mult, [kkt, kP], kkd)
                        else:
                            for c in range(8):
                                tsc("pool", kd[:, c * 64:(c + 1) * 64], kt[:, c * 64:(c + 1) * 64], Pt[:, c * 64 + 63:c * 64 + 64], None,
                                    ALU.mult, None, [kkt, kP], kkd)
                        def tail(hd=hd, tb=tb, kd=kd, kkd=kkd):
                            pbi = (hd * 8 + tb) % 2
                            pb = PSB[pbi]
                            for i in range(4):
                                tr(pb[:, i * 128:(i + 1) * 128], kd[:, i * 128:(i + 1) * 128], [kkd], f"psb{pbi}")
                            kdT, kkdT = btile()
                            cp("act", kdT, pb[:, 0:512], [f"psb{pbi}"], kkdT)
                            st(s_kd[tb * 512:(tb + 1) * 512, hd * 128:(hd + 1) * 128].rearrange("(i p) k -> p i k", p=128),
                               kdT.rearrange("p (i k) -> p i k", i=4), kkdT, "SCR_ALL")
                        hg_tails.append(tail)
                        if own:
                            st(s_kh[hd * 128:(hd + 1) * 128, (tb - 4) * 512:(tb - 3) * 512], kt, kkt, "SCR_ALL")
                            qh, kqh = btile()
                            stt("dve", qh, sq, HSCALE, Pt, ALU.mult, ALU.mult, [ksq, kP], kqh, relax=True)
                            st(s_qh[hd * 128:(hd + 1) * 128, (tb - 4) * 512:(tb - 3) * 512], qh, kqh, "SCR_ALL")
                        yield
            while hg_tails:
                hg_tails.pop(0)()
            yield
        dump_scr("s_qh", s_qh); dump_scr("s_kh", s_kh); dump_scr("s_kd", s_kd)
        if "PL" in dbg:
            P.dma("sp", lambda e: e.dma_start(out=dbg["PL"], in_=PL[:]), reads=[f"PL{h}" for h in range(8)], writes=["dbg_PL"], is_output=True)

        def rest_gen():
            STREAM[0] = 1
            for g2 in range(2):
                wb, kwb = load_group([(C_HI + 512 * g2, 512)])
                for ti in range(32):
                    ps, kps = ttype(wb, kwb, 512, ti)
                    vb, kvb = btile()
                    cp("act", vb, ps[:], [kps], kvb)
                    st(s_v[ti * 128:(ti + 1) * 128, g2 * 512:(g2 + 1) * 512], vb, kvb, "SCR_ALL")
                    yield
            for g2 in range(2):
                wb, kwb = load_group([(C_HG + 512 * g2, 512)])
                for ti in range(16, 32):
                    ps, kps = ttype(wb, kwb, 512, ti)
                    sgt, ksgt = ftile()
                    act(sgt, ps[:], AF.Sigmoid, [kps], ksgt)
                    tt("dve", sgt, sgt, ps[:], ALU.mult, [ksgt, kps], ksgt, relax=True)
                    gb_, kgb = ftile()
                    tt("dve", gb_, sgt, gw512, ALU.mult, [ksgt, k_gw], kgb, relax=True)
                    st(s_g[(ti - 16) * 128:(ti - 15) * 128, g2 * 512:(g2 + 1) * 512], gb_, kgb, "SCR_ALL")
                    yield

            def rope_store(ps, kps, pos0, scale, dst_plain, dst_rot):
                qb, kqb = btile()
                act(qb, ps[:], AF.Copy, [kps], kqb, scale=scale)
                if dst_plain is not None:
                    st(dst_plain, qb, kqb, "SCR_ALL")
                cs, kcs = ftile(); sn, ksn = ftile()
                ld(cs, c_cos[:, pos0:pos0 + 512], kcs); ld(sn, c_sin[:, pos0:pos0 + 512], ksn)
                pp, kpp = next_ps()
                mm(pp[:], perm[:], qb, True, True, ["perm", kqb], kpp)
                t1, kt1 = ftile()
                tt("dve", t1, qb, cs, ALU.mult, [kqb, kcs], kt1, relax=True)
                t2, kt2 = ftile()
                tt("dve", t2, pp[:], sn, ALU.mult, [kpp, ksn], kt2, relax=True)
                qr, kqr = btile()
                tt("dve", qr, t1, t2, ALU.add, [kt1, kt2], kqr, relax=True)
                st(dst_rot, qr, kqr, "SCR_ALL")

            for g2 in range(2):
                wb, kwb = load_group([(C_NQ + 512 * g2, 512)])
                for ct in range(4):
                    row0 = g2 * 512 + ct * 128
                    for tb in range(4, 8):
                        ps, kps = ftype(wb, kwb, ct * 128, tb)
                        c0 = (tb - 4) * 512
                        rope_store(ps, kps, tb * 512, 0.125, s_q[row0:row0 + 128, c0:c0 + 512], s_qr[row0:row0 + 128, c0:c0 + 512])
                        yield
            wb, kwb = load_group([(C_KC, 512)])
            for ct in range(4):
                dst = s_kc if ct < 2 else s_vc
                row0 = (ct % 2) * 128
                for tb in range(8):
                    ps, kps = ftype(wb, kwb, ct * 128, tb)
                    ob, kob = btile()
                    cp("act", ob, ps[:], [kps], kob)
                    st(dst[row0:row0 + 128, tb * 512:(tb + 1) * 512], ob, kob, "SCR_ALL")
                    yield
            wb, kwb = load_group([(C_KS, 256), (C_KW, 256)])
            for ct in range(4):
                dst = s_ks if ct < 2 else s_kw
                row0 = (ct % 2) * 128
                for tb in range(8):
                    ps, kps = ftype(wb, kwb, ct * 128, tb)
                    rope_store(ps, kps, tb * 512, 1.0, None, dst[row0:row0 + 128, tb * 512:(tb + 1) * 512])
                    yield
            wb, kwb = load_group([(C_VS, 256), (C_VW, 256)])
            for ti in range(32):
                ps, kps = ttype(wb, kwb, 512, ti)
                vb, kvb = btile()
                cp("act" if ti % 2 else "dve", vb, ps[:], [kps], kvb)
                st(s_vs[ti * 128:(ti + 1) * 128, :], vb[:, 0:256], kvb, "SCR_ALL")
                st(s_vw[ti * 128:(ti + 1) * 128, :], vb[:, 256:512], kvb, "SCR_ALL")
                yield
            wb, kwb = load_group([(C_NG, 48)])
            for ti in range(16, 32):
                ps, kps = ttype(wb, kwb, 48, ti)
                act(gates[:, ti - 16, :], ps[:, 0:48], AF.Sigmoid, [kps], f"gates{ti - 16}")
                yield
            for gsel, (c_base, dst) in enumerate(((C_GA, s_ga), (C_GB, s_gb))):
                for g2 in range(2):
                    wb, kwb = load_group([(c_base + 512 * g2, 512)])
                    for ct in range(4):
                        row0 = g2 * 512 + ct * 128
                        for tb in range(4, 8):
                            ps, kps = ftype(wb, kwb, ct * 128, tb)
                            ob, kob = btile()
                            act(ob, ps[:], AF.Sigmoid, [kps], kob)
                            st(dst[row0:row0 + 128, (tb - 4) * 512:(tb - 3) * 512], ob, kob, "SCR_ALL")
                            yield

        g_h = hgrn_gen(); g_r = rest_gen()
        alive_h = alive_r = True
        while alive_h or alive_r:
            if alive_h:
                STREAM[0] = 0
                try:
                    next(g_h)
                except StopIteration:
                    alive_h = False
            for _ in range(5 if alive_h else 1000000):
                if not alive_r:
                    break
                STREAM[0] = 1
                try:
                    next(g_r)
                except StopIteration:
                    alive_r = False
        STREAM[0] = 0

        for nm, ap_ in (("s_v", s_v), ("s_g", s_g), ("s_q", s_q), ("s_qr", s_qr), ("s_kc", s_kc), ("s_vc", s_vc), ("s_ks", s_ks),
                        ("s_kw", s_kw), ("s_vs", s_vs), ("s_vw", s_vw), ("s_ga", s_ga), ("s_gb", s_gb)):
            dump_scr(nm, ap_)
        if "gates" in dbg:
            P.dma("sp", lambda e: e.dma_start(out=dbg["gates"], in_=gates[:]), reads=[f"gates{i}" for i in range(16)], writes=["dbg_gates"], is_output=True)

        P.barrier()
        flush_dumps()

        FA.reset(); BA.reset()
        NBLK = 16
        Sst = [FA.alloc([128]) for _ in range(8)]
        Sb = [BA.alloc([128]) for _ in range(8)]
        for hd in range(8):
            P.op("pool", (lambda o_: (lambda e: e.memset(o_, 0.0)))(Sst[hd][0]), writes=[Sst[hd][1]])
        kdB = [[BA.alloc([4, 128]) for _ in range(2)] for _ in range(8)]
        vB = [[BA.alloc([4, 128]) for _ in range(2)] for _ in range(8)]
        qB = [[BA.alloc([256]) for _ in range(2)] for _ in range(8)]
        kB = [[BA.alloc([256]) for _ in range(2)] for _ in range(8)]
        gB = [[FA.alloc([4, 128]) for _ in range(2)] for _ in range(8)]
        yst = [[BA.alloc([4, 128]) for _ in range(2)] for _ in range(8)]
        ATs = [BA.alloc([64]) for _ in range(4)]
        c_tail = []
        for blk in range(NBLK):
            own = blk >= 8
            bi = blk % 2
            for hd in range(8):
                t0 = blk * 256
                kd_t, kkd = kdB[hd][bi]; v_t, kv = vB[hd][bi]
                ld(kd_t[0:64], s_kd[t0:t0 + 256, hd * 128:(hd + 1) * 128].rearrange("(j s) k -> s j k", s=64), kkd)
                ld(v_t[0:64], s_v[t0:t0 + 256, hd * 128:(hd + 1) * 128].rearrange("(j s) k -> s j k", s=64), kv)
                if own:
                    o0 = t0 - HALF
                    ld(qB[hd][bi][0], s_qh[hd * 128:(hd + 1) * 128, o0:o0 + 256], qB[hd][bi][1])
                    ld(kB[hd][bi][0], s_kh[hd * 128:(hd + 1) * 128, o0:o0 + 256], kB[hd][bi][1])
                    ld(gB[hd][bi][0][0:64], s_g[o0:o0 + 256, hd * 128:(hd + 1) * 128].rearrange("(j s) k -> s j k", s=64), gB[hd][bi][1])
            for j in range(4):
                c = blk * 4 + j

                def issue_A(hd_):
                    q_t_, kq_ = qB[hd_][bi]; k_t_, kk2_ = kB[hd_][bi]
                    psA_, kpsA_ = PSF[4 + hd_ % 2], f"psf{4 + hd_ % 2}"
                    mm(psA_[0:64, 0:64], k_t_[:, j * 64:(j + 1) * 64], q_t_[:, j * 64:(j + 1) * 64], True, True, [kk2_, kq_], kpsA_)
                    at_t_, kat_ = ATs[(c * 8 + hd_) % 4]
                    tt("dve", at_t_[0:64, :], psA_[0:64, 0:64], triL[0:64, 0:64], ALU.mult, [kpsA_, "triL"], kat_)

                if own:
                    issue_A(0)
                for hd in range(8):
                    kd_t, kkd = kdB[hd][bi]; v_t, kv = vB[hd][bi]
                    S_t, kS = Sst[hd]; Sb_t, kSb = Sb[hd]
                    psi = hd % 2
                    if own:
                        q_t, kq = qB[hd][bi]; k_t, kk2 = kB[hd][bi]; g_t, kg = gB[hd][bi]
                        y_t, ky = yst[hd][bi]
                        psO, kpsO = PSF[2 + psi], f"psf{2 + psi}"
                        if hd + 1 < 8:
                            issue_A(hd + 1)
                        mm(psO[0:64, 0:128], q_t[:, j * 64:(j + 1) * 64], Sb_t, True, False, [kq, kSb], kpsO)
                        at_t, kat = ATs[(c * 8 + hd) % 4]
                        mm(psO[0:64, 0:128], at_t[0:64, :], v_t[0:64, j, :], False, True, [kat, kv], kpsO)
                    psS, kpsS = PSF[psi], f"psf{psi}"
                    mm(psS[:, 0:128], kd_t[0:64, j, :], v_t[0:64, j, :], True, True, [kkd, kv], kpsS)
                    stt("dve", S_t, S_t, PL[:, hd, c:c + 1], psS[:, 0:128], ALU.mult, ALU.add, [kS, f"PL{hd}", kpsS], kS)
                    if c >= 31 and c < 63:
                        cp("act", Sb_t, S_t, [kS], kSb)
                    if own:
                        if c_tail:
                            c_tail.pop(0)()
                        slot = 8 + hd
                        ss = stat[0:64, slot:slot + 1]; kss = f"stat{slot}"
                        act(junk[0:64, 0:128], psO[0:64, 0:128], AF.Square, [kpsO], [kss, "junk"], accum=ss)
                        act(ss, ss, AF.Sqrt, [kss, "epsT"], kss, bias=epsT[0:64, 0:1], scale=1.0 / 128.0)

                        def tail(ss=ss, kss=kss, y_t=y_t, ky=ky, psO=psO, kpsO=kpsO, g_t=g_t, kg=kg, j=j):
                            P.op("dve", (lambda o_: (lambda e: e.reciprocal(out=o_, in_=o_)))(ss), reads=[kss], writes=[kss])
                            stt("dve", y_t[0:64, j, :], psO[0:64, 0:128], ss, g_t[0:64, j, :], ALU.mult, ALU.mult, [kpsO, kss, kg], ky)
                        c_tail.append(tail)
                while c_tail:
                    c_tail.pop(0)()
            if own:
                for hd in range(8):
                    y_t, ky = yst[hd][bi]
                    o0 = blk * 256 - HALF
                    st(s_ya[o0:o0 + 256, hd * 128:(hd + 1) * 128].rearrange("(j s) k -> s j k", s=64), y_t[0:64], ky, "SCR_ALL")
        dump_scr("s_ya", s_ya)
        P.barrier()
        flush_dumps()

        FA.reset(); BA.reset()
        kcmpT2, k_kcmp = BA.alloc([4, 256])
        VcAug, k_vca = BA.alloc([4, 2, 129])
        KEa, k_KEa = BA.alloc([T])
        KEb, k_KEb = BA.alloc([T])
        cmpb, k_cmpb = BA.alloc([2, HALF])
        ld(KEa[64:128], c_E, k_KEa + "E")
        ld(KEb[0:64], c_E, k_KEb + "E")
        ld(cmpb, c_cmpb.rearrange("c p t -> p c t"), k_cmpb)
        ovl_f, k_ovl = FA.alloc([2, 64])
        ld(ovl_f, c_ovl.rearrange("(c p) s -> p c s", p=128), k_ovl)
        P.op("pool", lambda e: e.memset(VcAug, 0.0), writes=[k_vca])
        P.op("pool", lambda e: e.memset(kcmpT2, 0.0), writes=[k_kcmp])
        for g in range(4):
            cp("dve", VcAug[:, g, :, 65:129], ovl_f, [k_ovl], k_vca)
            P.op("pool", (lambda o_: (lambda e: e.memset(o_, 1.0)))(VcAug[:, g, :, 64:65]), writes=[k_vca])
        selA_t, k_selA = FA.alloc([16, 64]); selB_t, k_selB = FA.alloc([16, 64])
        for q4 in range(2):
            ld(selA_t[:, q4 * 8:(q4 + 1) * 8, :], c_selA[q4 * 1024:(q4 + 1) * 1024, :].rearrange("(ti p) s -> p ti s", p=128), k_selA)
            ld(selB_t[:, q4 * 8:(q4 + 1) * 8, :], c_selB[q4 * 1024:(q4 + 1) * 1024, :].rearrange("(ti p) s -> p ti s", p=128), k_selB)
        kval_t, k_kval = FA.alloc([32])
        ld(kval_t, c_kval, k_kval)
        dmark_B = BA.off; dmark_F = FA.off

        w1st, k_w1st = FA.alloc([32, 256])
        w1b, k_w1b = BA.alloc([32, 256])
        w2st, k_w2st = FA.alloc([2, 64]); w2b, k_w2b = BA.alloc([2, 64])
        peT, k_peT = FA.alloc([32])
        kcTs = [BA.alloc([T]) for _ in range(2)]
        peTb, k_peTb = BA.alloc([32])
        cvec, k_cvec = FA.alloc([2])
        xss = [FA.alloc([256]) for _ in range(4)]
        geTs = [BA.alloc([2, 256]) for _ in range(2)]
        x2s = [FA.alloc([256]) for _ in range(4)]; inns = [FA.alloc([256]) for _ in range(4)]
        STREAM[0] = 1
        d0_tails = []
        for kv in range(2):
            w1_d = wk1 if kv == 0 else wv1
            w2_d = wk2 if kv == 0 else wv2
            pe_d = pe_k if kv == 0 else pe_v
            src_d = s_kc if kv == 0 else s_vc
            for q4 in range(2):
                ld(w1st[0:64, q4 * 16:(q4 + 1) * 16, :], w1_d[q4 * 1024:(q4 + 1) * 1024, :].rearrange("(l d) h -> d l h", d=64), k_w1st)
            cp("dve", w1b[0:64], w1st[0:64], [k_w1st], k_w1b)
            ld(w2st, w2_d.rearrange("(c p) d -> p c d", p=128), k_w2st)
            cp("dve", w2b, w2st, [k_w2st], k_w2b)
            ld(peT[0:64], pe_d.rearrange("l d -> d l"), k_peT, slow=True)
            cp("dve", peTb[0:64], peT[0:64], [k_peT], k_peTb)
            for hc in range(2):
                ps, kps = next_ps()
                for l in range(32):
                    mm(ps[:, 0:1], w1b[0:64, l, hc * 128:(hc + 1) * 128], peTb[0:64, l:l + 1], l == 0, l == 31, [k_w1b, k_peTb], kps)
                cp("dve", cvec[:, hc:hc + 1], ps[:, 0:1], [kps], k_cvec + f"_{hc}")
            for g in range(4):
                kcT, k_kcT = kcTs[g % 2]
                geT, k_geT = geTs[g % 2]
                ld(kcT[0:64], src_d[g * 64:(g + 1) * 64, :], k_kcT)
                for hc in range(2):
                    ps, kps = next_ps()
                    for l in range(32):
                        mm(ps[:, 0:255], w1b[0:64, l, hc * 128:(hc + 1) * 128], kcT[0:64, l:l + 16 * 254 + 1:16], l == 0, l == 31, [k_w1b, k_kcT], kps)
                    if hc == 0 and d0_tails:
                        d0_tails.pop(0)()
                    bi_ = (g % 2) * 2 + hc
                    xs, kxs = xss[bi_]; x2, k_x2 = x2s[bi_]; inn, k_inn = inns[bi_]
                    act(xs[:, 0:255], ps[:, 0:255], AF.Identity, [kps, k_cvec + f"_{hc}"], kxs, bias=cvec[:, hc:hc + 1])
                    tt("dve", x2[:, 0:255], xs[:, 0:255], xs[:, 0:255], ALU.mult, [kxs], k_x2)
                    tsc("dve", inn[:, 0:255], x2[:, 0:255], 0.044715, 1.0, ALU.mult, ALU.add, [k_x2], k_inn)
                    tt("dve", inn[:, 0:255], inn[:, 0:255], xs[:, 0:255], ALU.mult, [k_inn, kxs], k_inn)
                    act(inn[:, 0:255], inn[:, 0:255], AF.Sigmoid, [k_inn], k_inn, scale=1.5957691216057308)
                    tt("dve", geT[:, hc, 0:255], inn[:, 0:255], xs[:, 0:255], ALU.mult, [k_inn, kxs], k_geT + f"_{hc}")

                def tail(kv=kv, g=g, geT=geT, k_geT=k_geT):
                    if kv == 0:
                        ps, kps = next_ps()
                        for hc in range(2):
                            mm(ps[0:64, 0:255], w2b[:, hc, :], geT[:, hc, 0:255], hc == 0, hc == 1, [k_w2b, k_geT + f"_{hc}"], kps)
                        cp("dve", kcmpT2[0:64, g, 0:255], ps[0:64, 0:255], [kps], k_kcmp)
                        P.dma("sp", (lambda o_, i_: (lambda e: e.dma_start(out=o_, in_=i_)))(kcmpT2[64:128, g, :], kcmpT2[0:64, g, :]),
                              reads=[k_kcmp], writes=[k_kcmp + "hi"])
                    else:
                        for nc_ in range(2):
                            nn = 128 if nc_ == 0 else 127
                            ps, kps = next_ps()
                            for hc in range(2):
                                mm(ps[0:nn, 0:64], geT[:, hc, nc_ * 128:nc_ * 128 + nn], w2b[:, hc, :], hc == 0, hc == 1,
                                   [k_geT + f"_{hc}", k_w2b], kps)
                            cp("dve", VcAug[0:nn, g, nc_, 0:64], ps[0:nn, 0:64], [kps], k_vca)
                d0_tails.append(tail)
            while d0_tails:
                d0_tails.pop(0)()
        STREAM[0] = 0
        if "kcmp" in dbg:
            t32, k32 = FA.alloc([4, 256])
            cp("dve", t32, kcmpT2, [k_kcmp, k_kcmp + "hi"], k32)
            st(dbg["kcmp"], t32, k32, "dbg_kcmp", q="sp", is_output=True)
        if "vcmp" in dbg:
            t32b, k32b = FA.alloc([4, 2, 129])
            cp("dve", t32b, VcAug, [k_vca], k32b)
            st(dbg["vcmp"], t32b, k32b, "dbg_vcmp", q="sp", is_output=True)
        P.barrier()
        BA.off = dmark_B; FA.off = dmark_F
        BA.gen += 1; FA.gen += 1

        if stop_after == "D0":
            P.emit(nc)
            return nc
        q2, k_q2 = BA.alloc([2, HALF])
        QB = [BA.alloc([HALF]) for _ in range(4)]
        kwT2, k_kwT = BA.alloc([T])
        VsAug, k_vsa = BA.alloc([32, 65]); VwAug, k_vwa = BA.alloc([32, 65])
        pTs = [BA.alloc([512]) for _ in range(4)]
        ybb, k_ybb = BA.alloc([16, 256])
        biasTok, k_btok = BA.alloc([128])
        yb, k_yb = FA.alloc([16, 256])
        imp, k_imp = FA.alloc([16, 64])
        tk_a, k_tka = FA.alloc([64]); tk_b, k_tkb = FA.alloc([64]); tk_c, k_tkc = FA.alloc([64])
        m8a, k_m8a = FA.alloc([8]); m8b, k_m8b = FA.alloc([8])
        ptc = [0]

        def epilogue(psv, kpsv, ti, hloc, h, branch, first):
            slot = 16 + (ptc[0] % 8) * 2
            ptc[0] += 1
            rd = stat[:, slot:slot + 1]; krd = f"stat{slot}"
            gr = stat[:, slot + 1:slot + 2]; kgr = f"stat{slot + 1}"
            tsc("dve", rd, psv[:, 64:65], 1e-30, None, ALU.add, None, [kpsv], krd)
            P.op("dve", (lambda o_: (lambda e: e.reciprocal(out=o_, in_=o_)))(rd), reads=[krd], writes=[krd])
            tt("dve", gr, rd, gates[:, ti, branch * 16 + h:branch * 16 + h + 1], ALU.mult, [krd, f"gates{ti}"], kgr)
            dst = yb[:, ti, hloc * 64:(hloc + 1) * 64]
            kd_ = f"yb{ti}_{hloc}"
            if first:
                tsc("dve", dst, psv[:, 0:64], gr, None, ALU.mult, None, [kpsv, kgr], kd_)
            else:
                stt("dve", dst, psv[:, 0:64], gr, dst, ALU.mult, ALU.add, [kpsv, kgr, kd_], kd_)
            return rd, krd

        for g in range(4):
            for pr in range(2):
                ld(q2[:, pr, :], s_q[g * 256 + pr * 128:g * 256 + (pr + 1) * 128, :], k_q2)
                for r2_ in range(2):
                    hl_ = pr * 2 + r2_
                    ld(QB[hl_][0][64 * r2_:64 * r2_ + 64, :], s_qr[(4 * g + hl_) * 64:(4 * g + hl_ + 1) * 64, :], QB[hl_][1] + "q")
            for hf_ in range(2):
                ld((KEa if hf_ == 0 else KEb)[hf_ * 64:(hf_ + 1) * 64, :], s_ks[g * 64:(g + 1) * 64, :], (k_KEa if hf_ == 0 else k_KEb) + "k")
                ld(kwT2[hf_ * 64:(hf_ + 1) * 64, :], s_kw[g * 64:(g + 1) * 64, :], k_kwT)
            for q4 in range(4):
                ld(VsAug[:, q4 * 8:(q4 + 1) * 8, 0:64], s_vs[q4 * 1024:(q4 + 1) * 1024, g * 64:(g + 1) * 64].rearrange("(kt p) d -> p kt d", p=128), k_vsa)
                ld(VwAug[:, q4 * 8:(q4 + 1) * 8, 0:64], s_vw[q4 * 1024:(q4 + 1) * 1024, g * 64:(g + 1) * 64].rearrange("(kt p) d -> p kt d", p=128), k_vwa)
            cp("dve", VsAug[:, :, 64], kval_t, [k_kval], k_vsa)
            cp("dve", VwAug[:, :, 64], kval_t, [k_kval], k_vwa)
            if stop_after == "D1L":
                P.emit(nc)
                return nc
            units = [(hloc, tt_, nc_) for hloc in range(4) for tt_ in range(4) for nc_ in range(2)]

            def c_scores(ui):
                hloc, tt_, nc_ = units[ui]
                pr, r2 = hloc // 2, hloc % 2
                pb = 64 * r2
                tsl = slice(tt_ * 512, (tt_ + 1) * 512)
                ps, kps = PSF[ui % 2], f"psf{ui % 2}"
                mm(ps[:, :], kcmpT2[pb:pb + 64, g, nc_ * 128:(nc_ + 1) * 128], q2[pb:pb + 64, pr, tsl], True, False,
                   [k_kcmp, k_kcmp + "hi", k_q2], kps)
                mm(ps[:, :], ident[:], cmpb[:, nc_, tsl], False, True, ["ident", k_cmpb], kps)

            def c_rest(ui):
                hloc, tt_, nc_ = units[ui]
                h = 4 * g + hloc
                ps, kps = PSF[ui % 2], f"psf{ui % 2}"
                cb = 4 if ((ui // 2) % 2 == 0) else 2
                psC = [PSF[cb], PSF[cb + 1]]
                pT, kpT = pTs[ui % 2]
                act(pT, ps[:, :], AF.Exp, [kps], [f"{kpT}_{x}" for x in range(4)])
                for ts in range(4):
                    bank = psC[ts // 2]
                    mm(bank[:, (ts % 2) * 129:(ts % 2) * 129 + 129], pT[:, ts * 128:(ts + 1) * 128], VcAug[:, g, nc_, :],
                       nc_ == 0 and ts % 2 == 0, nc_ == 1, [f"{kpT}_{ts}", k_vca], f"psf{cb + ts // 2}", skip=True)
                if nc_ == 1:
                    for ts in range(4):
                        ti = tt_ * 4 + ts
                        kb = f"psf{cb + ts // 2}"
                        psv = psC[ts // 2][:, (ts % 2) * 129:(ts % 2) * 129 + 129]
                        rd, krd = epilogue(psv, kb, ti, hloc, h, 0, True)
                        ki = f"imp{ti}"
                        if hloc == 0:
                            tsc("dve", imp[:, ti, :], psv[:, 65:129], rd, None, ALU.mult, None, [kb, krd], ki)
                        else:
                            stt("dve", imp[:, ti, :], psv[:, 65:129], rd, imp[:, ti, :], ALU.mult, ALU.add, [kb, krd, ki], ki)

            c_scores(0)
            for ui in range(len(units)):
                if ui + 1 < len(units):
                    c_scores(ui + 1)
                c_rest(ui)
            if stop_after == "D1a":
                P.emit(nc)
                return nc
            def topk_gen():
                for ti in range(16):
                    ki = f"imp{ti}"
                    tt("dve", tk_a, imp[:, ti, :], selA_t[:, ti, :], ALU.mult, [ki, k_selA], k_tka)
                    tt("dve", tk_a, tk_a, selB_t[:, ti, :], ALU.add, [k_tka, k_selB], k_tka)
                    P.op("dve", lambda e: e.max(out=m8a, in_=tk_a), reads=[k_tka], writes=[k_m8a])
                    P.op("dve", lambda e: e.match_replace(out=tk_b, in_to_replace=m8a, in_values=tk_a, imm_value=-1e30), reads=[k_tka, k_m8a], writes=[k_tkb])
                    P.op("dve", lambda e: e.max(out=m8b, in_=tk_b), reads=[k_tkb], writes=[k_m8b])
                    P.op("dve", lambda e: e.match_replace(out=tk_c, in_to_replace=m8b, in_values=tk_b, imm_value=-1e30), reads=[k_tkb, k_m8b], writes=[k_tkc])
                    tt("dve", tk_c, tk_c, tk_a, ALU.not_equal, [k_tkc, k_tka], k_tkc)
                    tsc("dve", biasTok[:, 0:64], tk_c, -NEGB, NEGB, ALU.mult, ALU.add, [k_tkc], k_btok)
                    tsc("dve", biasTok[:, 64:128], tk_c, -NEGB, NEGB, ALU.mult, ALU.add, [k_tkc], k_btok)
                    yield
                    pbk = PSB[ti % 2]
                    tr(pbk[:, 0:128], biasTok[:, 0:128], [k_btok], f"psb{ti % 2}")
                    for hl_ in range(4):
                        bo = 64 if hl_ % 2 == 0 else 0
                        cp("act" if hl_ % 2 else "dve", QB[hl_][0][bo:bo + 64, ti * 128:(ti + 1) * 128], pbk[bo:bo + 64, 0:128],
                           [f"psb{ti % 2}"], QB[hl_][1] + f"b{ti}")
            def attn_gen(branches):
                SB = [(PSF[0], "psf0"), (PSF[1], "psf1"), (PSF[4], "psf4"), (PSF[5], "psf5")]
                LOOK = 3
                tiles = []
                gi_ = 0
                for hloc in range(4):
                    for branch in branches:
                        for tt_ in range(4):
                            kt0 = 16 + 4 * tt_
                            kts = list(range(0, kt0 + 4)) if branch == 1 else list(range(kt0 - 4, kt0 + 4))
                            grp = dict(hloc=hloc, branch=branch, tt_=tt_, kt0=kt0, started=[False] * 4, obi=gi_ % 2)
                            gi_ += 1
                            for ii, kt in enumerate(kts):
                                kk = kt - kt0
                                ts_lo = max(0, kk)
                                ts_hi = 3 if branch == 1 else min(3, kk + 4)
                                tiles.append(dict(g=grp, kt=kt, kk=kk, ts_lo=ts_lo, ts_hi=ts_hi, last=(ii == len(kts) - 1)))

                def scores(j):
                    t_ = tiles[j]; gr = t_["g"]
                    hloc, branch, tt_ = gr["hloc"], gr["branch"], gr["tt_"]
                    r2 = hloc % 2
                    pb = 64 * r2
                    qb_t, kqb = QB[hloc]
                    kt = t_["kt"]
                    ksl = slice(kt * 128, (kt + 1) * 128)
                    c0, c1 = t_["ts_lo"] * 128, (t_["ts_hi"] + 1) * 128
                    tsl = slice(tt_ * 512 + c0, tt_ * 512 + c1)
                    ps, kps = SB[j % 4]
                    if branch == 1:
                        KE, k_KE = (KEa, k_KEa) if r2 == 0 else (KEb, k_KEb)
                        mm(ps[:, c0:c1], KE[:, ksl], qb_t[:, tsl], True, True,
                           [k_KE + "E", k_KE + "k", kqb + "q"] + [kqb + f"b{tt_ * 4 + x}" for x in range(4)], kps)
                    else:
                        mm(ps[:, c0:c1], kwT2[pb:pb + 64, ksl], qb_t[pb:pb + 64, tsl], True, True, [k_kwT, kqb + "q"], kps)

                def rest(j):
                    t_ = tiles[j]; gr = t_["g"]
                    hloc, branch, tt_, kt0 = gr["hloc"], gr["branch"], gr["tt_"], gr["kt0"]
                    h = 4 * g + hloc
                    VA, k_VA = (VsAug, k_vsa) if branch == 1 else (VwAug, k_vwa)
                    psO, kpsO = PSF[2 + gr["obi"]], f"psf{2 + gr['obi']}"
                    started = gr["started"]
                    kt, kk = t_["kt"], t_["kk"]
                    c0, c1 = t_["ts_lo"] * 128, (t_["ts_hi"] + 1) * 128
                    ps, kps = SB[j % 4]
                    pT, kpT = pTs[j % 4]
                    act(pT[:, c0:c1], ps[:, c0:c1], AF.Exp, [kps], [f"{kpT}_{x}" for x in range(t_["ts_lo"], t_["ts_hi"] + 1)])
                    for ts in range(t_["ts_lo"], t_["ts_hi"] + 1):
                        Dd = ts - kk
                        sub = pT[:, ts * 128:(ts + 1) * 128]
                        ksub = f"{kpT}_{ts}"
                        if Dd == 0:
                            tt(MASK_ENG, sub, sub, triL[:], ALU.mult, [ksub, "triL"], ksub)
                        elif branch == 2 and Dd == 4:
                            tt(MASK_ENG, sub, sub, triU[:], ALU.mult, [ksub, "triU"], ksub)
                        mm(psO[:, ts * 65:(ts + 1) * 65], sub, VA[:, kt, :], not any(started), kt == kt0 + ts, [ksub, k_VA], kpsO, skip=True)
                        started[ts] = True
                    if t_["last"]:
                        for ts in range(4):
                            epilogue(psO[:, ts * 65:(ts + 1) * 65], kpsO, tt_ * 4 + ts, hloc, h, branch, False)
                        return True
                    return False

                n = len(tiles)
                for j in range(min(LOOK, n)):
                    scores(j)
                for j in range(n):
                    if j + LOOK < n:
                        scores(j + LOOK)
                    if rest(j):
                        yield

            def rr(gens):
                gens = list(gens)
                while gens:
                    for g_ in list(gens):
                        try:
                            next(g_)
                        except StopIteration:
                            gens.remove(g_)

            rr([attn_gen((2,)), topk_gen()])
            rr([attn_gen((1,))])
            if stop_after == "D1c":
                P.emit(nc)
                return nc
            cp("act", ybb, yb, [f"yb{ti}_{hl}" for ti in range(16) for hl in range(4)], k_ybb)
            for q4 in range(2):
                st(s_yb[q4 * 1024:(q4 + 1) * 1024, g * 256:(g + 1) * 256].rearrange("(ti p) c -> p ti c", p=128), ybb[:, q4 * 8:(q4 + 1) * 8, :], k_ybb, "SCR_ALL")
        dump_scr("s_yb", s_yb)
        P.barrier()
        flush_dumps()

        if stop_after == "D":
            P.emit(nc)
            return nc
        FA.reset(); BA.reset()

        def load_weight_bf16(dst, kdst, w_dram, nrows_k, ncols, stg):
            wv = w_dram.rearrange("(k p) n -> p k n", p=128)
            i = 0
            for k0 in range(0, nrows_k, 8):
                kn = min(8, nrows_k - k0)
                for c0 in range(0, ncols, 512):
                    cn = min(512, ncols - c0)
                    st_, kst = stg[i % len(stg)]
                    i += 1
                    ld(st_[:, 0:kn, 0:cn], wv[:, k0:k0 + kn, c0:c0 + cn], kst)
                    cp("act" if i % 2 else "dve", dst[:, k0:k0 + kn, c0:c0 + cn], st_[:, 0:kn, 0:cn], [kst], kdst)

        stg = [FA.alloc([8, 512]) for _ in range(2)]
        Wa_b, k_Wa = BA.alloc([8, D]); Wb_b, k_Wb = BA.alloc([8, D]); Wo_b, k_Wo = BA.alloc([8, D])
        load_weight_bf16(Wa_b, k_Wa, w_a, 8, D, stg)
        load_weight_bf16(Wb_b, k_Wb, w_b, 8, D, stg)
        load_weight_bf16(Wo_b, k_Wo, w_o, 8, D, stg)
        npost_bc, k_npost = FA.alloc([D]); nmpre_bc, k_nmpre = FA.alloc([D])
        ld(npost_bc, n_post.to_broadcast([128, D]), k_npost)
        ld(nmpre_bc, n_mpre.to_broadcast([128, D]), k_nmpre)
        yaT, k_yaT = BA.alloc([8, 512]); ybT, k_ybT = BA.alloc([8, 512]); mT, k_mT = BA.alloc([8, 512])
        ytok = [BA.alloc([D]) for _ in range(2)]
        sgt_ = [BA.alloc([512]) for _ in range(4)]
        vn_b = [BA.alloc([D]) for _ in range(2)]
        vTs = [BA.alloc([8, 128]) for _ in range(2)]
        m1s = [FA.alloc([512]) for _ in range(2)]
        m2s = [FA.alloc([512]) for _ in range(2)]
        xts2 = [FA.alloc([D]) for _ in range(2)]
        h1s = [FA.alloc([D]) for _ in range(2)]
        trc = [0]

        def transpose_tile(src_tok, ksrc, dstT, kdst, col0):
            i = trc[0] % 2
            trc[0] += 1
            pb = PSB[i]
            for k in range(8):
                tr(pb[:, k * 128:(k + 1) * 128], src_tok[:, k * 128:(k + 1) * 128], [ksrc], f"psb{i}")
            cp("act" if i else "dve", dstT[:, :, col0:col0 + 128], pb[:].rearrange("p (k t) -> p k t", k=8), [f"psb{i}"], kdst)

        def rms2(ps_a, kpa, ps_b, kpb, slot):
            s0 = stat[:, slot:slot + 1]; s1 = stat[:, slot + 1:slot + 2]
            k0, k1 = f"stat{slot}", f"stat{slot + 1}"
            act(junk[:, 0:512], ps_a, AF.Square, [kpa], [k0, "junk"], accum=s0)
            act(junk[:, 512:1024], ps_b, AF.Square, [kpb], [k1, "junk2"], accum=s1)
            tt("dve", s0, s0, s1, ALU.add, [k0, k1], k0)
            act(s0, s0, AF.Sqrt, [k0, "epsT"], k0, bias=epsT[:, 0:1], scale=1.0 / 1024.0)
            P.op("dve", (lambda o_: (lambda e: e.reciprocal(out=o_, in_=o_)))(s0), reads=[k0], writes=[k0])
            return s0, k0

        mTs = [(mT, k_mT), FA.alloc_bf16([8, 512])]

        def e_front(tb):
            mT_, k_mT_ = mTs[tb % 2]
            for which, (srcd, dstT, kdT) in enumerate(((s_ya, yaT, k_yaT), (s_yb, ybT, k_ybT))):
                for ts in range(4):
                    yt, kyt = ytok[(which * 4 + ts) % 2]
                    r0 = tb * 512 + ts * 128
                    ld(yt, srcd[r0:r0 + 128, :], kyt)
                    transpose_tile(yt, kyt, dstT, kdT, ts * 128)
                yield
            for ct in range(8):
                pa, kpa = PSF[0 + (ct % 2) * 2], f"psf{0 + (ct % 2) * 2}"
                pbb, kpb = PSF[1 + (ct % 2) * 2], f"psf{1 + (ct % 2) * 2}"
                for k in range(8):
                    mm(pa[:, :], Wa_b[:, k, ct * 128:(ct + 1) * 128], yaT[:, k, :], k == 0, k == 7, [k_Wa, k_yaT], kpa)
                for k in range(8):
                    mm(pbb[:, :], Wb_b[:, k, ct * 128:(ct + 1) * 128], ybT[:, k, :], k == 0, k == 7, [k_Wb, k_ybT], kpb)
                ga_t, kga = sgt_[(ct % 2) * 2]; gb_t, kgb2 = sgt_[(ct % 2) * 2 + 1]
                ld(ga_t, s_ga[ct * 128:(ct + 1) * 128, tb * 512:(tb + 1) * 512], kga)
                ld(gb_t, s_gb[ct * 128:(ct + 1) * 128, tb * 512:(tb + 1) * 512], kgb2)
                m1, km1 = m1s[ct % 2]; m2, km2 = m2s[ct % 2]
                tt("dve", m1, pa[:, :], ga_t, ALU.mult, [kpa, kga], km1)
                tt("dve", m2, pbb[:, :], gb_t, ALU.mult, [kpb, kgb2], km2)
                tt("pool", mT_[:, ct, :], m1, m2, ALU.add, [km1, km2], k_mT_ + f"_{ct}")
                if ct % 2 == 1:
                    yield

        def e_back(tb):
            mT_, k_mT_ = mTs[tb % 2]
            for ts in range(4):
                ti = tb * 4 + ts
                za, kza = PSF[4], "psf4"
                zb, kzb = PSF[5], "psf5"
                for nh, (zz, kzz) in enumerate(((za, kza), (zb, kzb))):
                    for k in range(8):
                        mm(zz[:, :], mT_[:, k, ts * 128:(ts + 1) * 128], Wo_b[:, k, nh * 512:(nh + 1) * 512], k == 0, k == 7,
                           [k_mT_ + f"_{k}", k_Wo], kzz)
                if e_tails:
                    e_tails.pop(0)()
                rs, krs = rms2(za[:, :], kza, zb[:, :], kzb, 40 + (ti % 2) * 4)
                xt, kx = xts2[ti % 2]; h1, kh1 = h1s[ti % 2]
                ld(xt, xo[ti * 128:(ti + 1) * 128, :], kx)
                stt("dve", h1[:, 0:512], za[:, :], rs, npost_bc[:, 0:512], ALU.mult, ALU.mult, [kza, krs, k_npost], kh1)
                stt("dve", h1[:, 512:1024], zb[:, :], rs, npost_bc[:, 512:1024], ALU.mult, ALU.mult, [kzb, krs, k_npost], kh1)
                tt("pool", h1, h1, xt, ALU.add, [kh1, kx], kh1)
                st(s_h1[ti * 128:(ti + 1) * 128, :], h1, kh1, "SCR_ALL")
                slot = 42 + (ti % 2) * 4
                ss = stat[:, slot:slot + 1]; kss = f"stat{slot}"
                act(junk[:, :], h1, AF.Square, [kh1], [kss, "junk", "junk2"], accum=ss)
                act(ss, ss, AF.Sqrt, [kss, "epsT"], kss, bias=epsT[:, 0:1], scale=1.0 / 1024.0)
                P.op("dve", (lambda o_: (lambda e: e.reciprocal(out=o_, in_=o_)))(ss), reads=[kss], writes=[kss])
                vn, kvn = vn_b[ti % 2]
                stt("dve", vn, h1, ss, nmpre_bc, ALU.mult, ALU.mult, [kh1, kss, k_nmpre], kvn)
                def tail(ti=ti, vn=vn, kvn=kvn):
                    vT_t, kvT = vTs[ti % 2]
                    transpose_tile(vn, kvn, vT_t, kvT, 0)
                    st(s_vT[:, ti * 128:(ti + 1) * 128].rearrange("(k p) t -> p k t", p=128), vT_t, kvT, "SCR_ALL")
                e_tails.append(tail)
                yield

        def drain(gens):
            gens = list(gens)
            while gens:
                for g_ in list(gens):
                    try:
                        next(g_)
                    except StopIteration:
                        gens.remove(g_)

        e_tails = []
        drain([e_front(0)])
        for tb in range(4):
            gs = [e_back(tb)]
            if tb + 1 < 4:
                gs.insert(0, e_front(tb + 1))
            drain(gs)
        while e_tails:
            e_tails.pop(0)()
        dump_scr("s_h1", s_h1)
        P.barrier()
        flush_dumps()

        FA.reset(); BA.reset()
        wd_b, k_wd = BA.alloc([32, D])
        wus = [FA.alloc([8, 256]) for _ in range(2)]
        wub = [BA.alloc([8, 256]) for _ in range(2)]
        vT_blk, k_vTb = BA.alloc([8, 512])
        actT, k_actT = FA.alloc_bf16([32, 512])
        nmpost_bc, k_nmpost = FA.alloc([D])
        ld(nmpost_bc, n_mpost.to_broadcast([128, D]), k_nmpost)
        rts = [FA.alloc([512]) for _ in range(2)]
        h1s = [FA.alloc([D]) for _ in range(2)]
        h2s = [FA.alloc([D]) for _ in range(2)]
        w_up_v = w_up.rearrange("(k p) n -> p k n", p=128)
        for tb in range(4):
            ld(vT_blk, s_vT[:, tb * 512:(tb + 1) * 512].rearrange("(k p) t -> p k t", p=128), k_vTb)
            for cg in range(16):
                ws_, kws_ = wus[cg % 2]; wb_, kwb_ = wub[cg % 2]
                ld(ws_, w_up_v[:, :, cg * 256:(cg + 1) * 256], kws_)
                cp("act" if cg % 2 else "dve", wb_, ws_, [kws_], kwb_)
                for c2 in range(2):
                    fft = cg * 2 + c2
                    ps, kps = PSF[fft % 2], f"psf{fft % 2}"
                    for k in range(8):
                        mm(ps[:, :], wb_[:, k, c2 * 128:(c2 + 1) * 128], vT_blk[:, k, :], k == 0, k == 7, [kwb_, k_vTb], kps)
                    rt, krt = rts[fft % 2]
                    act(rt, ps[:, :], AF.Relu, [kps], krt)
                    tt("dve", actT[:, fft, :], rt, ps[:, :], ALU.mult, [krt, kps], k_actT + f"_{fft}")
            if tb == 0:
                wdv = w_down.rearrange("(k p) n -> p k n", p=128)
                ci = 0
                for k0 in range(0, 32, 8):
                    for c0 in range(0, D, 256):
                        ws_, kws_ = wus[ci % 2]
                        ld(ws_, wdv[:, k0:k0 + 8, c0:c0 + 256], kws_)
                        cp("act" if ci % 2 else "dve", wd_b[:, k0:k0 + 8, c0:c0 + 256], ws_, [kws_], k_wd)
                        ci += 1
            for ts in range(4):
                ti = tb * 4 + ts
                za, kza = PSF[2 + (ti % 2) * 2], f"psf{2 + (ti % 2) * 2}"
                zb, kzb = PSF[3 + (ti % 2) * 2], f"psf{3 + (ti % 2) * 2}"
                for nh, (zz, kzz) in enumerate(((za, kza), (zb, kzb))):
                    for fft in range(32):
                        mm(zz[:, :], actT[:, fft, ts * 128:(ts + 1) * 128], wd_b[:, fft, nh * 512:(nh + 1) * 512], fft == 0, fft == 31,
                           [k_actT + f"_{fft}", k_wd], kzz)
                rs, krs = rms2(za[:, :], kza, zb[:, :], kzb, 48 + (ti % 2) * 4)
                h1, kh1 = h1s[ti % 2]; h2, kh2 = h2s[ti % 2]
                ld(h1, s_h1[ti * 128:(ti + 1) * 128, :], kh1)
                stt("dve", h2[:, 0:512], za[:, :], rs, nmpost_bc[:, 0:512], ALU.mult, ALU.mult, [kza, krs, k_nmpost], kh2)
                stt("dve", h2[:, 512:1024], zb[:, :], rs, nmpost_bc[:, 512:1024], ALU.mult, ALU.mult, [kzb, krs, k_nmpost], kh2)
                tt("pool", h2, h2, h1, ALU.add, [kh2, kh1], kh2)
                st(s_h2[ti * 128:(ti + 1) * 128, :], h2, kh2, "SCR_ALL")
        dump_scr("s_h2", s_h2)
        P.barrier()
        flush_dumps()

        FA.reset(); BA.reset()
        stg = [FA.alloc([8, 512]) for _ in range(2)]
        Wpg_b, k_Wpg = BA.alloc([8, D]); Wple_b, k_Wple = BA.alloc([2, D])
        load_weight_bf16(Wpg_b, k_Wpg, w_pg, 8, D, stg)
        load_weight_bf16(Wple_b, k_Wple, w_ple, 2, D, stg)
        nple_bc, k_nple = FA.alloc([D])
        ld(nple_bc, n_ple.to_broadcast([128, D]), k_nple)
        NB3 = 3
        h2s = [FA.alloc([D]) for _ in range(NB3)]
        h2bs = [BA.alloc([D]) for _ in range(NB3)]
        h2Ts = [BA.alloc([8, 128]) for _ in range(NB3)]
        pfs = [FA.alloc([256]) for _ in range(NB3)]
        pbs = [BA.alloc([256]) for _ in range(NB3)]
        pTs2 = [BA.alloc([2, 128]) for _ in range(NB3)]
        sgs = [FA.alloc([D]) for _ in range(NB3)]
        es_ = [FA.alloc([D]) for _ in range(NB3)]
        outs = [FA.alloc([D]) for _ in range(NB3)]

        def g_front(ti):
            b3 = ti % NB3
            h2, kh2 = h2s[b3]; h2b, kh2b = h2bs[b3]; h2T, kh2T = h2Ts[b3]
            ld(h2, s_h2[ti * 128:(ti + 1) * 128, :], kh2)
            cp("pool", h2b, h2, [kh2], kh2b)
            transpose_tile(h2b, kh2b, h2T, kh2T, 0)
            pf, kpf = pfs[b3]; pb_, kpb_ = pbs[b3]; pT_, kpT_ = pTs2[b3]
            ld(pf, po[ti * 128:(ti + 1) * 128, :], kpf)
            cp("pool", pb_, pf, [kpf], kpb_)
            i2 = trc[0] % 2
            trc[0] += 1
            pbk = PSB[i2]
            for k in range(2):
                tr(pbk[:, k * 128:(k + 1) * 128], pb_[:, k * 128:(k + 1) * 128], [kpb_], f"psb{i2}")
            cp("dve", pT_, pbk[:, 0:256].rearrange("p (k t) -> p k t", k=2), [f"psb{i2}"], kpT_)

        def g_banks(ti, nh):
            u = (2 * ti + nh) % 3
            return (PSF[2 * u], f"psf{2 * u}"), (PSF[2 * u + 1], f"psf{2 * u + 1}")

        def g_ufront(ti, nh):
            b3 = ti % NB3
            h2T, kh2T = h2Ts[b3]; pT_, kpT_ = pTs2[b3]
            (gp, kgp), (ep, kep) = g_banks(ti, nh)
            for k in range(8):
                mm(gp[:, :], h2T[:, k, :], Wpg_b[:, k, nh * 512:(nh + 1) * 512], k == 0, k == 7, [kh2T, k_Wpg], kgp)
            for k in range(2):
                mm(ep[:, :], pT_[:, k, :], Wple_b[:, k, nh * 512:(nh + 1) * 512], k == 0, k == 1, [kpT_, k_Wple], kep)

        def g_uback(ti, nh):
            b3 = ti % NB3
            sg, ksg = sgs[b3]; ee, kee = es_[b3]
            (gp, kgp), (ep, kep) = g_banks(ti, nh)
            act(sg[:, nh * 512:(nh + 1) * 512], gp[:, :], AF.Sigmoid, [kgp], ksg + f"_{nh}")
            tt("dve", ee[:, nh * 512:(nh + 1) * 512], sg[:, nh * 512:(nh + 1) * 512], ep[:, :], ALU.mult, [ksg + f"_{nh}", kep], kee + f"_{nh}")

        def g_tback(ti):
            b3 = ti % NB3
            h2, kh2 = h2s[b3]; ee, kee = es_[b3]; oo, koo = outs[b3]
            slot = 56 + b3 * 2
            ss = stat[:, slot:slot + 1]; kss = f"stat{slot}"
            act(junk[:, :], ee, AF.Square, [kee + "_0", kee + "_1"], [kss, "junk", "junk2"], accum=ss)
            act(ss, ss, AF.Sqrt, [kss, "epsT"], kss, bias=epsT[:, 0:1], scale=1.0 / 1024.0)
            P.op("dve", (lambda o_: (lambda e: e.reciprocal(out=o_, in_=o_)))(ss), reads=[kss], writes=[kss])
            stt("dve", oo, ee, ss, nple_bc, ALU.mult, ALU.mult, [kee + "_0", kee + "_1", kss, k_nple], koo)
            tt("pool", oo, oo, h2, ALU.add, [koo, kh2], koo)
            st(out[ti * 128:(ti + 1) * 128, :], oo, koo, "SCR_ALL", q="sp", is_output=True)

        NT = 16
        g_front(0); g_ufront(0, 0); g_ufront(0, 1); g_front(1)
        for ti in range(NT):
            if ti + 1 < NT:
                g_ufront(ti + 1, 0)
            g_uback(ti, 0)
            if ti + 1 < NT:
                g_ufront(ti + 1, 1)
            g_uback(ti, 1)
            if ti + 2 < NT:
                g_front(ti + 2)
            g_tback(ti)
        P.emit(nc)
    return nc


def _consts(j):
    c = {}
    c["c_ident"] = np.eye(128, dtype=np.float32).astype(NPBF)
    pm = np.zeros((128, 128), np.float32)
    for h in range(2):
        for d in range(64):
            partner = d + 8 if d < 8 else (d - 8 if d < 16 else d)
            pm[h * 64 + partner, h * 64 + d] = 1.0
    c["c_perm"] = pm.astype(NPBF)
    half = 8
    inv = 500000.0 ** (-np.arange(half) * 2.0 / 16)
    ang = np.arange(T, dtype=np.float64)[None, :] * inv[:, None]
    cos = np.ones((64, T), np.float32); sin = np.zeros((64, T), np.float32)
    cos[0:8] = np.cos(ang); cos[8:16] = np.cos(ang)
    sin[0:8] = -np.sin(ang); sin[8:16] = np.sin(ang)
    c["c_cos"] = np.concatenate([cos, cos], 0)
    c["c_sin"] = np.concatenate([sin, sin], 0)
    E = np.zeros((64, T), np.float32)
    E[np.arange(T) // 64, np.arange(T)] = 1.0
    c["c_E"] = E.astype(NPBF)
    kk = np.arange(128)[:, None]; tt_ = np.arange(128)[None, :]
    c["c_triL"] = (tt_ >= kk).astype(np.float32).astype(NPBF)
    c["c_triU"] = (tt_ < kk).astype(np.float32).astype(NPBF)
    n = np.arange(256)
    sel_start = np.arange(64) * 64
    ovl = ((n[:, None] * 16 < sel_start[None, :] + 64) & (n[:, None] * 16 + 31 >= sel_start[None, :])).astype(np.float32)
    ovl[255] = 0
    c["c_ovl"] = ovl
    t = HALF + np.arange(HALF)
    nmin = 0 if j == 1 else 128
    valid = (n[:, None] * 16 + 31 <= t[None, :]) & (n[:, None] >= nmin) & (n[:, None] < 255)
    c["c_cmpb"] = np.where(valid, 0.0, NEGB).astype(np.float32).reshape(2, 128, HALF).astype(NPBF)
    bmin = 0 if j == 1 else 32
    b = np.arange(64)[None, :]
    cur = (t // 64)[:, None]
    val = (b >= bmin) & (b <= cur)
    forced = val & ((b == bmin) | (b == cur) | (b == cur - 1))
    c["c_selA"] = (val & ~forced).astype(np.float32)
    c["c_selB"] = np.where(forced, 1e9 + 1000.0 * b, np.where(val, 0.0, -1e9)).astype(np.float32)
    kv = np.ones(T, np.float32)
    if j == 0:
        kv[:HALF] = 0
    c["c_kval"] = np.ascontiguousarray(kv.reshape(32, 128).T)
    return c


def make_in_maps(inputs):
    x = np.asarray(inputs["x"], np.float32)
    p = np.asarray(inputs["p"], np.float32)
    shared = {
        "w_in": np.ascontiguousarray(inputs["w_in"][0]),
        "w_a": np.ascontiguousarray(inputs["w_branch_a"][0]), "w_b": np.ascontiguousarray(inputs["w_branch_b"][0]),
        "w_o": np.ascontiguousarray(inputs["w_out"][0]),
        "w_up": np.ascontiguousarray(inputs["w_up"][0]), "w_down": np.ascontiguousarray(inputs["w_down"][0]),
        "w_ple": np.ascontiguousarray(inputs["w_ple"][0]), "w_pg": np.ascontiguousarray(inputs["w_ple_gate"][0]),
        "n_pre": np.ascontiguousarray(inputs["norm_pre_mix"]), "n_post": np.ascontiguousarray(inputs["norm_post_mix"]),
        "n_mpre": np.ascontiguousarray(inputs["norm_pre_mlp"]), "n_mpost": np.ascontiguousarray(inputs["norm_post_mlp"]),
        "n_ple": np.ascontiguousarray(inputs["norm_ple"]),
        "lbl": np.ascontiguousarray(inputs["hg_lb_logits"]), "gnw": np.ascontiguousarray(inputs["hg_gnorm"]),
        "pe_k": np.ascontiguousarray(inputs["cmp_pe_k"][0]), "pe_v": np.ascontiguousarray(inputs["cmp_pe_v"][0]),
        "wk1": np.ascontiguousarray(inputs["cmp_wk1"][0]), "wk2": np.ascontiguousarray(inputs["cmp_wk2"][0]),
        "wv1": np.ascontiguousarray(inputs["cmp_wv1"][0]), "wv2": np.ascontiguousarray(inputs["cmp_wv2"][0]),
    }
    shared = {k: np.asarray(v, np.float32) for k, v in shared.items()}
    cj = [_consts(0), _consts(1)]
    maps = []
    for c in range(8):
        b, j = c // 2, c % 2
        m = dict(shared)
        m.update(cj[j])
        m["xo"] = np.ascontiguousarray(x[b, j * HALF:(j + 1) * HALF])
        m["xc"] = np.ascontiguousarray(x[b, 0:HALF]) if j == 1 else np.zeros((HALF, D), np.float32)
        m["po"] = np.ascontiguousarray(p[0, b, j * HALF:(j + 1) * HALF])
        maps.append(m)
    return maps


def kernel(**inputs):
    nc = build()
    maps = make_in_maps(inputs)
    res = run_bass_kernel_spmd(nc, maps, core_ids=list(range(8)))
    outp = np.zeros((4, T, D), np.float32)
    for c in range(8):
        b, j = c // 2, c % 2
        outp[b, j * HALF:(j + 1) * HALF] = res.results[c]["out"]
    return outp
```

```python
import numpy as np
import ml_dtypes
from contextlib import ExitStack
import concourse.bass as bass
import concourse.mybir as mybir
from concourse.bass_utils import run_bass_kernel_spmd

F32 = mybir.dt.float32
BF16 = mybir.dt.bfloat16
AF = mybir.ActivationFunctionType
ALU = mybir.AluOpType
NPBF = ml_dtypes.bfloat16

NDMASEM = 8
import os
RELAX_OK = os.environ.get("KDBG_RELAX", "0") == "1"
QUEUES = ("sp", "act", "pool")
ENGS = ("pe", "act", "dve", "pool", "sp")


class Op:
    __slots__ = ("eng", "fn", "deps", "marked", "cnt", "is_dma", "dsem", "dval", "relax")

    def __init__(self, eng, fn, is_dma=False):
        self.relax = False
        self.eng = eng
        self.fn = fn
        self.deps = []
        self.marked = False
        self.cnt = None
        self.is_dma = is_dma
        self.dsem = None
        self.dval = None


class Prog:
    def __init__(self):
        self.ops = {e: [] for e in ENGS}
        self.last_w = {}
        self.readers = {}
        self.dma_n = {q: 0 for q in QUEUES}
        self.dma_hist = {q: [] for q in QUEUES}
        self.out_dmas = []
        self.bar_deps = []
        self.bar_pending = set()

    def _add_dep(self, op, d):
        if d is None or d is op:
            return
        if (not d.is_dma) and d.eng == op.eng and op.eng == "pe" and not op.is_dma:
            return
        if op.relax and (not d.is_dma) and (not op.is_dma) and d.eng == op.eng:
            return
        if d not in op.deps:
            op.deps.append(d)
            d.marked = True

    def _track(self, op, reads, writes):
        if op.eng in self.bar_pending:
            self.bar_pending.discard(op.eng)
            for d in self.bar_deps:
                self._add_dep(op, d)
        for k in reads:
            self._add_dep(op, self.last_w.get(k))
        for k in writes:
            self._add_dep(op, self.last_w.get(k))
            for r in self.readers.get(k, ()):
                self._add_dep(op, r)
        for k in reads:
            lst = self.readers.setdefault(k, [])
            if not op.is_dma:
                for i in range(len(lst) - 1, -1, -1):
                    if (not lst[i].is_dma) and lst[i].eng == op.eng:
                        del lst[i]
            lst.append(op)
        for k in writes:
            self.last_w[k] = op
            self.readers[k] = []

    def op(self, eng, fn, reads=(), writes=(), relax=False):
        o = Op(eng, fn)
        o.relax = relax and RELAX_OK
        self._track(o, reads, writes)
        self.ops[eng].append(o)
        return o

    def dma(self, q, fn, reads=(), writes=(), is_output=False):
        o = Op(q, fn, is_dma=True)
        n = self.dma_n[q]
        self.dma_n[q] = n + 1
        o.dsem = (q, n % NDMASEM)
        o.dval = 16 * (n // NDMASEM + 1)
        hist = self.dma_hist[q]
        if n >= NDMASEM:
            o.deps.append(hist[n - NDMASEM])
        hist.append(o)
        self._track(o, reads, writes)
        self.ops[q].append(o)
        if is_output:
            self.out_dmas.append(o)
        return o

    def barrier(self):
        deps = []
        for e in ENGS:
            for o in reversed(self.ops[e]):
                if not o.is_dma:
                    deps.append(o)
                    break
        for q in QUEUES:
            deps.extend(self.dma_hist[q][-NDMASEM:])
        self.bar_deps = deps
        self.bar_pending = set(ENGS)
        self.last_w = {}
        self.readers = {}

    def emit(self, nc):
        EPOCH = 4000
        nep = {}
        for e in ENGS:
            c = 0
            for o in self.ops[e]:
                if not o.is_dma and o.marked:
                    c += 1
                    o.cnt = c
            nep[e] = (c + EPOCH - 1) // EPOCH
        with ExitStack() as es:
            csem = {}
            for e in ENGS:
                for ep in range(nep[e]):
                    csem[(e, ep)] = es.enter_context(nc.semaphore(f"s_{e}{ep}"))
            dsem = {}
            for q in QUEUES:
                if self.dma_n[q] == 0:
                    continue
                for i in range(NDMASEM):
                    dsem[(q, i)] = es.enter_context(nc.semaphore(f"d_{q}{i}"))
            block = es.enter_context(nc.Block())
            final = list(self.out_dmas)

            def run(ename, eng):
                waited = {}
                for o in self.ops[ename]:
                    need = {}
                    for d in o.deps:
                        if d.is_dma:
                            key, val, sem = d.dsem, d.dval, dsem[d.dsem]
                            if waited.get(key, 0) >= val:
                                continue
                            if key not in need or need[key][1] < val:
                                need[key] = (sem, val, val)
                        else:
                            ep, v = (d.cnt - 1) // EPOCH, (d.cnt - 1) % EPOCH + 1
                            if waited.get(d.eng, (-1, 0)) >= (ep, v):
                                continue
                            if d.eng not in need or need[d.eng][2] < (ep, v):
                                need[d.eng] = (csem[(d.eng, ep)], v, (ep, v))
                    items = list(need.items())
                    for key, (sem, val, rec) in items[:-1]:
                        eng.wait_ge(sem, val)
                        waited[key] = rec
                    ins = o.fn(eng)
                    if items:
                        key, (sem, val, rec) = items[-1]
                        ins._wait_ge(sem, val)
                        waited[key] = rec
                    if o.is_dma:
                        ins.then_inc(dsem[o.dsem], 16)
                    elif o.marked:
                        ins.then_inc(csem[(ename, (o.cnt - 1) // EPOCH)], 1)
                if ename == "sp":
                    for d in final:
                        if waited.get(d.dsem, 0) >= d.dval:
                            continue
                        eng.wait_ge(dsem[d.dsem], d.dval)
                        waited[d.dsem] = d.dval

            block.tensor(lambda eng: run("pe", eng))
            block.scalar(lambda eng: run("act", eng))
            block.vector(lambda eng: run("dve", eng))
            block.gpsimd(lambda eng: run("pool", eng))
            block.sync(lambda eng: run("sp", eng))


class Arena:
    def __init__(self, tile, size, name):
        self.tile = tile
        self.size = size
        self.off = 0
        self.name = name
        self.gen = 0

    def reset(self):
        self.off = 0
        self.gen += 1

    def alloc_bf16(self, shape):
        n = int(np.prod(shape))
        assert n % 2 == 0
        nf = n // 2
        assert self.off + nf <= self.size, (self.name, self.off, nf, self.size)
        ap = self.tile[:, self.off:self.off + nf].bitcast(BF16)
        key = f"{self.name}{self.gen}_{self.off}b"
        self.off += nf
        if len(shape) == 2:
            ap = ap.rearrange("p (a b) -> p a b", a=shape[0])
        elif len(shape) == 3:
            ap = ap.rearrange("p (a b c) -> p a b c", a=shape[0], b=shape[1])
        return ap, key

    def alloc(self, shape):
        n = int(np.prod(shape))
        assert self.off + n <= self.size, (self.name, self.off, n, self.size)
        ap = self.tile[:, self.off:self.off + n]
        key = f"{self.name}{self.gen}_{self.off}"
        self.off += n
        if len(shape) == 2:
            ap = ap.rearrange("p (a b) -> p a b", a=shape[0])
        elif len(shape) == 3:
            ap = ap.rearrange("p (a b c) -> p a b c", a=shape[0], b=shape[1])
        return ap, key


T = 4096
HALF = 2048
D = 1024
C_HQ, C_HF, C_HI, C_HG, C_NQ = 0, 1024, 2048, 3072, 4096
C_KC, C_VC, C_KS, C_VS, C_KW, C_VW = 5120, 5376, 5632, 5888, 6144, 6400
C_NG, C_GA, C_GB, N_IN = 6656, 6704, 7728, 8752
EPS = 1e-6
NEGB = -30000.0

import os
MASK_ENG = os.environ.get("KDBG_MASKENG", "pool")
DBG_BRANCHES = tuple(int(c) for c in os.environ.get("KDBG_BRANCHES", "12"))
DBG_NOBIAS = os.environ.get("KDBG_NOBIAS", "0") == "1"
DBG_NOEPI = os.environ.get("KDBG_NOEPI", "0") == "1"
DBG_NOPV = os.environ.get("KDBG_NOPV", "0") == "1"
KD_BCAST = os.environ.get("KDBG_KDBCAST", "1") == "1"
STORE_QUEUES = tuple(x for x in os.environ.get("KDBG_STOREQ", "act").split(",") if x)
FSZ = 22 * 1024
BSZ = 46 * 1024


def build(debug=None, stop_after=None):
    nc = bass.Bass("TRN2", target_bir_lowering=False)
    P = Prog()

    def din(name, shape, dt=F32):
        return nc.dram_tensor(name, list(shape), dt, kind="ExternalInput").ap()

    def dscr(name, shape, dt=BF16):
        return nc.dram_tensor(name, list(shape), dt, kind="Internal").ap()

    xo = din("xo", [HALF, D]); xc = din("xc", [HALF, D]); po = din("po", [HALF, 256])
    w_in = din("w_in", [D, N_IN])
    w_a = din("w_a", [D, D]); w_b = din("w_b", [D, D]); w_o = din("w_o", [D, D])
    w_up = din("w_up", [D, 4096]); w_down = din("w_down", [4096, D])
    w_ple = din("w_ple", [256, D]); w_pg = din("w_pg", [D, D])
    n_pre = din("n_pre", [1, D]); n_post = din("n_post", [1, D]); n_mpre = din("n_mpre", [1, D])
    n_mpost = din("n_mpost", [1, D]); n_ple = din("n_ple", [1, D])
    lbl = din("lbl", [2, D]); gnw = din("gnw", [1, 128])
    pe_k = din("pe_k", [32, 64]); pe_v = din("pe_v", [32, 64])
    wk1 = din("wk1", [2048, 256]); wk2 = din("wk2", [256, 64])
    wv1 = din("wv1", [2048, 256]); wv2 = din("wv2", [256, 64])
    c_ident = din("c_ident", [128, 128], BF16)
    c_perm = din("c_perm", [128, 128], BF16)
    c_cos = din("c_cos", [128, T]); c_sin = din("c_sin", [128, T])
    c_E = din("c_E", [64, T], BF16)
    c_triL = din("c_triL", [128, 128], BF16); c_triU = din("c_triU", [128, 128], BF16)
    c_ovl = din("c_ovl", [256, 64])
    c_cmpb = din("c_cmpb", [2, 128, HALF], BF16)
    c_selA = din("c_selA", [HALF, 64]); c_selB = din("c_selB", [HALF, 64])
    c_kval = din("c_kval", [128, 32])
    out = nc.dram_tensor("out", [HALF, D], F32, kind="ExternalOutput").ap()

    s_qh = dscr("s_qh", [D, HALF]); s_kh = dscr("s_kh", [D, HALF])
    s_kd = dscr("s_kd", [T, D]); s_v = dscr("s_v", [T, D]); s_g = dscr("s_g", [HALF, D], F32)
    s_q = dscr("s_q", [D, HALF]); s_qr = dscr("s_qr", [D, HALF])
    s_kc = dscr("s_kc", [256, T]); s_vc = dscr("s_vc", [256, T])
    s_ks = dscr("s_ks", [256, T]); s_kw = dscr("s_kw", [256, T])
    s_vs = dscr("s_vs", [T, 256]); s_vw = dscr("s_vw", [T, 256])
    s_ga = dscr("s_ga", [D, HALF]); s_gb = dscr("s_gb", [D, HALF])
    s_ya = dscr("s_ya", [HALF, D]); s_yb = dscr("s_yb", [HALF, D])
    s_h1 = dscr("s_h1", [HALF, D], F32); s_h2 = dscr("s_h2", [HALF, D], F32); s_vT = dscr("s_vT", [D, HALF])

    dbg = {}
    if debug:
        for name, spec in debug.items():
            shape, dts = spec
            dbg[name] = nc.dram_tensor("dbg_" + name, list(shape), BF16 if dts == "bf16" else F32, kind="ExternalOutput").ap()

    es = ExitStack()
    with es:
        sbt = lambda name, shape, dt: es.enter_context(nc.sbuf_tensor(name, shape, dt))
        pst = lambda name, shape, dt: es.enter_context(nc.psum_tensor(name, shape, dt))
        fa_t = sbt("arenaF", [128, FSZ], F32)
        ba_t = sbt("arenaB", [128, BSZ], BF16)
        FA = Arena(fa_t, FSZ, "F")
        BA = Arena(ba_t, BSZ, "B")
        ident = sbt("ident", [128, 128], BF16)
        perm = sbt("perm", [128, 128], BF16)
        triL = sbt("triL", [128, 128], BF16)
        triU = sbt("triU", [128, 128], BF16)
        lb = sbt("lb", [128, 8], F32)
        omlb = sbt("omlb", [128, 8], F32)
        lb2 = sbt("lb2", [128, 8], F32)
        PL = sbt("PL", [128, 8, 64], F32)
        gates = sbt("gates", [128, 16, 48], F32)
        zeros = sbt("zeros", [128, 64], F32)
        epsT = sbt("epsT", [128, 1], F32)
        tinyT = sbt("tinyT", [128, 1], F32)
        onesT = sbt("onesT", [128, 1], F32)
        junk = sbt("junk", [128, 1024], BF16)
        stat = sbt("stat", [128, 64], F32)
        PSF = [pst(f"psf{i}", [128, 512], F32) for i in range(6)]
        PSB = [pst(f"psb{i}", [128, 1024], BF16) for i in range(2)]

        def ld(dst, src, key, q="sp", reads=(), slow=False):
            if slow:
                return P.dma(q, lambda e: e.dma_start(out=dst, in_=src, allow_slow_non_contiguous=True), reads=list(reads), writes=[key])
            return P.dma(q, lambda e: e.dma_start(out=dst, in_=src), reads=list(reads), writes=[key])

        uq = [0]

        def st(dst, src, key, wkey, q=None, is_output=False):
            if wkey == "SCR_ALL":
                uq[0] += 1
                wkey = f"scr{uq[0]}"
            if q is None:
                lw = P.last_w.get(key)
                q = lw.eng if (lw is not None and not lw.is_dma and lw.eng in STORE_QUEUES) else "pool"
            return P.dma(q, lambda e: e.dma_start(out=dst, in_=src), reads=[key], writes=[wkey], is_output=is_output)

        def mm(o, lhsT, rhs, start, stop, reads, okey, skip=False):
            if skip:
                return P.op("pe", lambda e: e.matmul(o, lhsT=lhsT, rhs=rhs, start=start, stop=stop, skip_group_check=True), reads=reads, writes=[okey])
            return P.op("pe", lambda e: e.matmul(o, lhsT=lhsT, rhs=rhs, start=start, stop=stop), reads=reads, writes=[okey])

        def tr(o, in_, reads, okey):
            return P.op("pe", lambda e: e.transpose(out=o, in_=in_, identity=ident[:]), reads=list(reads) + ["ident"], writes=[okey])

        def act(o, in_, func, reads, okey, bias=None, scale=None, accum=None, eng="act"):
            kw = {}
            if bias is not None:
                kw["bias"] = bias
            if scale is not None:
                kw["scale"] = scale
            if accum is not None:
                kw["accum_out"] = accum
            wk = [okey] if isinstance(okey, str) else list(okey)
            return P.op("act", lambda e: e.activation(out=o, in_=in_, func=func, **kw), reads=reads, writes=wk)

        def cp(eng, o, in_, reads, okey):
            if eng == "act":
                return P.op("act", lambda e: e.copy(out=o, in_=in_), reads=reads, writes=[okey])
            return P.op(eng, lambda e: e.tensor_copy(out=o, in_=in_), reads=reads, writes=[okey])

        def tt(eng, o, a, b, op, reads, okey, relax=False):
            return P.op(eng, lambda e: e.tensor_tensor(out=o, in0=a, in1=b, op=op), reads=reads, writes=[okey], relax=relax)

        def tsc(eng, o, a, s1, s2, op0, op1, reads, okey, relax=False):
            if op1 is None:
                return P.op(eng, lambda e: e.tensor_scalar(out=o, in0=a, scalar1=s1, scalar2=None, op0=op0), reads=reads, writes=[okey], relax=relax)
            return P.op(eng, lambda e: e.tensor_scalar(out=o, in0=a, scalar1=s1, scalar2=s2, op0=op0, op1=op1), reads=reads, writes=[okey], relax=relax)

        def stt(eng, o, a, s, b, op0, op1, reads, okey, relax=False):
            return P.op(eng, lambda e: e.scalar_tensor_tensor(out=o, in0=a, scalar=s, in1=b, op0=op0, op1=op1), reads=reads, writes=[okey], relax=relax)

        def dbg_out(name, src_ap, key):
            if name in dbg:
                st(dbg[name], src_ap, key, "dbg_" + name, q="sp", is_output=True)

        ld(ident[:], c_ident, "ident"); ld(perm[:], c_perm, "perm")
        ld(triL[:], c_triL, "triL"); ld(triU[:], c_triU, "triU")
        P.op("pool", lambda e: e.memset(zeros[:], 0.0), writes=["zeros"])
        P.op("pool", lambda e: e.memset(epsT[:], EPS), writes=["epsT"])
        P.op("pool", lambda e: e.memset(tinyT[:], 1e-30), writes=["tinyT"])
        P.op("pool", lambda e: e.memset(onesT[:], 1.0), writes=["onesT"])
        ld(lb[:], lbl[0, :].rearrange("(h p) -> p h", p=128), "lb", slow=True)
        ld(lb2[:], lbl[1, :].rearrange("(h p) -> p h", p=128), "lb2", slow=True)
        tt("dve", lb[:], lb[:], lb2[:], ALU.subtract, ["lb", "lb2"], "lb")
        act(lb[:], lb[:], AF.Sigmoid, ["lb"], "lb")
        tsc("dve", omlb[:], lb[:], -1.0, 1.0, ALU.mult, ALU.add, ["lb"], "omlb")

        def rms_rstd(src, skey, slot, n=1024.0, eng_reads=()):
            ss = stat[:, slot:slot + 1]
            k = f"stat{slot}"
            act(junk[:, :src.shape[-1]] if len(src.shape) == 2 else junk[:], src, AF.Square, [skey] + list(eng_reads), [k, "junk"], accum=ss)
            act(ss, ss, AF.Sqrt, [k, "epsT"], k, bias=epsT[:, 0:1], scale=1.0 / n)
            P.op("dve", lambda e: e.reciprocal(out=ss, in_=ss), reads=[k], writes=[k])
            return ss, k

        uT, k_uT = BA.alloc([8, T])
        nbc, k_nbc = FA.alloc([D])
        ld(nbc, n_pre.to_broadcast([128, D]), k_nbc)
        xts = [FA.alloc([D]) for _ in range(2)]
        xns = [BA.alloc([D]) for _ in range(2)]
        for ti in range(32):
            src = xc if ti < 16 else xo
            r0 = (ti % 16) * 128
            xt, kx = xts[ti % 2]
            xn, kn = xns[ti % 2]
            ld(xt, src[r0:r0 + 128, :], kx)
            ss, ks_ = rms_rstd(xt, kx, ti % 2)
            stt("dve", xn, xt, ss, nbc, ALU.mult, ALU.mult, [kx, ks_, k_nbc], kn)
            pb = PSB[ti % 2]
            for k in range(8):
                tr(pb[:, k * 128:(k + 1) * 128], xn[:, k * 128:(k + 1) * 128], [kn], f"psb{ti % 2}")
            dst = uT[:, :, ti * 128:(ti + 1) * 128]
            cp("act" if ti % 2 == 0 else "dve", dst, pb[:].rearrange("p (k t) -> p k t", k=8), [f"psb{ti % 2}"], f"uT{ti}")

        pending_dumps = []

        def dump_scr(name, src):
            if name in dbg:
                pending_dumps.append((name, src))

        def flush_dumps():
            for name, src in pending_dumps:
                P.dma("sp", (lambda d_, s_: (lambda e: e.dma_start(out=d_, in_=s_)))(dbg[name], src), reads=[], writes=["dbg_" + name], is_output=True)
            pending_dumps.clear()

        P.barrier()
        FA.reset()
        BA.off = 8 * T
        uT_keys_blk = lambda tb: [f"uT{ti}" for ti in range(4 * tb, 4 * tb + 4)]
        Wst = [FA.alloc([8, 512]) for _ in range(2)]
        Wb = [BA.alloc([8, 512]) for _ in range(2)]
        gw512, k_gw = FA.alloc([512])
        for i in range(4):
            ld(gw512[:, i * 128:(i + 1) * 128], gnw.to_broadcast([128, 128]), k_gw)
        STREAM = [0]
        WbH = FA.alloc_bf16([8, 512])
        ftiles_s = [[FA.alloc([512]) for _ in range(12)], [FA.alloc([512]) for _ in range(8)]]
        btiles_s = [[BA.alloc([512]) for _ in range(8)], [BA.alloc([512]) for _ in range(4)]]
        fctr = [0, 0]; bctr = [0, 0]

        def ftile():
            s_ = STREAM[0]
            fctr[s_] += 1
            return ftiles_s[s_][fctr[s_] % len(ftiles_s[s_])]

        def btile():
            s_ = STREAM[0]
            bctr[s_] += 1
            return btiles_s[s_][bctr[s_] % len(btiles_s[s_])]

        gctr = [0]
        wbctr = [0]
        psctr = [0, 0]

        def next_ps():
            s_ = STREAM[0]
            psctr[s_] += 1
            if s_ == 0:
                i = psctr[0] % 2
            else:
                i = 2 + psctr[1] % 4
            return PSF[i], f"psf{i}"

        def load_group(col_segs):
            gi = gctr[0] % 2
            gctr[0] += 1
            ws, kws = Wst[gi]
            if STREAM[0] == 0:
                wb, kwb = WbH
            else:
                wb, kwb = Wb[wbctr[0] % 2]
                wbctr[0] += 1
            off = 0
            for (c0, n) in col_segs:
                ld(ws[:, :, off:off + n], w_in[:, c0:c0 + n].rearrange("(k p) n -> p k n", p=128), kws)
                off += n
            cp("act", wb[:, 0:4, 0:off], ws[:, 0:4, 0:off], [kws], kwb)
            cp("act", wb[:, 4:8, 0:off], ws[:, 4:8, 0:off], [kws], kwb)
            return wb, kwb

        def ftype(wb, kwb, c_off, tb):
            ps, kps = next_ps()
            for k in range(8):
                mm(ps[:], wb[:, k, c_off:c_off + 128], uT[:, k, tb * 512:(tb + 1) * 512], k == 0, k == 7,
                   [kwb] + uT_keys_blk(tb), kps)
            return ps, kps

        def ttype(wb, kwb, ncols, ti):
            ps, kps = next_ps()
            for k in range(8):
                mm(ps[:, 0:ncols], uT[:, k, ti * 128:(ti + 1) * 128], wb[:, k, 0:ncols], k == 0, k == 7,
                   [kwb, f"uT{ti}"], kps)
            return ps, kps

        hg_tails = []

        def hgrn_gen():
            STREAM[0] = 0
            HSCALE = 128.0 ** -0.5
            for gi4 in range(4):
                wb, kwb = load_group([(C_HQ + 256 * gi4, 256), (C_HF + 256 * gi4, 256)])
                for hh in range(2):
                    hd = 2 * gi4 + hh
                    for tb in range(8):
                        own = tb >= 4
                        pf, kpf = ftype(wb, kwb, 256 + hh * 128, tb)
                        if hg_tails:
                            hg_tails.pop(0)()
                        sg, ksg = ftile()
                        act(sg, pf[:], AF.Sigmoid, [kpf], ksg)
                        fg, kfg = ftile()
                        act(fg, sg, AF.Identity, [ksg, "omlb", "lb"], kfg, bias=lb[:, hd:hd + 1], scale=omlb[:, hd:hd + 1])
                        Pt, kP = ftile()
                        for c in range(8):
                            P.op("dve", (lambda o_, d0: (lambda e: e.tensor_tensor_scan(out=o_, data0=d0, data1=zeros[:], initial=1.0,
                                                                                        op0=ALU.mult, op1=ALU.add)))(Pt[:, c * 64:(c + 1) * 64], fg[:, c * 64:(c + 1) * 64]),
                                 reads=[kfg, "zeros"], writes=[kP], relax=True)
                        cp("act", PL[:, hd, tb * 8:(tb + 1) * 8], Pt[:, 63::64], [kP], f"PL{hd}")
                        rP, krP = ftile()
                        tsc("dve", rP, Pt, 1e-30, None, ALU.max, None, [kP], krP, relax=True)
                        P.op("dve", (lambda o_: (lambda e: e.reciprocal(out=o_, in_=o_)))(rP), reads=[krP], writes=[krP], relax=True)
                        kk_, kkk = ftile()
                        act(kk_, fg, AF.Identity, [kfg, "onesT"], kkk, bias=onesT[:, 0:1], scale=-1.0)
                        kt, kkt = btile()
                        tt("dve", kt, kk_, rP, ALU.mult, [kkk, krP], kkt, relax=True)
                        kd, kkd = btile()
                        if KD_BCAST:
                            plb = Pt.rearrange("p (c s) -> p c s", s=64)[:, :, 63:64].to_broadcast([128, 8, 64])
                            tt("pool", kd.rearrange("p (c s) -> p c s", s=64), kt.rearrange("p (c s) -> p c s", s=64), plb, ALU.mult, [kkt, kP], kkd)
                        else:
                            for c in range(8):
                                tsc("pool", kd[:, c * 64:(c + 1) * 64], kt[:, c * 64:(c + 1) * 64], Pt[:, c * 64 + 63:c * 64 + 64], None,
                                    ALU.mult, None, [kkt, kP], kkd)
                        def tail(hd=hd, tb=tb, kd=kd, kkd=kkd):
                            pbi = (hd * 8 + tb) % 2
                            pb = PSB[pbi]
                            for i in range(4):
                                tr(pb[:, i * 128:(i + 1) * 128], kd[:, i * 128:(i + 1) * 128], [kkd], f"psb{pbi}")
                            kdT, kkdT = btile()
                            cp("act", kdT, pb[:, 0:512], [f"psb{pbi}"], kkdT)
                            st(s_kd[tb * 512:(tb + 1) * 512, hd * 128:(hd + 1) * 128].rearrange("(i p) k -> p i k", p=128),
                               kdT.rearrange("p (i k) -> p i k", i=4), kkdT, "SCR_ALL")
                        hg_tails.append(tail)
                        if own:
                            st(s_kh[hd * 128:(hd + 1) * 128, (tb - 4) * 512:(tb - 3) * 512], kt, kkt, "SCR_ALL")
                            pq, kpq = ftype(wb, kwb, hh * 128, tb)
                            sq, ksq = ftile()
                            act(sq, pq[:], AF.Sigmoid, [kpq], ksq)
                            tt("dve", sq, sq, pq[:], ALU.mult, [ksq, kpq], ksq, relax=True)
                            qh, kqh = btile()
                            stt("dve", qh, sq, HSCALE, Pt, ALU.mult, ALU.mult, [ksq, kP], kqh, relax=True)
                            st(s_qh[hd * 128:(hd + 1) * 128, (tb - 4) * 512:(tb - 3) * 512], qh, kqh, "SCR_ALL")
                        yield
            while hg_tails:
                hg_tails.pop(0)()
            yield
        dump_scr("s_qh", s_qh); dump_scr("s_kh", s_kh); dump_scr("s_kd", s_kd)
        if "PL" in dbg:
            P.dma("sp", lambda e: e.dma_start(out=dbg["PL"], in_=PL[:]), reads=[f"PL{h}" for h in range(8)], writes=["dbg_PL"], is_output=True)

        def rest_gen():
            STREAM[0] = 1
            for g2 in range(2):
                wb, kwb = load_group([(C_HI + 512 * g2, 512)])
                for ti in range(32):
                    ps, kps = ttype(wb, kwb, 512, ti)
                    vb, kvb = btile()
                    cp("act", vb, ps[:], [kps], kvb)
                    st(s_v[ti * 128:(ti + 1) * 128, g2 * 512:(g2 + 1) * 512], vb, kvb, "SCR_ALL")
                    yield
            for g2 in range(2):
                wb, kwb = load_group([(C_HG + 512 * g2, 512)])
                for ti in range(16, 32):
                    ps, kps = ttype(wb, kwb, 512, ti)
                    sgt, ksgt = ftile()
                    act(sgt, ps[:], AF.Sigmoid, [kps], ksgt)
                    tt("dve", sgt, sgt, ps[:], ALU.mult, [ksgt, kps], ksgt, relax=True)
                    gb_, kgb = ftile()
                    tt("dve", gb_, sgt, gw512, ALU.mult, [ksgt, k_gw], kgb, relax=True)
                    st(s_g[(ti - 16) * 128:(ti - 15) * 128, g2 * 512:(g2 + 1) * 512], gb_, kgb, "SCR_ALL")
                    yield

            def rope_store(ps, kps, pos0, scale, dst_plain, dst_rot):
                qb, kqb = btile()
                act(qb, ps[:], AF.Copy, [kps], kqb, scale=scale)
                if dst_plain is not None:
                    st(dst_plain, qb, kqb, "SCR_ALL")
                cs, kcs = ftile(); sn, ksn = ftile()
                ld(cs, c_cos[:, pos0:pos0 + 512], kcs); ld(sn, c_sin[:, pos0:pos0 + 512], ksn)
                pp, kpp = next_ps()
                mm(pp[:], perm[:], qb, True, True, ["perm", kqb], kpp)
                t1, kt1 = ftile()
                tt("dve", t1, qb, cs, ALU.mult, [kqb, kcs], kt1, relax=True)
                t2, kt2 = ftile()
                tt("dve", t2, pp[:], sn, ALU.mult, [kpp, ksn], kt2, relax=True)
                qr, kqr = btile()
                tt("dve", qr, t1, t2, ALU.add, [kt1, kt2], kqr, relax=True)
                st(dst_rot, qr, kqr, "SCR_ALL")

            for g2 in range(2):
                wb, kwb = load_group([(C_NQ + 512 * g2, 512)])
                for ct in range(4):
                    row0 = g2 * 512 + ct * 128
                    for tb in range(4, 8):
                        ps, kps = ftype(wb, kwb, ct * 128, tb)
                        c0 = (tb - 4) * 512
                        rope_store(ps, kps, tb * 512, 0.125, s_q[row0:row0 + 128, c0:c0 + 512], s_qr[row0:row0 + 128, c0:c0 + 512])
                        yield
            wb, kwb = load_group([(C_KC, 512)])
            for ct in range(4):
                dst = s_kc if ct < 2 else s_vc
                row0 = (ct % 2) * 128
                for tb in range(8):
                    ps, kps = ftype(wb, kwb, ct * 128, tb)
                    ob, kob = btile()
                    cp("act", ob, ps[:], [kps], kob)
                    st(dst[row0:row0 + 128, tb * 512:(tb + 1) * 512], ob, kob, "SCR_ALL")
                    yield
            wb, kwb = load_group([(C_KS, 256), (C_KW, 256)])
            for ct in range(4):
                dst = s_ks if ct < 2 else s_kw
                row0 = (ct % 2) * 128
                for tb in range(8):
                    ps, kps = ftype(wb, kwb, ct * 128, tb)
                    rope_store(ps, kps, tb * 512, 1.0, None, dst[row0:row0 + 128, tb * 512:(tb + 1) * 512])
                    yield
            wb, kwb = load_group([(C_VS, 256), (C_VW, 256)])
            for ti in range(32):
                ps, kps = ttype(wb, kwb, 512, ti)
                vb, kvb = btile()
                cp("act" if ti % 2 else "dve", vb, ps[:], [kps], kvb)
                st(s_vs[ti * 128:(ti + 1) * 128, :], vb[:, 0:256], kvb, "SCR_ALL")
                st(s_vw[ti * 128:(ti + 1) * 128, :], vb[:, 256:512], kvb, "SCR_ALL")
                yield
            wb, kwb = load_group([(C_NG, 48)])
            for ti in range(16, 32):
                ps, kps = ttype(wb, kwb, 48, ti)
                act(gates[:, ti - 16, :], ps[:, 0:48], AF.Sigmoid, [kps], f"gates{ti - 16}")
                yield
            for gsel, (c_base, dst) in enumerate(((C_GA, s_ga), (C_GB, s_gb))):
                for g2 in range(2):
                    wb, kwb = load_group([(c_base + 512 * g2, 512)])
                    for ct in range(4):
                        row0 = g2 * 512 + ct * 128
                        for tb in range(4, 8):
                            ps, kps = ftype(wb, kwb, ct * 128, tb)
                            ob, kob = btile()
                            act(ob, ps[:], AF.Sigmoid, [kps], kob)
                            st(dst[row0:row0 + 128, (tb - 4) * 512:(tb - 3) * 512], ob, kob, "SCR_ALL")
                            yield

        g_h = hgrn_gen(); g_r = rest_gen()
        alive_h = alive_r = True
        while alive_h or alive_r:
            if alive_h:
                STREAM[0] = 0
                try:
                    next(g_h)
                except StopIteration:
                    alive_h = False
            for _ in range(5 if alive_h else 1000000):
                if not alive_r:
                    break
                STREAM[0] = 1
                try:
                    next(g_r)
                except StopIteration:
                    alive_r = False
        STREAM[0] = 0

        for nm, ap_ in (("s_v", s_v), ("s_g", s_g), ("s_q", s_q), ("s_qr", s_qr), ("s_kc", s_kc), ("s_vc", s_vc), ("s_ks", s_ks),
                        ("s_kw", s_kw), ("s_vs", s_vs), ("s_vw", s_vw), ("s_ga", s_ga), ("s_gb", s_gb)):
            dump_scr(nm, ap_)
        if "gates" in dbg:
            P.dma("sp", lambda e: e.dma_start(out=dbg["gates"], in_=gates[:]), reads=[f"gates{i}" for i in range(16)], writes=["dbg_gates"], is_output=True)

        P.barrier()
        flush_dumps()

        FA.reset(); BA.reset()
        NBLK = 16
        Sst = [FA.alloc([128]) for _ in range(8)]
        Sb = [BA.alloc([128]) for _ in range(8)]
        for hd in range(8):
            P.op("pool", (lambda o_: (lambda e: e.memset(o_, 0.0)))(Sst[hd][0]), writes=[Sst[hd][1]])
        kdB = [[BA.alloc([4, 128]) for _ in range(2)] for _ in range(8)]
        vB = [[BA.alloc([4, 128]) for _ in range(2)] for _ in range(8)]
        qB = [[BA.alloc([256]) for _ in range(2)] for _ in range(8)]
        kB = [[BA.alloc([256]) for _ in range(2)] for _ in range(8)]
        gB = [[FA.alloc([4, 128]) for _ in range(2)] for _ in range(8)]
        yst = [[BA.alloc([4, 128]) for _ in range(2)] for _ in range(8)]
        ATs = [BA.alloc([64]) for _ in range(4)]
        c_tail = []
        for blk in range(NBLK):
            own = blk >= 8
            bi = blk % 2
            for hd in range(8):
                t0 = blk * 256
                kd_t, kkd = kdB[hd][bi]; v_t, kv = vB[hd][bi]
                ld(kd_t[0:64], s_kd[t0:t0 + 256, hd * 128:(hd + 1) * 128].rearrange("(j s) k -> s j k", s=64), kkd)
                ld(v_t[0:64], s_v[t0:t0 + 256, hd * 128:(hd + 1) * 128].rearrange("(j s) k -> s j k", s=64), kv)
                if own:
                    o0 = t0 - HALF
                    ld(qB[hd][bi][0], s_qh[hd * 128:(hd + 1) * 128, o0:o0 + 256], qB[hd][bi][1])
                    ld(kB[hd][bi][0], s_kh[hd * 128:(hd + 1) * 128, o0:o0 + 256], kB[hd][bi][1])
                    ld(gB[hd][bi][0][0:64], s_g[o0:o0 + 256, hd * 128:(hd + 1) * 128].rearrange("(j s) k -> s j k", s=64), gB[hd][bi][1])
            for j in range(4):
                c = blk * 4 + j

                def issue_A(hd_):
                    q_t_, kq_ = qB[hd_][bi]; k_t_, kk2_ = kB[hd_][bi]
                    psA_, kpsA_ = PSF[4 + hd_ % 2], f"psf{4 + hd_ % 2}"
                    mm(psA_[0:64, 0:64], k_t_[:, j * 64:(j + 1) * 64], q_t_[:, j * 64:(j + 1) * 64], True, True, [kk2_, kq_], kpsA_)
                    at_t_, kat_ = ATs[(c * 8 + hd_) % 4]
                    tt("dve", at_t_[0:64, :], psA_[0:64, 0:64], triL[0:64, 0:64], ALU.mult, [kpsA_, "triL"], kat_)

                if own:
                    issue_A(0)
                for hd in range(8):
                    kd_t, kkd = kdB[hd][bi]; v_t, kv = vB[hd][bi]
                    S_t, kS = Sst[hd]; Sb_t, kSb = Sb[hd]
                    psi = hd % 2
                    if own:
                        q_t, kq = qB[hd][bi]; k_t, kk2 = kB[hd][bi]; g_t, kg = gB[hd][bi]
                        y_t, ky = yst[hd][bi]
                        psO, kpsO = PSF[2 + psi], f"psf{2 + psi}"
                        if hd + 1 < 8:
                            issue_A(hd + 1)
                        mm(psO[0:64, 0:128], q_t[:, j * 64:(j + 1) * 64], Sb_t, True, False, [kq, kSb], kpsO)
                        at_t, kat = ATs[(c * 8 + hd) % 4]
                        mm(psO[0:64, 0:128], at_t[0:64, :], v_t[0:64, j, :], False, True, [kat, kv], kpsO)
                    psS, kpsS = PSF[psi], f"psf{psi}"
                    mm(psS[:, 0:128], kd_t[0:64, j, :], v_t[0:64, j, :], True, True, [kkd, kv], kpsS)
                    stt("dve", S_t, S_t, PL[:, hd, c:c + 1], psS[:, 0:128], ALU.mult, ALU.add, [kS, f"PL{hd}", kpsS], kS)
                    if c >= 31 and c < 63:
                        cp("act", Sb_t, S_t, [kS], kSb)
                    if own:
                        if c_tail:
                            c_tail.pop(0)()
                        slot = 8 + hd
                        ss = stat[0:64, slot:slot + 1]; kss = f"stat{slot}"
                        act(junk[0:64, 0:128], psO[0:64, 0:128], AF.Square, [kpsO], [kss, "junk"], accum=ss)
                        act(ss, ss, AF.Sqrt, [kss, "epsT"], kss, bias=epsT[0:64, 0:1], scale=1.0 / 128.0)

                        def tail(ss=ss, kss=kss, y_t=y_t, ky=ky, psO=psO, kpsO=kpsO, g_t=g_t, kg=kg, j=j):
                            P.op("dve", (lambda o_: (lambda e: e.reciprocal(out=o_, in_=o_)))(ss), reads=[kss], writes=[kss])
                            stt("dve", y_t[0:64, j, :], psO[0:64, 0:128], ss, g_t[0:64, j, :], ALU.mult, ALU.mult, [kpsO, kss, kg], ky)
                        c_tail.append(tail)
                while c_tail:
                    c_tail.pop(0)()
            if own:
                for hd in range(8):
                    y_t, ky = yst[hd][bi]
                    o0 = blk * 256 - HALF
                    st(s_ya[o0:o0 + 256, hd * 128:(hd + 1) * 128].rearrange("(j s) k -> s j k", s=64), y_t[0:64], ky, "SCR_ALL")
        dump_scr("s_ya", s_ya)
        P.barrier()
        flush_dumps()

        FA.reset(); BA.reset()
        kcmpT2, k_kcmp = BA.alloc([4, 256])
        VcAug, k_vca = BA.alloc([4, 2, 129])
        KEa, k_KEa = BA.alloc([T])
        KEb, k_KEb = BA.alloc([T])
        cmpb, k_cmpb = BA.alloc([2, HALF])
        ld(KEa[64:128], c_E, k_KEa + "E")
        ld(KEb[0:64], c_E, k_KEb + "E")
        ld(cmpb, c_cmpb.rearrange("c p t -> p c t"), k_cmpb)
        ovl_f, k_ovl = FA.alloc([2, 64])
        ld(ovl_f, c_ovl.rearrange("(c p) s -> p c s", p=128), k_ovl)
        P.op("pool", lambda e: e.memset(VcAug, 0.0), writes=[k_vca])
        P.op("pool", lambda e: e.memset(kcmpT2, 0.0), writes=[k_kcmp])
        for g in range(4):
            cp("dve", VcAug[:, g, :, 65:129], ovl_f, [k_ovl], k_vca)
            P.op("pool", (lambda o_: (lambda e: e.memset(o_, 1.0)))(VcAug[:, g, :, 64:65]), writes=[k_vca])
        selA_t, k_selA = FA.alloc([16, 64]); selB_t, k_selB = FA.alloc([16, 64])
        for q4 in range(2):
            ld(selA_t[:, q4 * 8:(q4 + 1) * 8, :], c_selA[q4 * 1024:(q4 + 1) * 1024, :].rearrange("(ti p) s -> p ti s", p=128), k_selA)
            ld(selB_t[:, q4 * 8:(q4 + 1) * 8, :], c_selB[q4 * 1024:(q4 + 1) * 1024, :].rearrange("(ti p) s -> p ti s", p=128), k_selB)
        kval_t, k_kval = FA.alloc([32])
        ld(kval_t, c_kval, k_kval)
        dmark_B = BA.off; dmark_F = FA.off

        w1st, k_w1st = FA.alloc([32, 256])
        w1b, k_w1b = BA.alloc([32, 256])
        w2st, k_w2st = FA.alloc([2, 64]); w2b, k_w2b = BA.alloc([2, 64])
        peT, k_peT = FA.alloc([32])
        kcTs = [BA.alloc([T]) for _ in range(2)]
        peTb, k_peTb = BA.alloc([32])
        cvec, k_cvec = FA.alloc([2])
        xss = [FA.alloc([256]) for _ in range(4)]
        geTs = [BA.alloc([2, 256]) for _ in range(2)]
        x2s = [FA.alloc([256]) for _ in range(4)]; inns = [FA.alloc([256]) for _ in range(4)]
        STREAM[0] = 1
        d0_tails = []
        for kv in range(2):
            w1_d = wk1 if kv == 0 else wv1
            w2_d = wk2 if kv == 0 else wv2
            pe_d = pe_k if kv == 0 else pe_v
            src_d = s_kc if kv == 0 else s_vc
            for q4 in range(2):
                ld(w1st[0:64, q4 * 16:(q4 + 1) * 16, :], w1_d[q4 * 1024:(q4 + 1) * 1024, :].rearrange("(l d) h -> d l h", d=64), k_w1st)
            cp("dve", w1b[0:64], w1st[0:64], [k_w1st], k_w1b)
            ld(w2st, w2_d.rearrange("(c p) d -> p c d", p=128), k_w2st)
            cp("dve", w2b, w2st, [k_w2st], k_w2b)
            ld(peT[0:64], pe_d.rearrange("l d -> d l"), k_peT, slow=True)
            cp("dve", peTb[0:64], peT[0:64], [k_peT], k_peTb)
            for hc in range(2):
                ps, kps = next_ps()
                for l in range(32):
                    mm(ps[:, 0:1], w1b[0:64, l, hc * 128:(hc + 1) * 128], peTb[0:64, l:l + 1], l == 0, l == 31, [k_w1b, k_peTb], kps)
                cp("dve", cvec[:, hc:hc + 1], ps[:, 0:1], [kps], k_cvec + f"_{hc}")
            for g in range(4):
                kcT, k_kcT = kcTs[g % 2]
                geT, k_geT = geTs[g % 2]
                ld(kcT[0:64], src_d[g * 64:(g + 1) * 64, :], k_kcT)
                for hc in range(2):
                    ps, kps = next_ps()
                    for l in range(32):
                        mm(ps[:, 0:255], w1b[0:64, l, hc * 128:(hc + 1) * 128], kcT[0:64, l:l + 16 * 254 + 1:16], l == 0, l == 31, [k_w1b, k_kcT], kps)
                    if hc == 0 and d0_tails:
                        d0_tails.pop(0)()
                    bi_ = (g % 2) * 2 + hc
                    xs, kxs = xss[bi_]; x2, k_x2 = x2s[bi_]; inn, k_inn = inns[bi_]
                    act(xs[:, 0:255], ps[:, 0:255], AF.Identity, [kps, k_cvec + f"_{hc}"], kxs, bias=cvec[:, hc:hc + 1])
                    tt("dve", x2[:, 0:255], xs[:, 0:255], xs[:, 0:255], ALU.mult, [kxs], k_x2)
                    tsc("dve", inn[:, 0:255], x2[:, 0:255], 0.044715, 1.0, ALU.mult, ALU.add, [k_x2], k_inn)
                    tt("dve", inn[:, 0:255], inn[:, 0:255], xs[:, 0:255], ALU.mult, [k_inn, kxs], k_inn)
                    act(inn[:, 0:255], inn[:, 0:255], AF.Sigmoid, [k_inn], k_inn, scale=1.5957691216057308)
                    tt("dve", geT[:, hc, 0:255], inn[:, 0:255], xs[:, 0:255], ALU.mult, [k_inn, kxs], k_geT + f"_{hc}")

                def tail(kv=kv, g=g, geT=geT, k_geT=k_geT):
                    if kv == 0:
                        ps, kps = next_ps()
                        for hc in range(2):
                            mm(ps[0:64, 0:255], w2b[:, hc, :], geT[:, hc, 0:255], hc == 0, hc == 1, [k_w2b, k_geT + f"_{hc}"], kps)
                        cp("dve", kcmpT2[0:64, g, 0:255], ps[0:64, 0:255], [kps], k_kcmp)
                        P.dma("sp", (lambda o_, i_: (lambda e: e.dma_start(out=o_, in_=i_)))(kcmpT2[64:128, g, :], kcmpT2[0:64, g, :]),
                              reads=[k_kcmp], writes=[k_kcmp + "hi"])
                    else:
                        for nc_ in range(2):
                            nn = 128 if nc_ == 0 else 127
                            ps, kps = next_ps()
                            for hc in range(2):
                                mm(ps[0:nn, 0:64], geT[:, hc, nc_ * 128:nc_ * 128 + nn], w2b[:, hc, :], hc == 0, hc == 1,
                                   [k_geT + f"_{hc}", k_w2b], kps)
                            cp("dve", VcAug[0:nn, g, nc_, 0:64], ps[0:nn, 0:64], [kps], k_vca)
                d0_tails.append(tail)
            while d0_tails:
                d0_tails.pop(0)()
        STREAM[0] = 0
        if "kcmp" in dbg:
            t32, k32 = FA.alloc([4, 256])
            cp("dve", t32, kcmpT2, [k_kcmp, k_kcmp + "hi"], k32)
            st(dbg["kcmp"], t32, k32, "dbg_kcmp", q="sp", is_output=True)
        if "vcmp" in dbg:
            t32b, k32b = FA.alloc([4, 2, 129])
            cp("dve", t32b, VcAug, [k_vca], k32b)
            st(dbg["vcmp"], t32b, k32b, "dbg_vcmp", q="sp", is_output=True)
        P.barrier()
        BA.off = dmark_B; FA.off = dmark_F
        BA.gen += 1; FA.gen += 1

        if stop_after == "D0":
            P.emit(nc)
            return nc
        q2, k_q2 = BA.alloc([2, HALF])
        QB = [BA.alloc([HALF]) for _ in range(4)]
        kwT2, k_kwT = BA.alloc([T])
        VsAug, k_vsa = BA.alloc([32, 65]); VwAug, k_vwa = BA.alloc([32, 65])
        pTs = [BA.alloc([512]) for _ in range(4)]
        ybb, k_ybb = BA.alloc([16, 256])
        biasToks = [BA.alloc([128]) for _ in range(2)]
        yb, k_yb = FA.alloc([16, 256])
        imp, k_imp = FA.alloc([16, 64])
        tk_a, k_tka = FA.alloc([64]); tk_b, k_tkb = FA.alloc([64]); tk_c, k_tkc = FA.alloc([64])
        m8a, k_m8a = FA.alloc([8]); m8b, k_m8b = FA.alloc([8])
        ptc = [0]

        def epilogue(psv, kpsv, ti, hloc, h, branch, first):
            slot = 16 + (ptc[0] % 8) * 2
            ptc[0] += 1
            rd = stat[:, slot:slot + 1]; krd = f"stat{slot}"
            gr = stat[:, slot + 1:slot + 2]; kgr = f"stat{slot + 1}"
            tsc("dve", rd, psv[:, 64:65], 1e-30, None, ALU.add, None, [kpsv], krd)
            P.op("dve", (lambda o_: (lambda e: e.reciprocal(out=o_, in_=o_)))(rd), reads=[krd], writes=[krd])
            tt("dve", gr, rd, gates[:, ti, branch * 16 + h:branch * 16 + h + 1], ALU.mult, [krd, f"gates{ti}"], kgr)
            dst = yb[:, ti, hloc * 64:(hloc + 1) * 64]
            kd_ = f"yb{ti}_{hloc}"
            if first:
                tsc("dve", dst, psv[:, 0:64], gr, None, ALU.mult, None, [kpsv, kgr], kd_)
            else:
                stt("dve", dst, psv[:, 0:64], gr, dst, ALU.mult, ALU.add, [kpsv, kgr, kd_], kd_)
            return rd, krd

        for g in range(4):
            for pr in range(2):
                ld(q2[:, pr, :], s_q[g * 256 + pr * 128:g * 256 + (pr + 1) * 128, :], k_q2)
                for r2_ in range(2):
                    hl_ = pr * 2 + r2_
                    ld(QB[hl_][0][64 * r2_:64 * r2_ + 64, :], s_qr[(4 * g + hl_) * 64:(4 * g + hl_ + 1) * 64, :], QB[hl_][1] + "q")
            for hf_ in range(2):
                ld((KEa if hf_ == 0 else KEb)[hf_ * 64:(hf_ + 1) * 64, :], s_ks[g * 64:(g + 1) * 64, :], (k_KEa if hf_ == 0 else k_KEb) + "k")
                ld(kwT2[hf_ * 64:(hf_ + 1) * 64, :], s_kw[g * 64:(g + 1) * 64, :], k_kwT)
            for q4 in range(4):
                ld(VsAug[:, q4 * 8:(q4 + 1) * 8, 0:64], s_vs[q4 * 1024:(q4 + 1) * 1024, g * 64:(g + 1) * 64].rearrange("(kt p) d -> p kt d", p=128), k_vsa)
                ld(VwAug[:, q4 * 8:(q4 + 1) * 8, 0:64], s_vw[q4 * 1024:(q4 + 1) * 1024, g * 64:(g + 1) * 64].rearrange("(kt p) d -> p kt d", p=128), k_vwa)
            cp("dve", VsAug[:, :, 64], kval_t, [k_kval], k_vsa)
            cp("dve", VwAug[:, :, 64], kval_t, [k_kval], k_vwa)
            if stop_after == "D1L":
                P.emit(nc)
                return nc
            units = [(hloc, tt_, nc_) for hloc in range(4) for tt_ in range(4) for nc_ in range(2)]

            def c_scores(ui):
                hloc, tt_, nc_ = units[ui]
                pr, r2 = hloc // 2, hloc % 2
                pb = 64 * r2
                tsl = slice(tt_ * 512, (tt_ + 1) * 512)
                ps, kps = PSF[ui % 2], f"psf{ui % 2}"
                mm(ps[:, :], kcmpT2[pb:pb + 64, g, nc_ * 128:(nc_ + 1) * 128], q2[pb:pb + 64, pr, tsl], True, False,
                   [k_kcmp, k_kcmp + "hi", k_q2], kps)
                mm(ps[:, :], ident[:], cmpb[:, nc_, tsl], False, True, ["ident", k_cmpb], kps)

            def c_rest(ui):
                hloc, tt_, nc_ = units[ui]
                h = 4 * g + hloc
                ps, kps = PSF[ui % 2], f"psf{ui % 2}"
                cb = 4 if ((ui // 2) % 2 == 0) else 2
                psC = [PSF[cb], PSF[cb + 1]]
                pT, kpT = pTs[ui % 2]
                act(pT, ps[:, :], AF.Exp, [kps], [f"{kpT}_{x}" for x in range(4)])
                for ts in range(4):
                    bank = psC[ts // 2]
                    mm(bank[:, (ts % 2) * 129:(ts % 2) * 129 + 129], pT[:, ts * 128:(ts + 1) * 128], VcAug[:, g, nc_, :],
                       nc_ == 0 and ts % 2 == 0, nc_ == 1, [f"{kpT}_{ts}", k_vca], f"psf{cb + ts // 2}", skip=True)
                if nc_ == 1:
                    for ts in range(4):
                        ti = tt_ * 4 + ts
                        kb = f"psf{cb + ts // 2}"
                        psv = psC[ts // 2][:, (ts % 2) * 129:(ts % 2) * 129 + 129]
                        rd, krd = epilogue(psv, kb, ti, hloc, h, 0, True)
                        ki = f"imp{ti}"
                        if hloc == 0:
                            tsc("dve", imp[:, ti, :], psv[:, 65:129], rd, None, ALU.mult, None, [kb, krd], ki)
                        else:
                            stt("dve", imp[:, ti, :], psv[:, 65:129], rd, imp[:, ti, :], ALU.mult, ALU.add, [kb, krd, ki], ki)

            c_scores(0)
            for ui in range(len(units)):
                if ui + 1 < len(units):
                    c_scores(ui + 1)
                c_rest(ui)
            if stop_after == "D1a":
                P.emit(nc)
                return nc
            def topk_gen():
                for ti in range(16):
                    biasTok, k_btok = biasToks[ti % 2]
                    ki = f"imp{ti}"
                    tt("dve", tk_a, imp[:, ti, :], selA_t[:, ti, :], ALU.mult, [ki, k_selA], k_tka)
                    tt("dve", tk_a, tk_a, selB_t[:, ti, :], ALU.add, [k_tka, k_selB], k_tka)
                    P.op("dve", lambda e: e.max(out=m8a, in_=tk_a), reads=[k_tka], writes=[k_m8a])
                    P.op("dve", lambda e: e.match_replace(out=tk_b, in_to_replace=m8a, in_values=tk_a, imm_value=-1e30), reads=[k_tka, k_m8a], writes=[k_tkb])
                    P.op("dve", lambda e: e.max(out=m8b, in_=tk_b), reads=[k_tkb], writes=[k_m8b])
                    P.op("dve", lambda e: e.match_replace(out=tk_c, in_to_replace=m8b, in_values=tk_b, imm_value=-1e30), reads=[k_tkb, k_m8b], writes=[k_tkc])
                    tt("dve", tk_c, tk_c, tk_a, ALU.not_equal, [k_tkc, k_tka], k_tkc)
                    tsc("dve", biasTok[:, 0:64], tk_c, -NEGB, NEGB, ALU.mult, ALU.add, [k_tkc], k_btok)
                    tsc("dve", biasTok[:, 64:128], tk_c, -NEGB, NEGB, ALU.mult, ALU.add, [k_tkc], k_btok)
                    yield
                    pbk = PSB[ti % 2]
                    tr(pbk[:, 0:128], biasTok[:, 0:128], [k_btok], f"psb{ti % 2}")
                    for hl_ in range(4):
                        bo = 64 if hl_ % 2 == 0 else 0
                        cp("act" if hl_ % 2 else "dve", QB[hl_][0][bo:bo + 64, ti * 128:(ti + 1) * 128], pbk[bo:bo + 64, 0:128],
                           [f"psb{ti % 2}"], QB[hl_][1] + f"b{ti}")
            def attn_gen(branches):
                SB = [(PSF[0], "psf0"), (PSF[1], "psf1"), (PSF[4], "psf4"), (PSF[5], "psf5")]
                LOOK = 3
                tiles = []
                gi_ = 0
                for hloc in range(4):
                    for branch in branches:
                        for tt_ in range(4):
                            kt0 = 16 + 4 * tt_
                            kts = list(range(0, kt0 + 4)) if branch == 1 else list(range(kt0 - 4, kt0 + 4))
                            grp = dict(hloc=hloc, branch=branch, tt_=tt_, kt0=kt0, started=[False] * 4, obi=gi_ % 2)
                            gi_ += 1
                            for ii, kt in enumerate(kts):
                                kk = kt - kt0
                                ts_lo = max(0, kk)
                                ts_hi = 3 if branch == 1 else min(3, kk + 4)
                                tiles.append(dict(g=grp, kt=kt, kk=kk, ts_lo=ts_lo, ts_hi=ts_hi, last=(ii == len(kts) - 1)))

                def scores(j):
                    t_ = tiles[j]; gr = t_["g"]
                    hloc, branch, tt_ = gr["hloc"], gr["branch"], gr["tt_"]
                    r2 = hloc % 2
                    pb = 64 * r2
                    qb_t, kqb = QB[hloc]
                    kt = t_["kt"]
                    ksl = slice(kt * 128, (kt + 1) * 128)
                    c0, c1 = t_["ts_lo"] * 128, (t_["ts_hi"] + 1) * 128
                    tsl = slice(tt_ * 512 + c0, tt_ * 512 + c1)
                    ps, kps = SB[j % 4]
                    if branch == 1:
                        KE, k_KE = (KEa, k_KEa) if r2 == 0 else (KEb, k_KEb)
                        mm(ps[:, c0:c1], KE[:, ksl], qb_t[:, tsl], True, True,
                           [k_KE + "E", k_KE + "k", kqb + "q"] + [kqb + f"b{tt_ * 4 + x}" for x in range(4)], kps)
                    else:
                        mm(ps[:, c0:c1], kwT2[pb:pb + 64, ksl], qb_t[pb:pb + 64, tsl], True, True, [k_kwT, kqb + "q"], kps)

                def rest(j):
                    t_ = tiles[j]; gr = t_["g"]
                    hloc, branch, tt_, kt0 = gr["hloc"], gr["branch"], gr["tt_"], gr["kt0"]
                    h = 4 * g + hloc
                    VA, k_VA = (VsAug, k_vsa) if branch == 1 else (VwAug, k_vwa)
                    psO, kpsO = PSF[2 + gr["obi"]], f"psf{2 + gr['obi']}"
                    started = gr["started"]
                    kt, kk = t_["kt"], t_["kk"]
                    c0, c1 = t_["ts_lo"] * 128, (t_["ts_hi"] + 1) * 128
                    ps, kps = SB[j % 4]
                    pT, kpT = pTs[j % 4]
                    act(pT[:, c0:c1], ps[:, c0:c1], AF.Exp, [kps], [f"{kpT}_{x}" for x in range(t_["ts_lo"], t_["ts_hi"] + 1)])
                    for ts in range(t_["ts_lo"], t_["ts_hi"] + 1):
                        Dd = ts - kk
                        sub = pT[:, ts * 128:(ts + 1) * 128]
                        ksub = f"{kpT}_{ts}"
                        if Dd == 0:
                            tt(MASK_ENG, sub, sub, triL[:], ALU.mult, [ksub, "triL"], ksub)
                        elif branch == 2 and Dd == 4:
                            tt(MASK_ENG, sub, sub, triU[:], ALU.mult, [ksub, "triU"], ksub)
                        mm(psO[:, ts * 65:(ts + 1) * 65], sub, VA[:, kt, :], not any(started), kt == kt0 + ts, [ksub, k_VA], kpsO, skip=True)
                        started[ts] = True
                    if t_["last"]:
                        for ts in range(4):
                            epilogue(psO[:, ts * 65:(ts + 1) * 65], kpsO, tt_ * 4 + ts, hloc, h, branch, False)
                        return True
                    return False

                n = len(tiles)
                for j in range(min(LOOK, n)):
                    scores(j)
                for j in range(n):
                    if j + LOOK < n:
                        scores(j + LOOK)
                    if rest(j):
                        yield

            def rr(gens):
                gens = list(gens)
                while gens:
                    for g_ in list(gens):
                        try:
                            next(g_)
                        except StopIteration:
                            gens.remove(g_)

            rr([attn_gen((2,)), topk_gen()])
            rr([attn_gen((1,))])
            if stop_after == "D1c":
                P.emit(nc)
                return nc
            cp("act", ybb, yb, [f"yb{ti}_{hl}" for ti in range(16) for hl in range(4)], k_ybb)
            for q4 in range(2):
                st(s_yb[q4 * 1024:(q4 + 1) * 1024, g * 256:(g + 1) * 256].rearrange("(ti p) c -> p ti c", p=128), ybb[:, q4 * 8:(q4 + 1) * 8, :], k_ybb, "SCR_ALL")
        dump_scr("s_yb", s_yb)
        P.barrier()
        flush_dumps()

        if stop_after == "D":
            P.emit(nc)
            return nc
        FA.reset(); BA.reset()

        def load_weight_bf16(dst, kdst, w_dram, nrows_k, ncols, stg):
            wv = w_dram.rearrange("(k p) n -> p k n", p=128)
            i = 0
            for k0 in range(0, nrows_k, 8):
                kn = min(8, nrows_k - k0)
                for c0 in range(0, ncols, 512):
                    cn = min(512, ncols - c0)
                    st_, kst = stg[i % len(stg)]
                    i += 1
                    ld(st_[:, 0:kn, 0:cn], wv[:, k0:k0 + kn, c0:c0 + cn], kst)
                    cp("act" if i % 2 else "dve", dst[:, k0:k0 + kn, c0:c0 + cn], st_[:, 0:kn, 0:cn], [kst], kdst)

        stg = [FA.alloc([8, 512]) for _ in range(2)]
        Wa_b, k_Wa = BA.alloc([8, D]); Wb_b, k_Wb = BA.alloc([8, D]); Wo_b, k_Wo = BA.alloc([8, D])
        load_weight_bf16(Wa_b, k_Wa, w_a, 8, D, stg)
        load_weight_bf16(Wb_b, k_Wb, w_b, 8, D, stg)
        load_weight_bf16(Wo_b, k_Wo, w_o, 8, D, stg)
        npost_bc, k_npost = FA.alloc([D]); nmpre_bc, k_nmpre = FA.alloc([D])
        ld(npost_bc, n_post.to_broadcast([128, D]), k_npost)
        ld(nmpre_bc, n_mpre.to_broadcast([128, D]), k_nmpre)
        yaT, k_yaT = BA.alloc([8, 512]); ybT, k_ybT = BA.alloc([8, 512]); mT, k_mT = BA.alloc([8, 512])
        ytok = [BA.alloc([D]) for _ in range(2)]
        sgt_ = [BA.alloc([512]) for _ in range(4)]
        vn_b = [BA.alloc([D]) for _ in range(2)]
        vTs = [BA.alloc([8, 128]) for _ in range(2)]
        m1s = [FA.alloc([512]) for _ in range(2)]
        m2s = [FA.alloc([512]) for _ in range(2)]
        xts2 = [FA.alloc([D]) for _ in range(2)]
        h1s = [FA.alloc([D]) for _ in range(2)]
        trc = [0]

        def transpose_tile(src_tok, ksrc, dstT, kdst, col0):
            i = trc[0] % 2
            trc[0] += 1
            pb = PSB[i]
            for k in range(8):
                tr(pb[:, k * 128:(k + 1) * 128], src_tok[:, k * 128:(k + 1) * 128], [ksrc], f"psb{i}")
            cp("act" if i else "dve", dstT[:, :, col0:col0 + 128], pb[:].rearrange("p (k t) -> p k t", k=8), [f"psb{i}"], kdst)

        def rms2(ps_a, kpa, ps_b, kpb, slot):
            s0 = stat[:, slot:slot + 1]; s1 = stat[:, slot + 1:slot + 2]
            k0, k1 = f"stat{slot}", f"stat{slot + 1}"
            act(junk[:, 0:512], ps_a, AF.Square, [kpa], [k0, "junk"], accum=s0)
            act(junk[:, 512:1024], ps_b, AF.Square, [kpb], [k1, "junk2"], accum=s1)
            tt("dve", s0, s0, s1, ALU.add, [k0, k1], k0)
            act(s0, s0, AF.Sqrt, [k0, "epsT"], k0, bias=epsT[:, 0:1], scale=1.0 / 1024.0)
            P.op("dve", (lambda o_: (lambda e: e.reciprocal(out=o_, in_=o_)))(s0), reads=[k0], writes=[k0])
            return s0, k0

        mTs = [(mT, k_mT), FA.alloc_bf16([8, 512])]

        def e_front(tb):
            mT_, k_mT_ = mTs[tb % 2]
            for which, (srcd, dstT, kdT) in enumerate(((s_ya, yaT, k_yaT), (s_yb, ybT, k_ybT))):
                for ts in range(4):
                    yt, kyt = ytok[(which * 4 + ts) % 2]
                    r0 = tb * 512 + ts * 128
                    ld(yt, srcd[r0:r0 + 128, :], kyt)
                    transpose_tile(yt, kyt, dstT, kdT, ts * 128)
                yield
            for ct in range(8):
                pa, kpa = PSF[0 + (ct % 2) * 2], f"psf{0 + (ct % 2) * 2}"
                pbb, kpb = PSF[1 + (ct % 2) * 2], f"psf{1 + (ct % 2) * 2}"
                for k in range(8):
                    mm(pa[:, :], Wa_b[:, k, ct * 128:(ct + 1) * 128], yaT[:, k, :], k == 0, k == 7, [k_Wa, k_yaT], kpa)
                for k in range(8):
                    mm(pbb[:, :], Wb_b[:, k, ct * 128:(ct + 1) * 128], ybT[:, k, :], k == 0, k == 7, [k_Wb, k_ybT], kpb)
                ga_t, kga = sgt_[(ct % 2) * 2]; gb_t, kgb2 = sgt_[(ct % 2) * 2 + 1]
                ld(ga_t, s_ga[ct * 128:(ct + 1) * 128, tb * 512:(tb + 1) * 512], kga)
                ld(gb_t, s_gb[ct * 128:(ct + 1) * 128, tb * 512:(tb + 1) * 512], kgb2)
                m1, km1 = m1s[ct % 2]; m2, km2 = m2s[ct % 2]
                tt("dve", m1, pa[:, :], ga_t, ALU.mult, [kpa, kga], km1)
                tt("dve", m2, pbb[:, :], gb_t, ALU.mult, [kpb, kgb2], km2)
                tt("pool", mT_[:, ct, :], m1, m2, ALU.add, [km1, km2], k_mT_ + f"_{ct}")
                if ct % 2 == 1:
                    yield

        def e_back(tb):
            mT_, k_mT_ = mTs[tb % 2]
            for ts in range(4):
                ti = tb * 4 + ts
                za, kza = PSF[4], "psf4"
                zb, kzb = PSF[5], "psf5"
                for nh, (zz, kzz) in enumerate(((za, kza), (zb, kzb))):
                    for k in range(8):
                        mm(zz[:, :], mT_[:, k, ts * 128:(ts + 1) * 128], Wo_b[:, k, nh * 512:(nh + 1) * 512], k == 0, k == 7,
                           [k_mT_ + f"_{k}", k_Wo], kzz)
                if e_tails:
                    e_tails.pop(0)()
                rs, krs = rms2(za[:, :], kza, zb[:, :], kzb, 40 + (ti % 2) * 4)
                xt, kx = xts2[ti % 2]; h1, kh1 = h1s[ti % 2]
                ld(xt, xo[ti * 128:(ti + 1) * 128, :], kx)
                stt("dve", h1[:, 0:512], za[:, :], rs, npost_bc[:, 0:512], ALU.mult, ALU.mult, [kza, krs, k_npost], kh1)
                stt("dve", h1[:, 512:1024], zb[:, :], rs, npost_bc[:, 512:1024], ALU.mult, ALU.mult, [kzb, krs, k_npost], kh1)
                tt("pool", h1, h1, xt, ALU.add, [kh1, kx], kh1)
                st(s_h1[ti * 128:(ti + 1) * 128, :], h1, kh1, "SCR_ALL")
                slot = 42 + (ti % 2) * 4
                ss = stat[:, slot:slot + 1]; kss = f"stat{slot}"
                act(junk[:, :], h1, AF.Square, [kh1], [kss, "junk", "junk2"], accum=ss)
                act(ss, ss, AF.Sqrt, [kss, "epsT"], kss, bias=epsT[:, 0:1], scale=1.0 / 1024.0)
                P.op("dve", (lambda o_: (lambda e: e.reciprocal(out=o_, in_=o_)))(ss), reads=[kss], writes=[kss])
                vn, kvn = vn_b[ti % 2]
                stt("dve", vn, h1, ss, nmpre_bc, ALU.mult, ALU.mult, [kh1, kss, k_nmpre], kvn)
                def tail(ti=ti, vn=vn, kvn=kvn):
                    vT_t, kvT = vTs[ti % 2]
                    transpose_tile(vn, kvn, vT_t, kvT, 0)
                    st(s_vT[:, ti * 128:(ti + 1) * 128].rearrange("(k p) t -> p k t", p=128), vT_t, kvT, "SCR_ALL")
                e_tails.append(tail)
                yield

        def drain(gens):
            gens = list(gens)
            while gens:
                for g_ in list(gens):
                    try:
                        next(g_)
                    except StopIteration:
                        gens.remove(g_)

        e_tails = []
        drain([e_front(0)])
        for tb in range(4):
            gs = [e_back(tb)]
            if tb + 1 < 4:
                gs.insert(0, e_front(tb + 1))
            drain(gs)
        while e_tails:
            e_tails.pop(0)()
        dump_scr("s_h1", s_h1)
        P.barrier()
        flush_dumps()

        FA.reset(); BA.reset()
        wd_b, k_wd = BA.alloc([32, D])
        wus = [FA.alloc([8, 256]) for _ in range(2)]
        wub = [BA.alloc([8, 256]) for _ in range(2)]
        vT_blk, k_vTb = BA.alloc([8, 512])
        actT, k_actT = FA.alloc_bf16([32, 512])
        nmpost_bc, k_nmpost = FA.alloc([D])
        ld(nmpost_bc, n_mpost.to_broadcast([128, D]), k_nmpost)
        rts = [FA.alloc([512]) for _ in range(2)]
        h1s = [FA.alloc([D]) for _ in range(2)]
        h2s = [FA.alloc([D]) for _ in range(2)]
        w_up_v = w_up.rearrange("(k p) n -> p k n", p=128)
        for tb in range(4):
            ld(vT_blk, s_vT[:, tb * 512:(tb + 1) * 512].rearrange("(k p) t -> p k t", p=128), k_vTb)
            for cg in range(16):
                ws_, kws_ = wus[cg % 2]; wb_, kwb_ = wub[cg % 2]
                ld(ws_, w_up_v[:, :, cg * 256:(cg + 1) * 256], kws_)
                cp("act" if cg % 2 else "dve", wb_, ws_, [kws_], kwb_)
                for c2 in range(2):
                    fft = cg * 2 + c2
                    ps, kps = PSF[fft % 2], f"psf{fft % 2}"
                    for k in range(8):
                        mm(ps[:, :], wb_[:, k, c2 * 128:(c2 + 1) * 128], vT_blk[:, k, :], k == 0, k == 7, [kwb_, k_vTb], kps)
                    rt, krt = rts[fft % 2]
                    act(rt, ps[:, :], AF.Relu, [kps], krt)
                    tt("dve", actT[:, fft, :], rt, ps[:, :], ALU.mult, [krt, kps], k_actT + f"_{fft}")
            if tb == 0:
                wdv = w_down.rearrange("(k p) n -> p k n", p=128)
                ci = 0
                for k0 in range(0, 32, 8):
                    for c0 in range(0, D, 256):
                        ws_, kws_ = wus[ci % 2]
                        ld(ws_, wdv[:, k0:k0 + 8, c0:c0 + 256], kws_)
                        cp("act" if ci % 2 else "dve", wd_b[:, k0:k0 + 8, c0:c0 + 256], ws_, [kws_], k_wd)
                        ci += 1
            for ts in range(4):
                ti = tb * 4 + ts
                za, kza = PSF[2 + (ti % 2) * 2], f"psf{2 + (ti % 2) * 2}"
                zb, kzb = PSF[3 + (ti % 2) * 2], f"psf{3 + (ti % 2) * 2}"
                for nh, (zz, kzz) in enumerate(((za, kza), (zb, kzb))):
                    for fft in range(32):
                        mm(zz[:, :], actT[:, fft, ts * 128:(ts + 1) * 128], wd_b[:, fft, nh * 512:(nh + 1) * 512], fft == 0, fft == 31,
                           [k_actT + f"_{fft}", k_wd], kzz)
                rs, krs = rms2(za[:, :], kza, zb[:, :], kzb, 48 + (ti % 2) * 4)
                h1, kh1 = h1s[ti % 2]; h2, kh2 = h2s[ti % 2]
                ld(h1, s_h1[ti * 128:(ti + 1) * 128, :], kh1)
                stt("dve", h2[:, 0:512], za[:, :], rs, nmpost_bc[:, 0:512], ALU.mult, ALU.mult, [kza, krs, k_nmpost], kh2)
                stt("dve", h2[:, 512:1024], zb[:, :], rs, nmpost_bc[:, 512:1024], ALU.mult, ALU.mult, [kzb, krs, k_nmpost], kh2)
                tt("pool", h2, h2, h1, ALU.add, [kh2, kh1], kh2)
                st(s_h2[ti * 128:(ti + 1) * 128, :], h2, kh2, "SCR_ALL")
        dump_scr("s_h2", s_h2)
        P.barrier()
        flush_dumps()

        FA.reset(); BA.reset()
        stg = [FA.alloc([8, 512]) for _ in range(2)]
        Wpg_b, k_Wpg = BA.alloc([8, D]); Wple_b, k_Wple = BA.alloc([2, D])
        load_weight_bf16(Wpg_b, k_Wpg, w_pg, 8, D, stg)
        load_weight_bf16(Wple_b, k_Wple, w_ple, 2, D, stg)
        nple_bc, k_nple = FA.alloc([D])
        ld(nple_bc, n_ple.to_broadcast([128, D]), k_nple)
        NB3 = 3
        h2s = [FA.alloc([D]) for _ in range(NB3)]
        h2bs = [BA.alloc([D]) for _ in range(NB3)]
        h2Ts = [BA.alloc([8, 128]) for _ in range(NB3)]
        pfs = [FA.alloc([256]) for _ in range(NB3)]
        pbs = [BA.alloc([256]) for _ in range(NB3)]
        pTs2 = [BA.alloc([2, 128]) for _ in range(NB3)]
        sgs = [FA.alloc([D]) for _ in range(NB3)]
        es_ = [FA.alloc([D]) for _ in range(NB3)]
        outs = [FA.alloc([D]) for _ in range(NB3)]

        def g_front(ti):
            b3 = ti % NB3
            h2, kh2 = h2s[b3]; h2b, kh2b = h2bs[b3]; h2T, kh2T = h2Ts[b3]
            ld(h2, s_h2[ti * 128:(ti + 1) * 128, :], kh2)
            cp("pool", h2b, h2, [kh2], kh2b)
            transpose_tile(h2b, kh2b, h2T, kh2T, 0)
            pf, kpf = pfs[b3]; pb_, kpb_ = pbs[b3]; pT_, kpT_ = pTs2[b3]
            ld(pf, po[ti * 128:(ti + 1) * 128, :], kpf)
            cp("pool", pb_, pf, [kpf], kpb_)
            i2 = trc[0] % 2
            trc[0] += 1
            pbk = PSB[i2]
            for k in range(2):
                tr(pbk[:, k * 128:(k + 1) * 128], pb_[:, k * 128:(k + 1) * 128], [kpb_], f"psb{i2}")
            cp("dve", pT_, pbk[:, 0:256].rearrange("p (k t) -> p k t", k=2), [f"psb{i2}"], kpT_)

        def g_banks(ti, nh):
            u = (2 * ti + nh) % 3
            return (PSF[2 * u], f"psf{2 * u}"), (PSF[2 * u + 1], f"psf{2 * u + 1}")

        def g_ufront(ti, nh):
            b3 = ti % NB3
            h2T, kh2T = h2Ts[b3]; pT_, kpT_ = pTs2[b3]
            (gp, kgp), (ep, kep) = g_banks(ti, nh)
            for k in range(8):
                mm(gp[:, :], h2T[:, k, :], Wpg_b[:, k, nh * 512:(nh + 1) * 512], k == 0, k == 7, [kh2T, k_Wpg], kgp)
            for k in range(2):
                mm(ep[:, :], pT_[:, k, :], Wple_b[:, k, nh * 512:(nh + 1) * 512], k == 0, k == 1, [kpT_, k_Wple], kep)

        def g_uback(ti, nh):
            b3 = ti % NB3
            sg, ksg = sgs[b3]; ee, kee = es_[b3]
            (gp, kgp), (ep, kep) = g_banks(ti, nh)
            act(sg[:, nh * 512:(nh + 1) * 512], gp[:, :], AF.Sigmoid, [kgp], ksg + f"_{nh}")
            tt("dve", ee[:, nh * 512:(nh + 1) * 512], sg[:, nh * 512:(nh + 1) * 512], ep[:, :], ALU.mult, [ksg + f"_{nh}", kep], kee + f"_{nh}")

        def g_tback(ti):
            b3 = ti % NB3
            h2, kh2 = h2s[b3]; ee, kee = es_[b3]; oo, koo = outs[b3]
            slot = 56 + b3 * 2
            ss = stat[:, slot:slot + 1]; kss = f"stat{slot}"
            act(junk[:, :], ee, AF.Square, [kee + "_0", kee + "_1"], [kss, "junk", "junk2"], accum=ss)
            act(ss, ss, AF.Sqrt, [kss, "epsT"], kss, bias=epsT[:, 0:1], scale=1.0 / 1024.0)
            P.op("dve", (lambda o_: (lambda e: e.reciprocal(out=o_, in_=o_)))(ss), reads=[kss], writes=[kss])
            stt("dve", oo, ee, ss, nple_bc, ALU.mult, ALU.mult, [kee + "_0", kee + "_1", kss, k_nple], koo)
            tt("pool", oo, oo, h2, ALU.add, [koo, kh2], koo)
            st(out[ti * 128:(ti + 1) * 128, :], oo, koo, "SCR_ALL", q="sp", is_output=True)

        NT = 16
        g_front(0); g_ufront(0, 0); g_ufront(0, 1); g_front(1)
        for ti in range(NT):
            if ti + 1 < NT:
                g_ufront(ti + 1, 0)
            g_uback(ti, 0)
            if ti + 1 < NT:
                g_ufront(ti + 1, 1)
            g_uback(ti, 1)
            if ti + 2 < NT:
                g_front(ti + 2)
            g_tback(ti)
        P.emit(nc)
    return nc


def _consts(j):
    c = {}
    c["c_ident"] = np.eye(128, dtype=np.float32).astype(NPBF)
    pm = np.zeros((128, 128), np.float32)
    for h in range(2):
        for d in range(64):
            partner = d + 8 if d < 8 else (d - 8 if d < 16 else d)
            pm[h * 64 + partner, h * 64 + d] = 1.0
    c["c_perm"] = pm.astype(NPBF)
    half = 8
    inv = 500000.0 ** (-np.arange(half) * 2.0 / 16)
    ang = np.arange(T, dtype=np.float64)[None, :] * inv[:, None]
    cos = np.ones((64, T), np.float32); sin = np.zeros((64, T), np.float32)
    cos[0:8] = np.cos(ang); cos[8:16] = np.cos(ang)
    sin[0:8] = -np.sin(ang); sin[8:16] = np.sin(ang)
    c["c_cos"] = np.concatenate([cos, cos], 0)
    c["c_sin"] = np.concatenate([sin, sin], 0)
    E = np.zeros((64, T), np.float32)
    E[np.arange(T) // 64, np.arange(T)] = 1.0
    c["c_E"] = E.astype(NPBF)
    kk = np.arange(128)[:, None]; tt_ = np.arange(128)[None, :]
    c["c_triL"] = (tt_ >= kk).astype(np.float32).astype(NPBF)
    c["c_triU"] = (tt_ < kk).astype(np.float32).astype(NPBF)
    n = np.arange(256)
    sel_start = np.arange(64) * 64
    ovl = ((n[:, None] * 16 < sel_start[None, :] + 64) & (n[:, None] * 16 + 31 >= sel_start[None, :])).astype(np.float32)
    ovl[255] = 0
    c["c_ovl"] = ovl
    t = HALF + np.arange(HALF)
    nmin = 0 if j == 1 else 128
    valid = (n[:, None] * 16 + 31 <= t[None, :]) & (n[:, None] >= nmin) & (n[:, None] < 255)
    c["c_cmpb"] = np.where(valid, 0.0, NEGB).astype(np.float32).reshape(2, 128, HALF).astype(NPBF)
    bmin = 0 if j == 1 else 32
    b = np.arange(64)[None, :]
    cur = (t // 64)[:, None]
    val = (b >= bmin) & (b <= cur)
    forced = val & ((b == bmin) | (b == cur) | (b == cur - 1))
    c["c_selA"] = (val & ~forced).astype(np.float32)
    c["c_selB"] = np.where(forced, 1e9 + 1000.0 * b, np.where(val, 0.0, -1e9)).astype(np.float32)
    kv = np.ones(T, np.float32)
    if j == 0:
        kv[:HALF] = 0
    c["c_kval"] = np.ascontiguousarray(kv.reshape(32, 128).T)
    return c


def make_in_maps(inputs):
    x = np.asarray(inputs["x"], np.float32)
    p = np.asarray(inputs["p"], np.float32)
    shared = {
        "w_in": np.ascontiguousarray(inputs["w_in"][0]),
        "w_a": np.ascontiguousarray(inputs["w_branch_a"][0]), "w_b": np.ascontiguousarray(inputs["w_branch_b"][0]),
        "w_o": np.ascontiguousarray(inputs["w_out"][0]),
        "w_up": np.ascontiguousarray(inputs["w_up"][0]), "w_down": np.ascontiguousarray(inputs["w_down"][0]),
        "w_ple": np.ascontiguousarray(inputs["w_ple"][0]), "w_pg": np.ascontiguousarray(inputs["w_ple_gate"][0]),
        "n_pre": np.ascontiguousarray(inputs["norm_pre_mix"]), "n_post": np.ascontiguousarray(inputs["norm_post_mix"]),
        "n_mpre": np.ascontiguousarray(inputs["norm_pre_mlp"]), "n_mpost": np.ascontiguousarray(inputs["norm_post_mlp"]),
        "n_ple": np.ascontiguousarray(inputs["norm_ple"]),
        "lbl": np.ascontiguousarray(inputs["hg_lb_logits"]), "gnw": np.ascontiguousarray(inputs["hg_gnorm"]),
        "pe_k": np.ascontiguousarray(inputs["cmp_pe_k"][0]), "pe_v": np.ascontiguousarray(inputs["cmp_pe_v"][0]),
        "wk1": np.ascontiguousarray(inputs["cmp_wk1"][0]), "wk2": np.ascontiguousarray(inputs["cmp_wk2"][0]),
        "wv1": np.ascontiguousarray(inputs["cmp_wv1"][0]), "wv2": np.ascontiguousarray(inputs["cmp_wv2"][0]),
    }
    shared = {k: np.asarray(v, np.float32) for k, v in shared.items()}
    cj = [_consts(0), _consts(1)]
    maps = []
    for c in range(8):
        b, j = c // 2, c % 2
        m = dict(shared)
        m.update(cj[j])
        m["xo"] = np.ascontiguousarray(x[b, j * HALF:(j + 1) * HALF])
        m["xc"] = np.ascontiguousarray(x[b, 0:HALF]) if j == 1 else np.zeros((HALF, D), np.float32)
        m["po"] = np.ascontiguousarray(p[0, b, j * HALF:(j + 1) * HALF])
        maps.append(m)
    return maps


def kernel(**inputs):
    nc = build()
    maps = make_in_maps(inputs)
    res = run_bass_kernel_spmd(nc, maps, core_ids=list(range(8)))
    outp = np.zeros((4, T, D), np.float32)
    for c in range(8):
        b, j = c // 2, c % 2
        outp[b, j * HALF:(j + 1) * HALF] = res.results[c]["out"]
    return outp
```

```python
import numpy as np
import ml_dtypes
from contextlib import ExitStack
import concourse.bass as bass
import concourse.mybir as mybir
from concourse.bass_utils import run_bass_kernel_spmd

F32 = mybir.dt.float32
BF16 = mybir.dt.bfloat16
AF = mybir.ActivationFunctionType
ALU = mybir.AluOpType
NPBF = ml_dtypes.bfloat16

NDMASEM = 8
import os
RELAX_OK = os.environ.get("KDBG_RELAX", "0") == "1"
QUEUES = ("sp", "act", "pool")
ENGS = ("pe", "act", "dve", "pool", "sp")


class Op:
    __slots__ = ("eng", "fn", "deps", "marked", "cnt", "is_dma", "dsem", "dval", "relax")

    def __init__(self, eng, fn, is_dma=False):
        self.relax = False
        self.eng = eng
        self.fn = fn
        self.deps = []
        self.marked = False
        self.cnt = None
        self.is_dma = is_dma
        self.dsem = None
        self.dval = None


class Prog:
    def __init__(self):
        self.ops = {e: [] for e in ENGS}
        self.last_w = {}
        self.readers = {}
        self.dma_n = {q: 0 for q in QUEUES}
        self.dma_hist = {q: [] for q in QUEUES}
        self.out_dmas = []
        self.bar_deps = []
        self.bar_pending = set()

    def _add_dep(self, op, d):
        if d is None or d is op:
            return
        if (not d.is_dma) and d.eng == op.eng and op.eng == "pe" and not op.is_dma:
            return
        if op.relax and (not d.is_dma) and (not op.is_dma) and d.eng == op.eng:
            return
        if d not in op.deps:
            op.deps.append(d)
            d.marked = True

    def _track(self, op, reads, writes):
        if op.eng in self.bar_pending:
            self.bar_pending.discard(op.eng)
            for d in self.bar_deps:
                self._add_dep(op, d)
        for k in reads:
            self._add_dep(op, self.last_w.get(k))
        for k in writes:
            self._add_dep(op, self.last_w.get(k))
            for r in self.readers.get(k, ()):
                self._add_dep(op, r)
        for k in reads:
            lst = self.readers.setdefault(k, [])
            if not op.is_dma:
                for i in range(len(lst) - 1, -1, -1):
                    if (not lst[i].is_dma) and lst[i].eng == op.eng:
                        del lst[i]
            lst.append(op)
        for k in writes:
            self.last_w[k] = op
            self.readers[k] = []

    def op(self, eng, fn, reads=(), writes=(), relax=False):
        o = Op(eng, fn)
        o.relax = relax and RELAX_OK
        self._track(o, reads, writes)
        self.ops[eng].append(o)
        return o

    def dma(self, q, fn, reads=(), writes=(), is_output=False):
        o = Op(q, fn, is_dma=True)
        n = self.dma_n[q]
        self.dma_n[q] = n + 1
        o.dsem = (q, n % NDMASEM)
        o.dval = 16 * (n // NDMASEM + 1)
        hist = self.dma_hist[q]
        if n >= NDMASEM:
            o.deps.append(hist[n - NDMASEM])
        hist.append(o)
        self._track(o, reads, writes)
        self.ops[q].append(o)
        if is_output:
            self.out_dmas.append(o)
        return o

    def barrier(self):
        deps = []
        for e in ENGS:
            for o in reversed(self.ops[e]):
                if not o.is_dma:
                    deps.append(o)
                    break
        for q in QUEUES:
            deps.extend(self.dma_hist[q][-NDMASEM:])
        self.bar_deps = deps
        self.bar_pending = set(ENGS)
        self.last_w = {}
        self.readers = {}

    def emit(self, nc):
        EPOCH = 4000
        nep = {}
        for e in ENGS:
            c = 0
            for o in self.ops[e]:
                if not o.is_dma and o.marked:
                    c += 1
                    o.cnt = c
            nep[e] = (c + EPOCH - 1) // EPOCH
        with ExitStack() as es:
            csem = {}
            for e in ENGS:
                for ep in range(nep[e]):
                    csem[(e, ep)] = es.enter_context(nc.semaphore(f"s_{e}{ep}"))
            dsem = {}
            for q in QUEUES:
                if self.dma_n[q] == 0:
                    continue
                for i in range(NDMASEM):
                    dsem[(q, i)] = es.enter_context(nc.semaphore(f"d_{q}{i}"))
            block = es.enter_context(nc.Block())
            final = list(self.out_dmas)

            def run(ename, eng):
                waited = {}
                for o in self.ops[ename]:
                    need = {}
                    for d in o.deps:
                        if d.is_dma:
                            key, val, sem = d.dsem, d.dval, dsem[d.dsem]
                            if waited.get(key, 0) >= val:
                                continue
                            if key not in need or need[key][1] < val:
                                need[key] = (sem, val, val)
                        else:
                            ep, v = (d.cnt - 1) // EPOCH, (d.cnt - 1) % EPOCH + 1
                            if waited.get(d.eng, (-1, 0)) >= (ep, v):
                                continue
                            if d.eng not in need or need[d.eng][2] < (ep, v):
                                need[d.eng] = (csem[(d.eng, ep)], v, (ep, v))
                    items = list(need.items())
                    for key, (sem, val, rec) in items[:-1]:
                        eng.wait_ge(sem, val)
                        waited[key] = rec
                    ins = o.fn(eng)
                    if items:
                        key, (sem, val, rec) = items[-1]
                        ins._wait_ge(sem, val)
                        waited[key] = rec
                    if o.is_dma:
                        ins.then_inc(dsem[o.dsem], 16)
                    elif o.marked:
                        ins.then_inc(csem[(ename, (o.cnt - 1) // EPOCH)], 1)
                if ename == "sp":
                    for d in final:
                        if waited.get(d.dsem, 0) >= d.dval:
                            continue
                        eng.wait_ge(dsem[d.dsem], d.dval)
                        waited[d.dsem] = d.dval

            block.tensor(lambda eng: run("pe", eng))
            block.scalar(lambda eng: run("act", eng))
            block.vector(lambda eng: run("dve", eng))
            block.gpsimd(lambda eng: run("pool", eng))
            block.sync(lambda eng: run("sp", eng))


class Arena:
    def __init__(self, tile, size, name):
        self.tile = tile
        self.size = size
        self.off = 0
        self.name = name
        self.gen = 0

    def reset(self):
        self.off = 0
        self.gen += 1

    def alloc_bf16(self, shape):
        n = int(np.prod(shape))
        assert n % 2 == 0
        nf = n // 2
        assert self.off + nf <= self.size, (self.name, self.off, nf, self.size)
        ap = self.tile[:, self.off:self.off + nf].bitcast(BF16)
        key = f"{self.name}{self.gen}_{self.off}b"
        self.off += nf
        if len(shape) == 2:
            ap = ap.rearrange("p (a b) -> p a b", a=shape[0])
        elif len(shape) == 3:
            ap = ap.rearrange("p (a b c) -> p a b c", a=shape[0], b=shape[1])
        return ap, key

    def alloc(self, shape):
        n = int(np.prod(shape))
        assert self.off + n <= self.size, (self.name, self.off, n, self.size)
        ap = self.tile[:, self.off:self.off + n]
        key = f"{self.name}{self.gen}_{self.off}"
        self.off += n
        if len(shape) == 2:
            ap = ap.rearrange("p (a b) -> p a b", a=shape[0])
        elif len(shape) == 3:
            ap = ap.rearrange("p (a b c) -> p a b c", a=shape[0], b=shape[1])
        return ap, key


T = 4096
HALF = 2048
D = 1024
C_HQ, C_HF, C_HI, C_HG, C_NQ = 0, 1024, 2048, 3072, 4096
C_KC, C_VC, C_KS, C_VS, C_KW, C_VW = 5120, 5376, 5632, 5888, 6144, 6400
C_NG, C_GA, C_GB, N_IN = 6656, 6704, 7728, 8752
EPS = 1e-6
NEGB = -30000.0

import os
MASK_ENG = os.environ.get("KDBG_MASKENG", "pool")
DBG_BRANCHES = tuple(int(c) for c in os.environ.get("KDBG_BRANCHES", "12"))
DBG_NOBIAS = os.environ.get("KDBG_NOBIAS", "0") == "1"
DBG_NOEPI = os.environ.get("KDBG_NOEPI", "0") == "1"
DBG_NOPV = os.environ.get("KDBG_NOPV", "0") == "1"
KD_BCAST = os.environ.get("KDBG_KDBCAST", "1") == "1"
STORE_QUEUES = tuple(x for x in os.environ.get("KDBG_STOREQ", "act").split(",") if x)
FSZ = 22 * 1024
BSZ = 46 * 1024


def build(debug=None, stop_after=None):
    nc = bass.Bass("TRN2", target_bir_lowering=False)
    P = Prog()

    def din(name, shape, dt=F32):
        return nc.dram_tensor(name, list(shape), dt, kind="ExternalInput").ap()

    def dscr(name, shape, dt=BF16):
        return nc.dram_tensor(name, list(shape), dt, kind="Internal").ap()

    xo = din("xo", [HALF, D]); xc = din("xc", [HALF, D]); po = din("po", [HALF, 256])
    w_in = din("w_in", [D, N_IN])
    w_a = din("w_a", [D, D]); w_b = din("w_b", [D, D]); w_o = din("w_o", [D, D])
    w_up = din("w_up", [D, 4096]); w_down = din("w_down", [4096, D])
    w_ple = din("w_ple", [256, D]); w_pg = din("w_pg", [D, D])
    n_pre = din("n_pre", [1, D]); n_post = din("n_post", [1, D]); n_mpre = din("n_mpre", [1, D])
    n_mpost = din("n_mpost", [1, D]); n_ple = din("n_ple", [1, D])
    lbl = din("lbl", [2, D]); gnw = din("gnw", [1, 128])
    pe_k = din("pe_k", [32, 64]); pe_v = din("pe_v", [32, 64])
    wk1 = din("wk1", [2048, 256]); wk2 = din("wk2", [256, 64])
    wv1 = din("wv1", [2048, 256]); wv2 = din("wv2", [256, 64])
    c_ident = din("c_ident", [128, 128], BF16)
    c_perm = din("c_perm", [128, 128], BF16)
    c_cos = din("c_cos", [128, T]); c_sin = din("c_sin", [128, T])
    c_E = din("c_E", [64, T], BF16)
    c_triL = din("c_triL", [128, 128], BF16); c_triU = din("c_triU", [128, 128], BF16)
    c_ovl = din("c_ovl", [256, 64])
    c_cmpb = din("c_cmpb", [2, 128, HALF], BF16)
    c_selA = din("c_selA", [HALF, 64]); c_selB = din("c_selB", [HALF, 64])
    c_kval = din("c_kval", [128, 32])
    out = nc.dram_tensor("out", [HALF, D], F32, kind="ExternalOutput").ap()

    s_qh = dscr("s_qh", [D, HALF]); s_kh = dscr("s_kh", [D, HALF])
    s_kd = dscr("s_kd", [T, D]); s_v = dscr("s_v", [T, D]); s_g = dscr("s_g", [HALF, D], F32)
    s_q = dscr("s_q", [D, HALF]); s_qr = dscr("s_qr", [D, HALF])
    s_kc = dscr("s_kc", [256, T]); s_vc = dscr("s_vc", [256, T])
    s_ks = dscr("s_ks", [256, T]); s_kw = dscr("s_kw", [256, T])
    s_vs = dscr("s_vs", [T, 256]); s_vw = dscr("s_vw", [T, 256])
    s_ga = dscr("s_ga", [D, HALF]); s_gb = dscr("s_gb", [D, HALF])
    s_ya = dscr("s_ya", [HALF, D]); s_yb = dscr("s_yb", [HALF, D])
    s_h1 = dscr("s_h1", [HALF, D], F32); s_h2 = dscr("s_h2", [HALF, D], F32); s_vT = dscr("s_vT", [D, HALF])

    dbg = {}
    if debug:
        for name, spec in debug.items():
            shape, dts = spec
            dbg[name] = nc.dram_tensor("dbg_" + name, list(shape), BF16 if dts == "bf16" else F32, kind="ExternalOutput").ap()

    es = ExitStack()
    with es:
        sbt = lambda name, shape, dt: es.enter_context(nc.sbuf_tensor(name, shape, dt))
        pst = lambda name, shape, dt: es.enter_context(nc.psum_tensor(name, shape, dt))
        fa_t = sbt("arenaF", [128, FSZ], F32)
        ba_t = sbt("arenaB", [128, BSZ], BF16)
        FA = Arena(fa_t, FSZ, "F")
        BA = Arena(ba_t, BSZ, "B")
        ident = sbt("ident", [128, 128], BF16)
        perm = sbt("perm", [128, 128], BF16)
        triL = sbt("triL", [128, 128], BF16)
        triU = sbt("triU", [128, 128], BF16)
        lb = sbt("lb", [128, 8], F32)
        omlb = sbt("omlb", [128, 8], F32)
        lb2 = sbt("lb2", [128, 8], F32)
        PL = sbt("PL", [128, 8, 64], F32)
        gates = sbt("gates", [128, 16, 48], F32)
        zeros = sbt("zeros", [128, 64], F32)
        epsT = sbt("epsT", [128, 1], F32)
        tinyT = sbt("tinyT", [128, 1], F32)
        onesT = sbt("onesT", [128, 1], F32)
        junk = sbt("junk", [128, 1024], BF16)
        stat = sbt("stat", [128, 64], F32)
        PSF = [pst(f"psf{i}", [128, 512], F32) for i in range(6)]
        PSB = [pst(f"psb{i}", [128, 1024], BF16) for i in range(2)]

        def ld(dst, src, key, q="sp", reads=(), slow=False):
            if slow:
                return P.dma(q, lambda e: e.dma_start(out=dst, in_=src, allow_slow_non_contiguous=True), reads=list(reads), writes=[key])
            return P.dma(q, lambda e: e.dma_start(out=dst, in_=src), reads=list(reads), writes=[key])

        uq = [0]

        def st(dst, src, key, wkey, q=None, is_output=False):
            if wkey == "SCR_ALL":
                uq[0] += 1
                wkey = f"scr{uq[0]}"
            if q is None:
                lw = P.last_w.get(key)
                q = lw.eng if (lw is not None and not lw.is_dma and lw.eng in STORE_QUEUES) else "pool"
            return P.dma(q, lambda e: e.dma_start(out=dst, in_=src), reads=[key], writes=[wkey], is_output=is_output)

        def mm(o, lhsT, rhs, start, stop, reads, okey, skip=False):
            if skip:
                return P.op("pe", lambda e: e.matmul(o, lhsT=lhsT, rhs=rhs, start=start, stop=stop, skip_group_check=True), reads=reads, writes=[okey])
            return P.op("pe", lambda e: e.matmul(o, lhsT=lhsT, rhs=rhs, start=start, stop=stop), reads=reads, writes=[okey])

        def tr(o, in_, reads, okey):
            return P.op("pe", lambda e: e.transpose(out=o, in_=in_, identity=ident[:]), reads=list(reads) + ["ident"], writes=[okey])

        def act(o, in_, func, reads, okey, bias=None, scale=None, accum=None, eng="act"):
            kw = {}
            if bias is not None:
                kw["bias"] = bias
            if scale is not None:
                kw["scale"] = scale
            if accum is not None:
                kw["accum_out"] = accum
            wk = [okey] if isinstance(okey, str) else list(okey)
            return P.op("act", lambda e: e.activation(out=o, in_=in_, func=func, **kw), reads=reads, writes=wk)

        def cp(eng, o, in_, reads, okey):
            if eng == "act":
                return P.op("act", lambda e: e.copy(out=o, in_=in_), reads=reads, writes=[okey])
            return P.op(eng, lambda e: e.tensor_copy(out=o, in_=in_), reads=reads, writes=[okey])

        def tt(eng, o, a, b, op, reads, okey, relax=False):
            return P.op(eng, lambda e: e.tensor_tensor(out=o, in0=a, in1=b, op=op), reads=reads, writes=[okey], relax=relax)

        def tsc(eng, o, a, s1, s2, op0, op1, reads, okey, relax=False):
            if op1 is None:
                return P.op(eng, lambda e: e.tensor_scalar(out=o, in0=a, scalar1=s1, scalar2=None, op0=op0), reads=reads, writes=[okey], relax=relax)
            return P.op(eng, lambda e: e.tensor_scalar(out=o, in0=a, scalar1=s1, scalar2=s2, op0=op0, op1=op1), reads=reads, writes=[okey], relax=relax)

        def stt(eng, o, a, s, b, op0, op1, reads, okey, relax=False):
            return P.op(eng, lambda e: e.scalar_tensor_tensor(out=o, in0=a, scalar=s, in1=b, op0=op0, op1=op1), reads=reads, writes=[okey], relax=relax)

        def dbg_out(name, src_ap, key):
            if name in dbg:
                st(dbg[name], src_ap, key, "dbg_" + name, q="sp", is_output=True)

        ld(ident[:], c_ident, "ident"); ld(perm[:], c_perm, "perm")
        ld(triL[:], c_triL, "triL"); ld(triU[:], c_triU, "triU")
        P.op("pool", lambda e: e.memset(zeros[:], 0.0), writes=["zeros"])
        P.op("pool", lambda e: e.memset(epsT[:], EPS), writes=["epsT"])
        P.op("pool", lambda e: e.memset(tinyT[:], 1e-30), writes=["tinyT"])
        P.op("pool", lambda e: e.memset(onesT[:], 1.0), writes=["onesT"])
        ld(lb[:], lbl[0, :].rearrange("(h p) -> p h", p=128), "lb", slow=True)
        ld(lb2[:], lbl[1, :].rearrange("(h p) -> p h", p=128), "lb2", slow=True)
        tt("dve", lb[:], lb[:], lb2[:], ALU.subtract, ["lb", "lb2"], "lb")
        act(lb[:], lb[:], AF.Sigmoid, ["lb"], "lb")
        tsc("dve", omlb[:], lb[:], -1.0, 1.0, ALU.mult, ALU.add, ["lb"], "omlb")

        def rms_rstd(src, skey, slot, n=1024.0, eng_reads=()):
            ss = stat[:, slot:slot + 1]
            k = f"stat{slot}"
            act(junk[:, :src.shape[-1]] if len(src.shape) == 2 else junk[:], src, AF.Square, [skey] + list(eng_reads), [k, "junk"], accum=ss)
            act(ss, ss, AF.Sqrt, [k, "epsT"], k, bias=epsT[:, 0:1], scale=1.0 / n)
            P.op("dve", lambda e: e.reciprocal(out=ss, in_=ss), reads=[k], writes=[k])
            return ss, k

        uT, k_uT = BA.alloc([8, T])
        nbc, k_nbc = FA.alloc([D])
        ld(nbc, n_pre.to_broadcast([128, D]), k_nbc)
        xts = [FA.alloc([D]) for _ in range(2)]
        xns = [BA.alloc([D]) for _ in range(2)]
        for ti in range(32):
            src = xc if ti < 16 else xo
            r0 = (ti % 16) * 128
            xt, kx = xts[ti % 2]
            xn, kn = xns[ti % 2]
            ld(xt, src[r0:r0 + 128, :], kx)
            ss, ks_ = rms_rstd(xt, kx, ti % 2)
            stt("dve", xn, xt, ss, nbc, ALU.mult, ALU.mult, [kx, ks_, k_nbc], kn)
            pb = PSB[ti % 2]
            for k in range(8):
                tr(pb[:, k * 128:(k + 1) * 128], xn[:, k * 128:(k + 1) * 128], [kn], f"psb{ti % 2}")
            dst = uT[:, :, ti * 128:(ti + 1) * 128]
            cp("act" if ti % 2 == 0 else "dve", dst, pb[:].rearrange("p (k t) -> p k t", k=8), [f"psb{ti % 2}"], f"uT{ti}")

        pending_dumps = []

        def dump_scr(name, src):
            if name in dbg:
                pending_dumps.append((name, src))

        def flush_dumps():
            for name, src in pending_dumps:
                P.dma("sp", (lambda d_, s_: (lambda e: e.dma_start(out=d_, in_=s_)))(dbg[name], src), reads=[], writes=["dbg_" + name], is_output=True)
            pending_dumps.clear()

        P.barrier()
        FA.reset()
        BA.off = 8 * T
        uT_keys_blk = lambda tb: [f"uT{ti}" for ti in range(4 * tb, 4 * tb + 4)]
        Wst = [FA.alloc([8, 512]) for _ in range(2)]
        Wb = [BA.alloc([8, 512]) for _ in range(2)]
        gw512, k_gw = FA.alloc([512])
        for i in range(4):
            ld(gw512[:, i * 128:(i + 1) * 128], gnw.to_broadcast([128, 128]), k_gw)
        STREAM = [0]
        WbH = FA.alloc_bf16([8, 512])
        ftiles_s = [[FA.alloc([512]) for _ in range(12)], [FA.alloc([512]) for _ in range(8)]]
        btiles_s = [[BA.alloc([512]) for _ in range(8)], [BA.alloc([512]) for _ in range(4)]]
        fctr = [0, 0]; bctr = [0, 0]

        def ftile():
            s_ = STREAM[0]
            fctr[s_] += 1
            return ftiles_s[s_][fctr[s_] % len(ftiles_s[s_])]

        def btile():
            s_ = STREAM[0]
            bctr[s_] += 1
            return btiles_s[s_][bctr[s_] % len(btiles_s[s_])]

        gctr = [0]
        wbctr = [0]
        psctr = [0, 0]

        def next_ps():
            s_ = STREAM[0]
            psctr[s_] += 1
            if s_ == 0:
                i = psctr[0] % 2
            else:
                i = 2 + psctr[1] % 4
            return PSF[i], f"psf{i}"

        def load_group(col_segs):
            gi = gctr[0] % 2
            gctr[0] += 1
            ws, kws = Wst[gi]
            if STREAM[0] == 0:
                wb, kwb = WbH
            else:
                wb, kwb = Wb[wbctr[0] % 2]
                wbctr[0] += 1
            off = 0
            for (c0, n) in col_segs:
                ld(ws[:, :, off:off + n], w_in[:, c0:c0 + n].rearrange("(k p) n -> p k n", p=128), kws)
                off += n
            cp("act", wb[:, 0:4, 0:off], ws[:, 0:4, 0:off], [kws], kwb)
            cp("act", wb[:, 4:8, 0:off], ws[:, 4:8, 0:off], [kws], kwb)
            return wb, kwb

        def ftype(wb, kwb, c_off, tb):
            ps, kps = next_ps()
            for k in range(8):
                mm(ps[:], wb[:, k, c_off:c_off + 128], uT[:, k, tb * 512:(tb + 1) * 512], k == 0, k == 7,
                   [kwb] + uT_keys_blk(tb), kps)
            return ps, kps

        def ttype(wb, kwb, ncols, ti):
            ps, kps = next_ps()
            for k in range(8):
                mm(ps[:, 0:ncols], uT[:, k, ti * 128:(ti + 1) * 128], wb[:, k, 0:ncols], k == 0, k == 7,
                   [kwb, f"uT{ti}"], kps)
            return ps, kps

        hg_tails = []

        def hgrn_gen():
            STREAM[0] = 0
            HSCALE = 128.0 ** -0.5
            for gi4 in range(4):
                wb, kwb = load_group([(C_HQ + 256 * gi4, 256), (C_HF + 256 * gi4, 256)])
                for hh in range(2):
                    hd = 2 * gi4 + hh
                    for tb in range(8):
                        own = tb >= 4
                        pf, kpf = ftype(wb, kwb, 256 + hh * 128, tb)
                        if hg_tails:
                            hg_tails.pop(0)()
                        sg, ksg = ftile()
                        act(sg, pf[:], AF.Sigmoid, [kpf], ksg)
                        if own:
                            pq, kpq = ftype(wb, kwb, hh * 128, tb)
                            sq, ksq = ftile()
                            act(sq, pq[:], AF.Sigmoid, [kpq], ksq)
                            tt("dve", sq, sq, pq[:], ALU.mult, [ksq, kpq], ksq, relax=True)
                        fg, kfg = ftile()
                        act(fg, sg, AF.Identity, [ksg, "omlb", "lb"], kfg, bias=lb[:, hd:hd + 1], scale=omlb[:, hd:hd + 1])
                        Pt, kP = ftile()
                        for c in range(8):
                            P.op("dve", (lambda o_, d0: (lambda e: e.tensor_tensor_scan(out=o_, data0=d0, data1=zeros[:], initial=1.0,
                                                                                        op0=ALU.mult, op1=ALU.add)))(Pt[:, c * 64:(c + 1) * 64], fg[:, c * 64:(c + 1) * 64]),
                                 reads=[kfg, "zeros"], writes=[kP], relax=True)
                        cp("act", PL[:, hd, tb * 8:(tb + 1) * 8], Pt[:, 63::64], [kP], f"PL{hd}")
                        rP, krP = ftile()
                        tsc("dve", rP, Pt, 1e-30, None, ALU.max, None, [kP], krP, relax=True)
                        P.op("dve", (lambda o_: (lambda e: e.reciprocal(out=o_, in_=o_)))(rP), reads=[krP], writes=[krP], relax=True)
                        kk_, kkk = ftile()
                        act(kk_, fg, AF.Identity, [kfg, "onesT"], kkk, bias=onesT[:, 0:1], scale=-1.0)
                        kt, kkt = btile()
                        tt("dve", kt, kk_, rP, ALU.mult, [kkk, krP], kkt, relax=True)
                        kd, kkd = btile()
                        if KD_BCAST:
                            plb = Pt.rearrange("p (c s) -> p c s", s=64)[:, :, 63:64].to_broadcast([128, 8, 64])
                            tt("pool", kd.rearrange("p (c s) -> p c s", s=64), kt.rearrange("p (c s) -> p c s", s=64), plb, ALU.mult, [kkt, kP], kkd)
                        else:
                            for c in range(8):
                                tsc("pool", kd[:, c * 64:(c + 1) * 64], kt[:, c * 64:(c + 1) * 64], Pt[:, c * 64 + 63:c * 64 + 64], None,
                                    ALU.mult, None, [kkt, kP], kkd)
                        def tail(hd=hd, tb=tb, kd=kd, kkd=kkd):
                            pbi = (hd * 8 + tb) % 2
                            pb = PSB[pbi]
                            for i in range(4):
                                tr(pb[:, i * 128:(i + 1) * 128], kd[:, i * 128:(i + 1) * 128], [kkd], f"psb{pbi}")
                            kdT, kkdT = btile()
                            cp("act", kdT, pb[:, 0:512], [f"psb{pbi}"], kkdT)
                            st(s_kd[tb * 512:(tb + 1) * 512, hd * 128:(hd + 1) * 128].rearrange("(i p) k -> p i k", p=128),
                               kdT.rearrange("p (i k) -> p i k", i=4), kkdT, "SCR_ALL")
                        hg_tails.append(tail)
                        if own:
                            st(s_kh[hd * 128:(hd + 1) * 128, (tb - 4) * 512:(tb - 3) * 512], kt, kkt, "SCR_ALL")
                            qh, kqh = btile()
                            stt("dve", qh, sq, HSCALE, Pt, ALU.mult, ALU.mult, [ksq, kP], kqh, relax=True)
                            st(s_qh[hd * 128:(hd + 1) * 128, (tb - 4) * 512:(tb - 3) * 512], qh, kqh, "SCR_ALL")
                        yield
            while hg_tails:
                hg_tails.pop(0)()
            yield
        dump_scr("s_qh", s_qh); dump_scr("s_kh", s_kh); dump_scr("s_kd", s_kd)
        if "PL" in dbg:
            P.dma("sp", lambda e: e.dma_start(out=dbg["PL"], in_=PL[:]), reads=[f"PL{h}" for h in range(8)], writes=["dbg_PL"], is_output=True)

        def rest_gen():
            STREAM[0] = 1
            for g2 in range(2):
                wb, kwb = load_group([(C_HI + 512 * g2, 512)])
                for ti in range(32):
                    ps, kps = ttype(wb, kwb, 512, ti)
                    vb, kvb = btile()
                    cp("act", vb, ps[:], [kps], kvb)
                    st(s_v[ti * 128:(ti + 1) * 128, g2 * 512:(g2 + 1) * 512], vb, kvb, "SCR_ALL")
                    yield
            for g2 in range(2):
                wb, kwb = load_group([(C_HG + 512 * g2, 512)])
                for ti in range(16, 32):
                    ps, kps = ttype(wb, kwb, 512, ti)
                    sgt, ksgt = ftile()
                    act(sgt, ps[:], AF.Sigmoid, [kps], ksgt)
                    tt("dve", sgt, sgt, ps[:], ALU.mult, [ksgt, kps], ksgt, relax=True)
                    gb_, kgb = ftile()
                    tt("dve", gb_, sgt, gw512, ALU.mult, [ksgt, k_gw], kgb, relax=True)
                    st(s_g[(ti - 16) * 128:(ti - 15) * 128, g2 * 512:(g2 + 1) * 512], gb_, kgb, "SCR_ALL")
                    yield

            def rope_store(ps, kps, pos0, scale, dst_plain, dst_rot):
                qb, kqb = btile()
                act(qb, ps[:], AF.Copy, [kps], kqb, scale=scale)
                if dst_plain is not None:
                    st(dst_plain, qb, kqb, "SCR_ALL")
                cs, kcs = ftile(); sn, ksn = ftile()
                ld(cs, c_cos[:, pos0:pos0 + 512], kcs); ld(sn, c_sin[:, pos0:pos0 + 512], ksn)
                pp, kpp = next_ps()
                mm(pp[:], perm[:], qb, True, True, ["perm", kqb], kpp)
                t1, kt1 = ftile()
                tt("dve", t1, qb, cs, ALU.mult, [kqb, kcs], kt1, relax=True)
                t2, kt2 = ftile()
                tt("dve", t2, pp[:], sn, ALU.mult, [kpp, ksn], kt2, relax=True)
                qr, kqr = btile()
                tt("dve", qr, t1, t2, ALU.add, [kt1, kt2], kqr, relax=True)
                st(dst_rot, qr, kqr, "SCR_ALL")

            for g2 in range(2):
                wb, kwb = load_group([(C_NQ + 512 * g2, 512)])
                for ct in range(4):
                    row0 = g2 * 512 + ct * 128
                    for tb in range(4, 8):
                        ps, kps = ftype(wb, kwb, ct * 128, tb)
                        c0 = (tb - 4) * 512
                        rope_store(ps, kps, tb * 512, 0.125, s_q[row0:row0 + 128, c0:c0 + 512], s_qr[row0:row0 + 128, c0:c0 + 512])
                        yield
            wb, kwb = load_group([(C_KC, 512)])
            for ct in range(4):
                dst = s_kc if ct < 2 else s_vc
                row0 = (ct % 2) * 128
                for tb in range(8):
                    ps, kps = ftype(wb, kwb, ct * 128, tb)
                    ob, kob = btile()
                    cp("act", ob, ps[:], [kps], kob)
                    st(dst[row0:row0 + 128, tb * 512:(tb + 1) * 512], ob, kob, "SCR_ALL")
                    yield
            wb, kwb = load_group([(C_KS, 256), (C_KW, 256)])
            for ct in range(4):
                dst = s_ks if ct < 2 else s_kw
                row0 = (ct % 2) * 128
                for tb in range(8):
                    ps, kps = ftype(wb, kwb, ct * 128, tb)
                    rope_store(ps, kps, tb * 512, 1.0, None, dst[row0:row0 + 128, tb * 512:(tb + 1) * 512])
                    yield
            wb, kwb = load_group([(C_VS, 256), (C_VW, 256)])
            for ti in range(32):
                ps, kps = ttype(wb, kwb, 512, ti)
                vb, kvb = btile()
                cp("act" if ti % 2 else "dve", vb, ps[:], [kps], kvb)
                st(s_vs[ti * 128:(ti + 1) * 128, :], vb[:, 0:256], kvb, "SCR_ALL")
                st(s_vw[ti * 128:(ti + 1) * 128, :], vb[:, 256:512], kvb, "SCR_ALL")
                yield
            wb, kwb = load_group([(C_NG, 48)])
            for ti in range(16, 32):
                ps, kps = ttype(wb, kwb, 48, ti)
                act(gates[:, ti - 16, :], ps[:, 0:48], AF.Sigmoid, [kps], f"gates{ti - 16}")
                yield
            for gsel, (c_base, dst) in enumerate(((C_GA, s_ga), (C_GB, s_gb))):
                for g2 in range(2):
                    wb, kwb = load_group([(c_base + 512 * g2, 512)])
                    for ct in range(4):
                        row0 = g2 * 512 + ct * 128
                        for tb in range(4, 8):
                            ps, kps = ftype(wb, kwb, ct * 128, tb)
                            ob, kob = btile()
                            act(ob, ps[:], AF.Sigmoid, [kps], kob)
                            st(dst[row0:row0 + 128, (tb - 4) * 512:(tb - 3) * 512], ob, kob, "SCR_ALL")
                            yield

        g_h = hgrn_gen(); g_r = rest_gen()
        alive_h = alive_r = True
        while alive_h or alive_r:
            if alive_h:
                STREAM[0] = 0
                try:
                    next(g_h)
                except StopIteration:
                    alive_h = False
            for _ in range(5 if alive_h else 1000000):
                if not alive_r:
                    break
                STREAM[0] = 1
                try:
                    next(g_r)
                except StopIteration:
                    alive_r = False
        STREAM[0] = 0

        for nm, ap_ in (("s_v", s_v), ("s_g", s_g), ("s_q", s_q), ("s_qr", s_qr), ("s_kc", s_kc), ("s_vc", s_vc), ("s_ks", s_ks),
                        ("s_kw", s_kw), ("s_vs", s_vs), ("s_vw", s_vw), ("s_ga", s_ga), ("s_gb", s_gb)):
            dump_scr(nm, ap_)
        if "gates" in dbg:
            P.dma("sp", lambda e: e.dma_start(out=dbg["gates"], in_=gates[:]), reads=[f"gates{i}" for i in range(16)], writes=["dbg_gates"], is_output=True)

        P.barrier()
        flush_dumps()

        FA.reset(); BA.reset()
        NBLK = 16
        Sst = [FA.alloc([128]) for _ in range(8)]
        Sb = [BA.alloc([128]) for _ in range(8)]
        for hd in range(8):
            P.op("pool", (lambda o_: (lambda e: e.memset(o_, 0.0)))(Sst[hd][0]), writes=[Sst[hd][1]])
        kdB = [[BA.alloc([4, 128]) for _ in range(2)] for _ in range(8)]
        vB = [[BA.alloc([4, 128]) for _ in range(2)] for _ in range(8)]
        qB = [[BA.alloc([256]) for _ in range(2)] for _ in range(8)]
        kB = [[BA.alloc([256]) for _ in range(2)] for _ in range(8)]
        gB = [[FA.alloc([4, 128]) for _ in range(2)] for _ in range(8)]
        yst = [[BA.alloc([4, 128]) for _ in range(2)] for _ in range(8)]
        ATs = [BA.alloc([64]) for _ in range(4)]
        c_tail = []
        for blk in range(NBLK):
            own = blk >= 8
            bi = blk % 2
            for hd in range(8):
                t0 = blk * 256
                kd_t, kkd = kdB[hd][bi]; v_t, kv = vB[hd][bi]
                ld(kd_t[0:64], s_kd[t0:t0 + 256, hd * 128:(hd + 1) * 128].rearrange("(j s) k -> s j k", s=64), kkd)
                ld(v_t[0:64], s_v[t0:t0 + 256, hd * 128:(hd + 1) * 128].rearrange("(j s) k -> s j k", s=64), kv)
                if own:
                    o0 = t0 - HALF
                    ld(qB[hd][bi][0], s_qh[hd * 128:(hd + 1) * 128, o0:o0 + 256], qB[hd][bi][1])
                    ld(kB[hd][bi][0], s_kh[hd * 128:(hd + 1) * 128, o0:o0 + 256], kB[hd][bi][1])
                    ld(gB[hd][bi][0][0:64], s_g[o0:o0 + 256, hd * 128:(hd + 1) * 128].rearrange("(j s) k -> s j k", s=64), gB[hd][bi][1])
            for j in range(4):
                c = blk * 4 + j

                def issue_A(hd_):
                    q_t_, kq_ = qB[hd_][bi]; k_t_, kk2_ = kB[hd_][bi]
                    psA_, kpsA_ = PSF[4 + hd_ % 2], f"psf{4 + hd_ % 2}"
                    mm(psA_[0:64, 0:64], k_t_[:, j * 64:(j + 1) * 64], q_t_[:, j * 64:(j + 1) * 64], True, True, [kk2_, kq_], kpsA_)
                    at_t_, kat_ = ATs[(c * 8 + hd_) % 4]
                    tt("dve", at_t_[0:64, :], psA_[0:64, 0:64], triL[0:64, 0:64], ALU.mult, [kpsA_, "triL"], kat_)

                if own:
                    issue_A(0)
                for hd in range(8):
                    kd_t, kkd = kdB[hd][bi]; v_t, kv = vB[hd][bi]
                    S_t, kS = Sst[hd]; Sb_t, kSb = Sb[hd]
                    psi = hd % 2
                    if own:
                        q_t, kq = qB[hd][bi]; k_t, kk2 = kB[hd][bi]; g_t, kg = gB[hd][bi]
                        y_t, ky = yst[hd][bi]
                        psO, kpsO = PSF[2 + psi], f"psf{2 + psi}"
                        if hd + 1 < 8:
                            issue_A(hd + 1)
                        mm(psO[0:64, 0:128], q_t[:, j * 64:(j + 1) * 64], Sb_t, True, False, [kq, kSb], kpsO)
                        at_t, kat = ATs[(c * 8 + hd) % 4]
                        mm(psO[0:64, 0:128], at_t[0:64, :], v_t[0:64, j, :], False, True, [kat, kv], kpsO)
                    psS, kpsS = PSF[psi], f"psf{psi}"
                    mm(psS[:, 0:128], kd_t[0:64, j, :], v_t[0:64, j, :], True, True, [kkd, kv], kpsS)
                    stt("dve", S_t, S_t, PL[:, hd, c:c + 1], psS[:, 0:128], ALU.mult, ALU.add, [kS, f"PL{hd}", kpsS], kS)
                    if c >= 31 and c < 63:
                        cp("act", Sb_t, S_t, [kS], kSb)
                    if own:
                        if c_tail:
                            c_tail.pop(0)()
                        slot = 8 + hd
                        ss = stat[0:64, slot:slot + 1]; kss = f"stat{slot}"
                        act(junk[0:64, 0:128], psO[0:64, 0:128], AF.Square, [kpsO], [kss, "junk"], accum=ss)
                        act(ss, ss, AF.Sqrt, [kss, "epsT"], kss, bias=epsT[0:64, 0:1], scale=1.0 / 128.0)

                        def tail(ss=ss, kss=kss, y_t=y_t, ky=ky, psO=psO, kpsO=kpsO, g_t=g_t, kg=kg, j=j):
                            P.op("dve", (lambda o_: (lambda e: e.reciprocal(out=o_, in_=o_)))(ss), reads=[kss], writes=[kss])
                            stt("dve", y_t[0:64, j, :], psO[0:64, 0:128], ss, g_t[0:64, j, :], ALU.mult, ALU.mult, [kpsO, kss, kg], ky)
                        c_tail.append(tail)
                while c_tail:
                    c_tail.pop(0)()
            if own:
                for hd in range(8):
                    y_t, ky = yst[hd][bi]
                    o0 = blk * 256 - HALF
                    st(s_ya[o0:o0 + 256, hd * 128:(hd + 1) * 128].rearrange("(j s) k -> s j k", s=64), y_t[0:64], ky, "SCR_ALL")
        dump_scr("s_ya", s_ya)
        P.barrier()
        flush_dumps()

        FA.reset(); BA.reset()
        kcmpT2, k_kcmp = BA.alloc([4, 256])
        VcAug, k_vca = BA.alloc([4, 2, 129])
        KEa, k_KEa = BA.alloc([T])
        KEb, k_KEb = BA.alloc([T])
        cmpb, k_cmpb = BA.alloc([2, HALF])
        ld(KEa[64:128], c_E, k_KEa + "E")
        ld(KEb[0:64], c_E, k_KEb + "E")
        ld(cmpb, c_cmpb.rearrange("c p t -> p c t"), k_cmpb)
        ovl_f, k_ovl = FA.alloc([2, 64])
        ld(ovl_f, c_ovl.rearrange("(c p) s -> p c s", p=128), k_ovl)
        P.op("pool", lambda e: e.memset(VcAug, 0.0), writes=[k_vca])
        P.op("pool", lambda e: e.memset(kcmpT2, 0.0), writes=[k_kcmp])
        for g in range(4):
            cp("dve", VcAug[:, g, :, 65:129], ovl_f, [k_ovl], k_vca)
            P.op("pool", (lambda o_: (lambda e: e.memset(o_, 1.0)))(VcAug[:, g, :, 64:65]), writes=[k_vca])
        selA_t, k_selA = FA.alloc([16, 64]); selB_t, k_selB = FA.alloc([16, 64])
        for q4 in range(2):
            ld(selA_t[:, q4 * 8:(q4 + 1) * 8, :], c_selA[q4 * 1024:(q4 + 1) * 1024, :].rearrange("(ti p) s -> p ti s", p=128), k_selA)
            ld(selB_t[:, q4 * 8:(q4 + 1) * 8, :], c_selB[q4 * 1024:(q4 + 1) * 1024, :].rearrange("(ti p) s -> p ti s", p=128), k_selB)
        kval_t, k_kval = FA.alloc([32])
        ld(kval_t, c_kval, k_kval)
        dmark_B = BA.off; dmark_F = FA.off

        w1st, k_w1st = FA.alloc([32, 256])
        w1b, k_w1b = BA.alloc([32, 256])
        w2st, k_w2st = FA.alloc([2, 64]); w2b, k_w2b = BA.alloc([2, 64])
        peT, k_peT = FA.alloc([32])
        kcTs = [BA.alloc([T]) for _ in range(2)]
        peTb, k_peTb = BA.alloc([32])
        cvec, k_cvec = FA.alloc([2])
        xss = [FA.alloc([256]) for _ in range(4)]
        geTs = [BA.alloc([2, 256]) for _ in range(2)]
        x2s = [FA.alloc([256]) for _ in range(4)]; inns = [FA.alloc([256]) for _ in range(4)]
        STREAM[0] = 1
        d0_tails = []
        for kv in range(2):
            w1_d = wk1 if kv == 0 else wv1
            w2_d = wk2 if kv == 0 else wv2
            pe_d = pe_k if kv == 0 else pe_v
            src_d = s_kc if kv == 0 else s_vc
            for q4 in range(2):
                ld(w1st[0:64, q4 * 16:(q4 + 1) * 16, :], w1_d[q4 * 1024:(q4 + 1) * 1024, :].rearrange("(l d) h -> d l h", d=64), k_w1st)
            cp("dve", w1b[0:64], w1st[0:64], [k_w1st], k_w1b)
            ld(w2st, w2_d.rearrange("(c p) d -> p c d", p=128), k_w2st)
            cp("dve", w2b, w2st, [k_w2st], k_w2b)
            ld(peT[0:64], pe_d.rearrange("l d -> d l"), k_peT, slow=True)
            cp("dve", peTb[0:64], peT[0:64], [k_peT], k_peTb)
            for hc in range(2):
                ps, kps = next_ps()
                for l in range(32):
                    mm(ps[:, 0:1], w1b[0:64, l, hc * 128:(hc + 1) * 128], peTb[0:64, l:l + 1], l == 0, l == 31, [k_w1b, k_peTb], kps)
                cp("dve", cvec[:, hc:hc + 1], ps[:, 0:1], [kps], k_cvec + f"_{hc}")
            for g in range(4):
                kcT, k_kcT = kcTs[g % 2]
                geT, k_geT = geTs[g % 2]
                ld(kcT[0:64], src_d[g * 64:(g + 1) * 64, :], k_kcT)
                for hc in range(2):
                    ps, kps = next_ps()
                    for l in range(32):
                        mm(ps[:, 0:255], w1b[0:64, l, hc * 128:(hc + 1) * 128], kcT[0:64, l:l + 16 * 254 + 1:16], l == 0, l == 31, [k_w1b, k_kcT], kps)
                    if hc == 0 and d0_tails:
                        d0_tails.pop(0)()
                    bi_ = (g % 2) * 2 + hc
                    xs, kxs = xss[bi_]; x2, k_x2 = x2s[bi_]; inn, k_inn = inns[bi_]
                    act(xs[:, 0:255], ps[:, 0:255], AF.Identity, [kps, k_cvec + f"_{hc}"], kxs, bias=cvec[:, hc:hc + 1])
                    tt("dve", x2[:, 0:255], xs[:, 0:255], xs[:, 0:255], ALU.mult, [kxs], k_x2)
                    tsc("dve", inn[:, 0:255], x2[:, 0:255], 0.044715, 1.0, ALU.mult, ALU.add, [k_x2], k_inn)
                    tt("dve", inn[:, 0:255], inn[:, 0:255], xs[:, 0:255], ALU.mult, [k_inn, kxs], k_inn)
                    act(inn[:, 0:255], inn[:, 0:255], AF.Sigmoid, [k_inn], k_inn, scale=1.5957691216057308)
                    tt("dve", geT[:, hc, 0:255], inn[:, 0:255], xs[:, 0:255], ALU.mult, [k_inn, kxs], k_geT + f"_{hc}")

                def tail(kv=kv, g=g, geT=geT, k_geT=k_geT):
                    if kv == 0:
                        ps, kps = next_ps()
                        for hc in range(2):
                            mm(ps[0:64, 0:255], w2b[:, hc, :], geT[:, hc, 0:255], hc == 0, hc == 1, [k_w2b, k_geT + f"_{hc}"], kps)
                        cp("dve", kcmpT2[0:64, g, 0:255], ps[0:64, 0:255], [kps], k_kcmp)
                        P.dma("sp", (lambda o_, i_: (lambda e: e.dma_start(out=o_, in_=i_)))(kcmpT2[64:128, g, :], kcmpT2[0:64, g, :]),
                              reads=[k_kcmp], writes=[k_kcmp + "hi"])
                    else:
                        for nc_ in range(2):
                            nn = 128 if nc_ == 0 else 127
                            ps, kps = next_ps()
                            for hc in range(2):
                                mm(ps[0:nn, 0:64], geT[:, hc, nc_ * 128:nc_ * 128 + nn], w2b[:, hc, :], hc == 0, hc == 1,
                                   [k_geT + f"_{hc}", k_w2b], kps)
                            cp("dve", VcAug[0:nn, g, nc_, 0:64], ps[0:nn, 0:64], [kps], k_vca)
                d0_tails.append(tail)
            while d0_tails:
                d0_tails.pop(0)()
        STREAM[0] = 0
        if "kcmp" in dbg:
            t32, k32 = FA.alloc([4, 256])
            cp("dve", t32, kcmpT2, [k_kcmp, k_kcmp + "hi"], k32)
            st(dbg["kcmp"], t32, k32, "dbg_kcmp", q="sp", is_output=True)
        if "vcmp" in dbg:
            t32b, k32b = FA.alloc([4, 2, 129])
            cp("dve", t32b, VcAug, [k_vca], k32b)
            st(dbg["vcmp"], t32b, k32b, "dbg_vcmp", q="sp", is_output=True)
        P.barrier()
        BA.off = dmark_B; FA.off = dmark_F
        BA.gen += 1; FA.gen += 1

        if stop_after == "D0":
            P.emit(nc)
            return nc
        q2, k_q2 = BA.alloc([2, HALF])
        QB = [BA.alloc([HALF]) for _ in range(4)]
        kwT2, k_kwT = BA.alloc([T])
        VsAug, k_vsa = BA.alloc([32, 65]); VwAug, k_vwa = BA.alloc([32, 65])
        pTs = [BA.alloc([512]) for _ in range(4)]
        ybb, k_ybb = BA.alloc([16, 256])
        biasToks = [BA.alloc([128]) for _ in range(2)]
        yb, k_yb = FA.alloc([16, 256])
        imp, k_imp = FA.alloc([16, 64])
        tk_a, k_tka = FA.alloc([64]); tk_b, k_tkb = FA.alloc([64]); tk_c, k_tkc = FA.alloc([64])
        m8a, k_m8a = FA.alloc([8]); m8b, k_m8b = FA.alloc([8])
        ptc = [0]

        def epilogue(psv, kpsv, ti, hloc, h, branch, first):
            slot = 16 + (ptc[0] % 8) * 2
            ptc[0] += 1
            rd = stat[:, slot:slot + 1]; krd = f"stat{slot}"
            gr = stat[:, slot + 1:slot + 2]; kgr = f"stat{slot + 1}"
            tsc("dve", rd, psv[:, 64:65], 1e-30, None, ALU.add, None, [kpsv], krd)
            P.op("dve", (lambda o_: (lambda e: e.reciprocal(out=o_, in_=o_)))(rd), reads=[krd], writes=[krd])
            tt("dve", gr, rd, gates[:, ti, branch * 16 + h:branch * 16 + h + 1], ALU.mult, [krd, f"gates{ti}"], kgr)
            dst = yb[:, ti, hloc * 64:(hloc + 1) * 64]
            kd_ = f"yb{ti}_{hloc}"
            if first:
                tsc("dve", dst, psv[:, 0:64], gr, None, ALU.mult, None, [kpsv, kgr], kd_)
            else:
                stt("dve", dst, psv[:, 0:64], gr, dst, ALU.mult, ALU.add, [kpsv, kgr, kd_], kd_)
            return rd, krd

        for g in range(4):
            for pr in range(2):
                ld(q2[:, pr, :], s_q[g * 256 + pr * 128:g * 256 + (pr + 1) * 128, :], k_q2)
                for r2_ in range(2):
                    hl_ = pr * 2 + r2_
                    ld(QB[hl_][0][64 * r2_:64 * r2_ + 64, :], s_qr[(4 * g + hl_) * 64:(4 * g + hl_ + 1) * 64, :], QB[hl_][1] + "q")
            for hf_ in range(2):
                ld((KEa if hf_ == 0 else KEb)[hf_ * 64:(hf_ + 1) * 64, :], s_ks[g * 64:(g + 1) * 64, :], (k_KEa if hf_ == 0 else k_KEb) + "k")
                ld(kwT2[hf_ * 64:(hf_ + 1) * 64, :], s_kw[g * 64:(g + 1) * 64, :], k_kwT)
            for q4 in range(4):
                ld(VsAug[:, q4 * 8:(q4 + 1) * 8, 0:64], s_vs[q4 * 1024:(q4 + 1) * 1024, g * 64:(g + 1) * 64].rearrange("(kt p) d -> p kt d", p=128), k_vsa)
                ld(VwAug[:, q4 * 8:(q4 + 1) * 8, 0:64], s_vw[q4 * 1024:(q4 + 1) * 1024, g * 64:(g + 1) * 64].rearrange("(kt p) d -> p kt d", p=128), k_vwa)
            cp("dve", VsAug[:, :, 64], kval_t, [k_kval], k_vsa)
            cp("dve", VwAug[:, :, 64], kval_t, [k_kval], k_vwa)
            if stop_after == "D1L":
                P.emit(nc)
                return nc
            units = [(hloc, tt_, nc_) for hloc in range(4) for tt_ in range(4) for nc_ in range(2)]

            def c_scores(ui):
                hloc, tt_, nc_ = units[ui]
                pr, r2 = hloc // 2, hloc % 2
                pb = 64 * r2
                tsl = slice(tt_ * 512, (tt_ + 1) * 512)
                ps, kps = PSF[ui % 2], f"psf{ui % 2}"
                mm(ps[:, :], kcmpT2[pb:pb + 64, g, nc_ * 128:(nc_ + 1) * 128], q2[pb:pb + 64, pr, tsl], True, False,
                   [k_kcmp, k_kcmp + "hi", k_q2], kps)
                mm(ps[:, :], ident[:], cmpb[:, nc_, tsl], False, True, ["ident", k_cmpb], kps)

            def c_rest(ui):
                hloc, tt_, nc_ = units[ui]
                h = 4 * g + hloc
                ps, kps = PSF[ui % 2], f"psf{ui % 2}"
                cb = 4 if ((ui // 2) % 2 == 0) else 2
                psC = [PSF[cb], PSF[cb + 1]]
                pT, kpT = pTs[ui % 2]
                act(pT, ps[:, :], AF.Exp, [kps], [f"{kpT}_{x}" for x in range(4)])
                for ts in range(4):
                    bank = psC[ts // 2]
                    mm(bank[:, (ts % 2) * 129:(ts % 2) * 129 + 129], pT[:, ts * 128:(ts + 1) * 128], VcAug[:, g, nc_, :],
                       nc_ == 0 and ts % 2 == 0, nc_ == 1, [f"{kpT}_{ts}", k_vca], f"psf{cb + ts // 2}", skip=True)
                if nc_ == 1:
                    for ts in range(4):
                        ti = tt_ * 4 + ts
                        kb = f"psf{cb + ts // 2}"
                        psv = psC[ts // 2][:, (ts % 2) * 129:(ts % 2) * 129 + 129]
                        rd, krd = epilogue(psv, kb, ti, hloc, h, 0, True)
                        ki = f"imp{ti}"
                        if hloc == 0:
                            tsc("dve", imp[:, ti, :], psv[:, 65:129], rd, None, ALU.mult, None, [kb, krd], ki)
                        else:
                            stt("dve", imp[:, ti, :], psv[:, 65:129], rd, imp[:, ti, :], ALU.mult, ALU.add, [kb, krd, ki], ki)

            c_scores(0)
            for ui in range(len(units)):
                if ui + 1 < len(units):
                    c_scores(ui + 1)
                c_rest(ui)
            if stop_after == "D1a":
                P.emit(nc)
                return nc
            def topk_gen():
                for ti in range(16):
                    biasTok, k_btok = biasToks[ti % 2]
                    ki = f"imp{ti}"
                    tt("dve", tk_a, imp[:, ti, :], selA_t[:, ti, :], ALU.mult, [ki, k_selA], k_tka)
                    tt("dve", tk_a, tk_a, selB_t[:, ti, :], ALU.add, [k_tka, k_selB], k_tka)
                    P.op("dve", lambda e: e.max(out=m8a, in_=tk_a), reads=[k_tka], writes=[k_m8a])
                    P.op("dve", lambda e: e.match_replace(out=tk_b, in_to_replace=m8a, in_values=tk_a, imm_value=-1e30), reads=[k_tka, k_m8a], writes=[k_tkb])
                    P.op("dve", lambda e: e.max(out=m8b, in_=tk_b), reads=[k_tkb], writes=[k_m8b])
                    P.op("dve", lambda e: e.match_replace(out=tk_c, in_to_replace=m8b, in_values=tk_b, imm_value=-1e30), reads=[k_tkb, k_m8b], writes=[k_tkc])
                    tt("dve", tk_c, tk_c, tk_a, ALU.not_equal, [k_tkc, k_tka], k_tkc)
                    tsc("dve", biasTok[:, 0:64], tk_c, -NEGB, NEGB, ALU.mult, ALU.add, [k_tkc], k_btok)
                    tsc("dve", biasTok[:, 64:128], tk_c, -NEGB, NEGB, ALU.mult, ALU.add, [k_tkc], k_btok)
                    yield
                    pbk = PSB[ti % 2]
                    tr(pbk[:, 0:128], biasTok[:, 0:128], [k_btok], f"psb{ti % 2}")
                    for hl_ in range(4):
                        bo = 64 if hl_ % 2 == 0 else 0
                        cp("act" if hl_ % 2 else "dve", QB[hl_][0][bo:bo + 64, ti * 128:(ti + 1) * 128], pbk[bo:bo + 64, 0:128],
                           [f"psb{ti % 2}"], QB[hl_][1] + f"b{ti}")
            def attn_gen(branches):
                SB = [(PSF[0], "psf0"), (PSF[1], "psf1"), (PSF[4], "psf4"), (PSF[5], "psf5")]
                LOOK = 3
                tiles = []
                gi_ = 0
                for hloc in range(4):
                    for branch in branches:
                        for tt_ in range(4):
                            kt0 = 16 + 4 * tt_
                            kts = list(range(0, kt0 + 4)) if branch == 1 else list(range(kt0 - 4, kt0 + 4))
                            grp = dict(hloc=hloc, branch=branch, tt_=tt_, kt0=kt0, started=[False] * 4, obi=gi_ % 2)
                            gi_ += 1
                            for ii, kt in enumerate(kts):
                                kk = kt - kt0
                                ts_lo = max(0, kk)
                                ts_hi = 3 if branch == 1 else min(3, kk + 4)
                                tiles.append(dict(g=grp, kt=kt, kk=kk, ts_lo=ts_lo, ts_hi=ts_hi, last=(ii == len(kts) - 1)))

                def scores(j):
                    t_ = tiles[j]; gr = t_["g"]
                    hloc, branch, tt_ = gr["hloc"], gr["branch"], gr["tt_"]
                    r2 = hloc % 2
                    pb = 64 * r2
                    qb_t, kqb = QB[hloc]
                    kt = t_["kt"]
                    ksl = slice(kt * 128, (kt + 1) * 128)
                    c0, c1 = t_["ts_lo"] * 128, (t_["ts_hi"] + 1) * 128
                    tsl = slice(tt_ * 512 + c0, tt_ * 512 + c1)
                    ps, kps = SB[j % 4]
                    if branch == 1:
                        KE, k_KE = (KEa, k_KEa) if r2 == 0 else (KEb, k_KEb)
                        mm(ps[:, c0:c1], KE[:, ksl], qb_t[:, tsl], True, True,
                           [k_KE + "E", k_KE + "k", kqb + "q"] + [kqb + f"b{tt_ * 4 + x}" for x in range(4)], kps)
                    else:
                        mm(ps[:, c0:c1], kwT2[pb:pb + 64, ksl], qb_t[pb:pb + 64, tsl], True, True, [k_kwT, kqb + "q"], kps)

                def rest(j):
                    t_ = tiles[j]; gr = t_["g"]
                    hloc, branch, tt_, kt0 = gr["hloc"], gr["branch"], gr["tt_"], gr["kt0"]
                    h = 4 * g + hloc
                    VA, k_VA = (VsAug, k_vsa) if branch == 1 else (VwAug, k_vwa)
                    psO, kpsO = PSF[2 + gr["obi"]], f"psf{2 + gr['obi']}"
                    started = gr["started"]
                    kt, kk = t_["kt"], t_["kk"]
                    c0, c1 = t_["ts_lo"] * 128, (t_["ts_hi"] + 1) * 128
                    ps, kps = SB[j % 4]
                    pT, kpT = pTs[j % 4]
                    act(pT[:, c0:c1], ps[:, c0:c1], AF.Exp, [kps], [f"{kpT}_{x}" for x in range(t_["ts_lo"], t_["ts_hi"] + 1)])
                    for ts in range(t_["ts_lo"], t_["ts_hi"] + 1):
                        Dd = ts - kk
                        sub = pT[:, ts * 128:(ts + 1) * 128]
                        ksub = f"{kpT}_{ts}"
                        if Dd == 0:
                            tt(MASK_ENG, sub, sub, triL[:], ALU.mult, [ksub, "triL"], ksub)
                        elif branch == 2 and Dd == 4:
                            tt(MASK_ENG, sub, sub, triU[:], ALU.mult, [ksub, "triU"], ksub)
                        mm(psO[:, ts * 65:(ts + 1) * 65], sub, VA[:, kt, :], not any(started), kt == kt0 + ts, [ksub, k_VA], kpsO, skip=True)
                        started[ts] = True
                    if t_["last"]:
                        for ts in range(4):
                            epilogue(psO[:, ts * 65:(ts + 1) * 65], kpsO, tt_ * 4 + ts, hloc, h, branch, False)
                        return True
                    return False

                n = len(tiles)
                for j in range(min(LOOK, n)):
                    scores(j)
                for j in range(n):
                    if j + LOOK < n:
                        scores(j + LOOK)
                    if rest(j):
                        yield

            def rr(gens):
                gens = list(gens)
                while gens:
                    for g_ in list(gens):
                        try:
                            next(g_)
                        except StopIteration:
                            gens.remove(g_)

            rr([attn_gen((2,)), topk_gen()])
            rr([attn_gen((1,))])
            if stop_after == "D1c":
                P.emit(nc)
                return nc
            cp("act", ybb, yb, [f"yb{ti}_{hl}" for ti in range(16) for hl in range(4)], k_ybb)
            for q4 in range(2):
                st(s_yb[q4 * 1024:(q4 + 1) * 1024, g * 256:(g + 1) * 256].rearrange("(ti p) c -> p ti c", p=128), ybb[:, q4 * 8:(q4 + 1) * 8, :], k_ybb, "SCR_ALL")
        dump_scr("s_yb", s_yb)
        P.barrier()
        flush_dumps()

        if stop_after == "D":
            P.emit(nc)
            return nc
        FA.reset(); BA.reset()

        def load_weight_bf16(dst, kdst, w_dram, nrows_k, ncols, stg):
            wv = w_dram.rearrange("(k p) n -> p k n", p=128)
            i = 0
            for k0 in range(0, nrows_k, 8):
                kn = min(8, nrows_k - k0)
                for c0 in range(0, ncols, 512):
                    cn = min(512, ncols - c0)
                    st_, kst = stg[i % len(stg)]
                    i += 1
                    ld(st_[:, 0:kn, 0:cn], wv[:, k0:k0 + kn, c0:c0 + cn], kst)
                    cp("act" if i % 2 else "dve", dst[:, k0:k0 + kn, c0:c0 + cn], st_[:, 0:kn, 0:cn], [kst], kdst)

        stg = [FA.alloc([8, 512]) for _ in range(2)]
        Wa_b, k_Wa = BA.alloc([8, D]); Wb_b, k_Wb = BA.alloc([8, D]); Wo_b, k_Wo = BA.alloc([8, D])
        load_weight_bf16(Wa_b, k_Wa, w_a, 8, D, stg)
        load_weight_bf16(Wb_b, k_Wb, w_b, 8, D, stg)
        load_weight_bf16(Wo_b, k_Wo, w_o, 8, D, stg)
        npost_bc, k_npost = FA.alloc([D]); nmpre_bc, k_nmpre = FA.alloc([D])
        ld(npost_bc, n_post.to_broadcast([128, D]), k_npost)
        ld(nmpre_bc, n_mpre.to_broadcast([128, D]), k_nmpre)
        yaT, k_yaT = BA.alloc([8, 512]); ybT, k_ybT = BA.alloc([8, 512]); mT, k_mT = BA.alloc([8, 512])
        ytok = [BA.alloc([D]) for _ in range(2)]
        sgt_ = [BA.alloc([512]) for _ in range(4)]
        vn_b = [BA.alloc([D]) for _ in range(2)]
        vTs = [BA.alloc([8, 128]) for _ in range(2)]
        m1s = [FA.alloc([512]) for _ in range(2)]
        m2s = [FA.alloc([512]) for _ in range(2)]
        xts2 = [FA.alloc([D]) for _ in range(2)]
        h1s = [FA.alloc([D]) for _ in range(2)]
        trc = [0]

        def transpose_tile(src_tok, ksrc, dstT, kdst, col0):
            i = trc[0] % 2
            trc[0] += 1
            pb = PSB[i]
            for k in range(8):
                tr(pb[:, k * 128:(k + 1) * 128], src_tok[:, k * 128:(k + 1) * 128], [ksrc], f"psb{i}")
            cp("act" if i else "dve", dstT[:, :, col0:col0 + 128], pb[:].rearrange("p (k t) -> p k t", k=8), [f"psb{i}"], kdst)

        def rms2(ps_a, kpa, ps_b, kpb, slot):
            s0 = stat[:, slot:slot + 1]; s1 = stat[:, slot + 1:slot + 2]
            k0, k1 = f"stat{slot}", f"stat{slot + 1}"
            act(junk[:, 0:512], ps_a, AF.Square, [kpa], [k0, "junk"], accum=s0)
            act(junk[:, 512:1024], ps_b, AF.Square, [kpb], [k1, "junk2"], accum=s1)
            tt("dve", s0, s0, s1, ALU.add, [k0, k1], k0)
            act(s0, s0, AF.Sqrt, [k0, "epsT"], k0, bias=epsT[:, 0:1], scale=1.0 / 1024.0)
            P.op("dve", (lambda o_: (lambda e: e.reciprocal(out=o_, in_=o_)))(s0), reads=[k0], writes=[k0])
            return s0, k0

        mTs = [(mT, k_mT), FA.alloc_bf16([8, 512])]

        def e_front(tb):
            mT_, k_mT_ = mTs[tb % 2]
            for which, (srcd, dstT, kdT) in enumerate(((s_ya, yaT, k_yaT), (s_yb, ybT, k_ybT))):
                for ts in range(4):
                    yt, kyt = ytok[(which * 4 + ts) % 2]
                    r0 = tb * 512 + ts * 128
                    ld(yt, srcd[r0:r0 + 128, :], kyt)
                    transpose_tile(yt, kyt, dstT, kdT, ts * 128)
                yield
            for ct in range(8):
                pa, kpa = PSF[0 + (ct % 2) * 2], f"psf{0 + (ct % 2) * 2}"
                pbb, kpb = PSF[1 + (ct % 2) * 2], f"psf{1 + (ct % 2) * 2}"
                for k in range(8):
                    mm(pa[:, :], Wa_b[:, k, ct * 128:(ct + 1) * 128], yaT[:, k, :], k == 0, k == 7, [k_Wa, k_yaT], kpa)
                for k in range(8):
                    mm(pbb[:, :], Wb_b[:, k, ct * 128:(ct + 1) * 128], ybT[:, k, :], k == 0, k == 7, [k_Wb, k_ybT], kpb)
                ga_t, kga = sgt_[(ct % 2) * 2]; gb_t, kgb2 = sgt_[(ct % 2) * 2 + 1]
                ld(ga_t, s_ga[ct * 128:(ct + 1) * 128, tb * 512:(tb + 1) * 512], kga)
                ld(gb_t, s_gb[ct * 128:(ct + 1) * 128, tb * 512:(tb + 1) * 512], kgb2)
                m1, km1 = m1s[ct % 2]; m2, km2 = m2s[ct % 2]
                tt("dve", m1, pa[:, :], ga_t, ALU.mult, [kpa, kga], km1)
                tt("dve", m2, pbb[:, :], gb_t, ALU.mult, [kpb, kgb2], km2)
                tt("pool", mT_[:, ct, :], m1, m2, ALU.add, [km1, km2], k_mT_ + f"_{ct}")
                if ct % 2 == 1:
                    yield

        def e_back(tb):
            mT_, k_mT_ = mTs[tb % 2]
            for ts in range(4):
                ti = tb * 4 + ts
                za, kza = PSF[4], "psf4"
                zb, kzb = PSF[5], "psf5"
                for nh, (zz, kzz) in enumerate(((za, kza), (zb, kzb))):
                    for k in range(8):
                        mm(zz[:, :], mT_[:, k, ts * 128:(ts + 1) * 128], Wo_b[:, k, nh * 512:(nh + 1) * 512], k == 0, k == 7,
                           [k_mT_ + f"_{k}", k_Wo], kzz)
                if e_tails:
                    e_tails.pop(0)()
                rs, krs = rms2(za[:, :], kza, zb[:, :], kzb, 40 + (ti % 2) * 4)
                xt, kx = xts2[ti % 2]; h1, kh1 = h1s[ti % 2]
                ld(xt, xo[ti * 128:(ti + 1) * 128, :], kx)
                stt("dve", h1[:, 0:512], za[:, :], rs, npost_bc[:, 0:512], ALU.mult, ALU.mult, [kza, krs, k_npost], kh1)
                stt("dve", h1[:, 512:1024], zb[:, :], rs, npost_bc[:, 512:1024], ALU.mult, ALU.mult, [kzb, krs, k_npost], kh1)
                tt("pool", h1, h1, xt, ALU.add, [kh1, kx], kh1)
                st(s_h1[ti * 128:(ti + 1) * 128, :], h1, kh1, "SCR_ALL")
                slot = 42 + (ti % 2) * 4
                ss = stat[:, slot:slot + 1]; kss = f"stat{slot}"
                act(junk[:, :], h1, AF.Square, [kh1], [kss, "junk", "junk2"], accum=ss)
                act(ss, ss, AF.Sqrt, [kss, "epsT"], kss, bias=epsT[:, 0:1], scale=1.0 / 1024.0)
                P.op("dve", (lambda o_: (lambda e: e.reciprocal(out=o_, in_=o_)))(ss), reads=[kss], writes=[kss])
                vn, kvn = vn_b[ti % 2]
                stt("dve", vn, h1, ss, nmpre_bc, ALU.mult, ALU.mult, [kh1, kss, k_nmpre], kvn)
                def tail(ti=ti, vn=vn, kvn=kvn):
                    vT_t, kvT = vTs[ti % 2]
                    transpose_tile(vn, kvn, vT_t, kvT, 0)
                    st(s_vT[:, ti * 128:(ti + 1) * 128].rearrange("(k p) t -> p k t", p=128), vT_t, kvT, "SCR_ALL")
                e_tails.append(tail)
                yield

        def drain(gens):
            gens = list(gens)
            while gens:
                for g_ in list(gens):
                    try:
                        next(g_)
                    except StopIteration:
                        gens.remove(g_)

        e_tails = []
        drain([e_front(0)])
        for tb in range(4):
            gs = [e_back(tb)]
            if tb + 1 < 4:
                gs.insert(0, e_front(tb + 1))
            drain(gs)
        while e_tails:
            e_tails.pop(0)()
        dump_scr("s_h1", s_h1)
        P.barrier()
        flush_dumps()

        FA.reset(); BA.reset()
        wd_b, k_wd = BA.alloc([32, D])
        wus = [FA.alloc([8, 256]) for _ in range(2)]
        wub = [BA.alloc([8, 256]) for _ in range(2)]
        vT_blk, k_vTb = BA.alloc([8, 512])
        actT, k_actT = FA.alloc_bf16([32, 512])
        nmpost_bc, k_nmpost = FA.alloc([D])
        ld(nmpost_bc, n_mpost.to_broadcast([128, D]), k_nmpost)
        rts = [FA.alloc([512]) for _ in range(2)]
        h1s = [FA.alloc([D]) for _ in range(2)]
        h2s = [FA.alloc([D]) for _ in range(2)]
        w_up_v = w_up.rearrange("(k p) n -> p k n", p=128)
        for tb in range(4):
            ld(vT_blk, s_vT[:, tb * 512:(tb + 1) * 512].rearrange("(k p) t -> p k t", p=128), k_vTb)
            for cg in range(16):
                ws_, kws_ = wus[cg % 2]; wb_, kwb_ = wub[cg % 2]
                ld(ws_, w_up_v[:, :, cg * 256:(cg + 1) * 256], kws_)
                cp("act" if cg % 2 else "dve", wb_, ws_, [kws_], kwb_)
                for c2 in range(2):
                    fft = cg * 2 + c2
                    ps, kps = PSF[fft % 2], f"psf{fft % 2}"
                    for k in range(8):
                        mm(ps[:, :], wb_[:, k, c2 * 128:(c2 + 1) * 128], vT_blk[:, k, :], k == 0, k == 7, [kwb_, k_vTb], kps)
                    rt, krt = rts[fft % 2]
                    act(rt, ps[:, :], AF.Relu, [kps], krt)
                    tt("dve", actT[:, fft, :], rt, ps[:, :], ALU.mult, [krt, kps], k_actT + f"_{fft}")
            if tb == 0:
                wdv = w_down.rearrange("(k p) n -> p k n", p=128)
                ci = 0
                for k0 in range(0, 32, 8):
                    for c0 in range(0, D, 256):
                        ws_, kws_ = wus[ci % 2]
                        ld(ws_, wdv[:, k0:k0 + 8, c0:c0 + 256], kws_)
                        cp("act" if ci % 2 else "dve", wd_b[:, k0:k0 + 8, c0:c0 + 256], ws_, [kws_], k_wd)
                        ci += 1
            for ts in range(4):
                ti = tb * 4 + ts
                za, kza = PSF[2 + (ti % 2) * 2], f"psf{2 + (ti % 2) * 2}"
                zb, kzb = PSF[3 + (ti % 2) * 2], f"psf{3 + (ti % 2) * 2}"
                for nh, (zz, kzz) in enumerate(((za, kza), (zb, kzb))):
                    for fft in range(32):
                        mm(zz[:, :], actT[:, fft, ts * 128:(ts + 1) * 128], wd_b[:, fft, nh * 512:(nh + 1) * 512], fft == 0, fft == 31,
                           [k_actT + f"_{fft}", k_wd], kzz)
                rs, krs = rms2(za[:, :], kza, zb[:, :], kzb, 48 + (ti % 2) * 4)
                h1, kh1 = h1s[ti % 2]; h2, kh2 = h2s[ti % 2]
                ld(h1, s_h1[ti * 128:(ti + 1) * 128, :], kh1)
                stt("dve", h2[:, 0:512], za[:, :], rs, nmpost_bc[:, 0:512], ALU.mult, ALU.mult, [kza, krs, k_nmpost], kh2)
                stt("dve", h2[:, 512:1024], zb[:, :], rs, nmpost_bc[:, 512:1024], ALU.mult, ALU.mult, [kzb, krs, k_nmpost], kh2)
                tt("pool", h2, h2, h1, ALU.add, [kh2, kh1], kh2)
                st(s_h2[ti * 128:(ti + 1) * 128, :], h2, kh2, "SCR_ALL")
        dump_scr("s_h2", s_h2)
        P.barrier()
        flush_dumps()

        FA.reset(); BA.reset()
        stg = [FA.alloc([8, 512]) for _ in range(2)]
        Wpg_b, k_Wpg = BA.alloc([8, D]); Wple_b, k_Wple = BA.alloc([2, D])
        load_weight_bf16(Wpg_b, k_Wpg, w_pg, 8, D, stg)
        load_weight_bf16(Wple_b, k_Wple, w_ple, 2, D, stg)
        nple_bc, k_nple = FA.alloc([D])
        ld(nple_bc, n_ple.to_broadcast([128, D]), k_nple)
        NB3 = 3
        h2s = [FA.alloc([D]) for _ in range(NB3)]
        h2bs = [BA.alloc([D]) for _ in range(NB3)]
        h2Ts = [BA.alloc([8, 128]) for _ in range(NB3)]
        pfs = [FA.alloc([256]) for _ in range(NB3)]
        pbs = [BA.alloc([256]) for _ in range(NB3)]
        pTs2 = [BA.alloc([2, 128]) for _ in range(NB3)]
        sgs = [FA.alloc([D]) for _ in range(NB3)]
        es_ = [FA.alloc([D]) for _ in range(NB3)]
        outs = [FA.alloc([D]) for _ in range(NB3)]

        def g_front(ti):
            b3 = ti % NB3
            h2, kh2 = h2s[b3]; h2b, kh2b = h2bs[b3]; h2T, kh2T = h2Ts[b3]
            ld(h2, s_h2[ti * 128:(ti + 1) * 128, :], kh2)
            cp("pool", h2b, h2, [kh2], kh2b)
            transpose_tile(h2b, kh2b, h2T, kh2T, 0)
            pf, kpf = pfs[b3]; pb_, kpb_ = pbs[b3]; pT_, kpT_ = pTs2[b3]
            ld(pf, po[ti * 128:(ti + 1) * 128, :], kpf)
            cp("pool", pb_, pf, [kpf], kpb_)
            i2 = trc[0] % 2
            trc[0] += 1
            pbk = PSB[i2]
            for k in range(2):
                tr(pbk[:, k * 128:(k + 1) * 128], pb_[:, k * 128:(k + 1) * 128], [kpb_], f"psb{i2}")
            cp("dve", pT_, pbk[:, 0:256].rearrange("p (k t) -> p k t", k=2), [f"psb{i2}"], kpT_)

        def g_banks(ti, nh):
            u = (2 * ti + nh) % 3
            return (PSF[2 * u], f"psf{2 * u}"), (PSF[2 * u + 1], f"psf{2 * u + 1}")

        def g_ufront(ti, nh):
            b3 = ti % NB3
            h2T, kh2T = h2Ts[b3]; pT_, kpT_ = pTs2[b3]
            (gp, kgp), (ep, kep) = g_banks(ti, nh)
            for k in range(8):
                mm(gp[:, :], h2T[:, k, :], Wpg_b[:, k, nh * 512:(nh + 1) * 512], k == 0, k == 7, [kh2T, k_Wpg], kgp)
            for k in range(2):
                mm(ep[:, :], pT_[:, k, :], Wple_b[:, k, nh * 512:(nh + 1) * 512], k == 0, k == 1, [kpT_, k_Wple], kep)

        def g_uback(ti, nh):
            b3 = ti % NB3
            sg, ksg = sgs[b3]; ee, kee = es_[b3]
            (gp, kgp), (ep, kep) = g_banks(ti, nh)
            act(sg[:, nh * 512:(nh + 1) * 512], gp[:, :], AF.Sigmoid, [kgp], ksg + f"_{nh}")
            tt("dve", ee[:, nh * 512:(nh + 1) * 512], sg[:, nh * 512:(nh + 1) * 512], ep[:, :], ALU.mult, [ksg + f"_{nh}", kep], kee + f"_{nh}")

        def g_tback(ti):
            b3 = ti % NB3
            h2, kh2 = h2s[b3]; ee, kee = es_[b3]; oo, koo = outs[b3]
            slot = 56 + b3 * 2
            ss = stat[:, slot:slot + 1]; kss = f"stat{slot}"
            act(junk[:, :], ee, AF.Square, [kee + "_0", kee + "_1"], [kss, "junk", "junk2"], accum=ss)
            act(ss, ss, AF.Sqrt, [kss, "epsT"], kss, bias=epsT[:, 0:1], scale=1.0 / 1024.0)
            P.op("dve", (lambda o_: (lambda e: e.reciprocal(out=o_, in_=o_)))(ss), reads=[kss], writes=[kss])
            stt("dve", oo, ee, ss, nple_bc, ALU.mult, ALU.mult, [kee + "_0", kee + "_1", kss, k_nple], koo)
            tt("pool", oo, oo, h2, ALU.add, [koo, kh2], koo)
            st(out[ti * 128:(ti + 1) * 128, :], oo, koo, "SCR_ALL", q="sp", is_output=True)

        NT = 16
        g_front(0); g_ufront(0, 0); g_ufront(0, 1); g_front(1)
        for ti in range(NT):
            if ti + 1 < NT:
                g_ufront(ti + 1, 0)
            g_uback(ti, 0)
            if ti + 1 < NT:
                g_ufront(ti + 1, 1)
            g_uback(ti, 1)
            if ti + 2 < NT:
                g_front(ti + 2)
            g_tback(ti)
        P.emit(nc)
    return nc


def _consts(j):
    c = {}
    c["c_ident"] = np.eye(128, dtype=np.float32).astype(NPBF)
    pm = np.zeros((128, 128), np.float32)
    for h in range(2):
        for d in range(64):
            partner = d + 8 if d < 8 else (d - 8 if d < 16 else d)
            pm[h * 64 + partner, h * 64 + d] = 1.0
    c["c_perm"] = pm.astype(NPBF)
    half = 8
    inv = 500000.0 ** (-np.arange(half) * 2.0 / 16)
    ang = np.arange(T, dtype=np.float64)[None, :] * inv[:, None]
    cos = np.ones((64, T), np.float32); sin = np.zeros((64, T), np.float32)
    cos[0:8] = np.cos(ang); cos[8:16] = np.cos(ang)
    sin[0:8] = -np.sin(ang); sin[8:16] = np.sin(ang)
    c["c_cos"] = np.concatenate([cos, cos], 0)
    c["c_sin"] = np.concatenate([sin, sin], 0)
    E = np.zeros((64, T), np.float32)
    E[np.arange(T) // 64, np.arange(T)] = 1.0
    c["c_E"] = E.astype(NPBF)
    kk = np.arange(128)[:, None]; tt_ = np.arange(128)[None, :]
    c["c_triL"] = (tt_ >= kk).astype(np.float32).astype(NPBF)
    c["c_triU"] = (tt_ < kk).astype(np.float32).astype(NPBF)
    n = np.arange(256)
    sel_start = np.arange(64) * 64
    ovl = ((n[:, None] * 16 < sel_start[None, :] + 64) & (n[:, None] * 16 + 31 >= sel_start[None, :])).astype(np.float32)
    ovl[255] = 0
    c["c_ovl"] = ovl
    t = HALF + np.arange(HALF)
    nmin = 0 if j == 1 else 128
    valid = (n[:, None] * 16 + 31 <= t[None, :]) & (n[:, None] >= nmin) & (n[:, None] < 255)
    c["c_cmpb"] = np.where(valid, 0.0, NEGB).astype(np.float32).reshape(2, 128, HALF).astype(NPBF)
    bmin = 0 if j == 1 else 32
    b = np.arange(64)[None, :]
    cur = (t // 64)[:, None]
    val = (b >= bmin) & (b <= cur)
    forced = val & ((b == bmin) | (b == cur) | (b == cur - 1))
    c["c_selA"] = (val & ~forced).astype(np.float32)
    c["c_selB"] = np.where(forced, 1e9 + 1000.0 * b, np.where(val, 0.0, -1e9)).astype(np.float32)
    kv = np.ones(T, np.float32)
    if j == 0:
        kv[:HALF] = 0
    c["c_kval"] = np.ascontiguousarray(kv.reshape(32, 128).T)
    return c


def make_in_maps(inputs):
    x = np.asarray(inputs["x"], np.float32)
    p = np.asarray(inputs["p"], np.float32)
    shared = {
        "w_in": np.ascontiguousarray(inputs["w_in"][0]),
        "w_a": np.ascontiguousarray(inputs["w_branch_a"][0]), "w_b": np.ascontiguousarray(inputs["w_branch_b"][0]),
        "w_o": np.ascontiguousarray(inputs["w_out"][0]),
        "w_up": np.ascontiguousarray(inputs["w_up"][0]), "w_down": np.ascontiguousarray(inputs["w_down"][0]),
        "w_ple": np.ascontiguousarray(inputs["w_ple"][0]), "w_pg": np.ascontiguousarray(inputs["w_ple_gate"][0]),
        "n_pre": np.ascontiguousarray(inputs["norm_pre_mix"]), "n_post": np.ascontiguousarray(inputs["norm_post_mix"]),
        "n_mpre": np.ascontiguousarray(inputs["norm_pre_mlp"]), "n_mpost": np.ascontiguousarray(inputs["norm_post_mlp"]),
        "n_ple": np.ascontiguousarray(inputs["norm_ple"]),
        "lbl": np.ascontiguousarray(inputs["hg_lb_logits"]), "gnw": np.ascontiguousarray(inputs["hg_gnorm"]),
        "pe_k": np.ascontiguousarray(inputs["cmp_pe_k"][0]), "pe_v": np.ascontiguousarray(inputs["cmp_pe_v"][0]),
        "wk1": np.ascontiguousarray(inputs["cmp_wk1"][0]), "wk2": np.ascontiguousarray(inputs["cmp_wk2"][0]),
        "wv1": np.ascontiguousarray(inputs["cmp_wv1"][0]), "wv2": np.ascontiguousarray(inputs["cmp_wv2"][0]),
    }
    shared = {k: np.asarray(v, np.float32) for k, v in shared.items()}
    cj = [_consts(0), _consts(1)]
    maps = []
    for c in range(8):
        b, j = c // 2, c % 2
        m = dict(shared)
        m.update(cj[j])
        m["xo"] = np.ascontiguousarray(x[b, j * HALF:(j + 1) * HALF])
        m["xc"] = np.ascontiguousarray(x[b, 0:HALF]) if j == 1 else np.zeros((HALF, D), np.float32)
        m["po"] = np.ascontiguousarray(p[0, b, j * HALF:(j + 1) * HALF])
        maps.append(m)
    return maps


def kernel(**inputs):
    nc = build()
    maps = make_in_maps(inputs)
    res = run_bass_kernel_spmd(nc, maps, core_ids=list(range(8)))
    outp = np.zeros((4, T, D), np.float32)
    for c in range(8):
        b, j = c // 2, c % 2
        outp[b, j * HALF:(j + 1) * HALF] = res.results[c]["out"]
    return outp
```
